# Optimizing a Trainium2 kernel written in Bass

```python
import math
import jax, jax.numpy as jnp
from jax import lax
import numpy as np

D_MODEL = 2048
BATCH = 4
SEQ = 4096
DEPTH = 2

QBLOCK = 128
RMS_EPS = 1e-6
NEG_INF = -1e30
D_FF = 5632

MLA_HEADS = 4
MLA_Q_RANK = 512
MLA_KV_RANK = 256
MLA_NOPE = 128
MLA_ROPE = 64
MLA_V = 128
ROPE_THETA = 10000.0

DIFF_HEADS = 4
DIFF_QK = 64
DIFF_V = 128

NSA_HEADS = 4
NSA_D = 128
NSA_CMP_LEN = 32
NSA_CMP_STRIDE = 16
NSA_SEL_BLOCK = 64
NSA_N_SELECT = 16
NSA_WINDOW = 512
NSA_FORCE_BONUS = 1e4

SB_HEADS = 4
SB_D = 128

N_ALIBI_HEADS = DIFF_HEADS + NSA_HEADS
MIX_WIDTH = MLA_HEADS * MLA_V + DIFF_HEADS * DIFF_V + NSA_HEADS * NSA_D + SB_HEADS * SB_D

IN_SIZES = (
    MLA_Q_RANK, MLA_KV_RANK, MLA_ROPE,
    DIFF_HEADS * 2 * DIFF_QK, DIFF_HEADS * 2 * DIFF_QK, DIFF_HEADS * DIFF_V,
    NSA_HEADS * NSA_D, NSA_D, NSA_D, NSA_D, NSA_D, NSA_D, NSA_D, NSA_HEADS * 3,
    SB_HEADS * SB_D, SB_HEADS * SB_D, SB_HEADS * SB_D,
)
D_IN = sum(IN_SIZES)

kernel_name = 'hybrid_mla_diff_nsa_stickbreak_macaron'


def _in_offsets():
    offs, acc = [], 0
    for w in IN_SIZES[:-1]:
        acc += w
        offs.append(acc)
    return offs


def _rms(x, g):
    xf = x.astype(jnp.float32)
    y = xf * lax.rsqrt(jnp.mean(xf * xf, axis=-1, keepdims=True) + RMS_EPS)
    return (y * g.astype(jnp.float32)).astype(x.dtype)


def _swiglu(x, w_gate, w_up, w_down):
    return (jax.nn.silu(x @ w_gate) * (x @ w_up)) @ w_down


def _heads(x, h):
    b, s, c = x.shape
    return x.reshape(b, s, h, c // h).transpose(0, 2, 1, 3)


def _merge(x):
    b, h, s, d = x.shape
    return x.transpose(0, 2, 1, 3).reshape(b, s, h * d)


def _to_qblocks(x):
    b, h, s, d = x.shape
    return x.reshape(b, h, s // QBLOCK, QBLOCK, d).transpose(2, 0, 1, 3, 4)


def _from_qblocks(y):
    nb, b, h, qb, d = y.shape
    return y.transpose(1, 2, 0, 3, 4).reshape(b, h, nb * qb, d)


def _sweep(fn, *qs):
    nb = qs[0].shape[2] // QBLOCK
    out = lax.map(lambda a: fn(*a), (jnp.arange(nb), *[_to_qblocks(z) for z in qs]))
    return _from_qblocks(out)


def _masked_softmax(s, mask):
    s = jnp.where(mask, s.astype(jnp.float32), NEG_INF)
    m = jnp.max(s, axis=-1, keepdims=True)
    e = jnp.where(mask, jnp.exp(s - m), 0.0)
    return e / jnp.maximum(jnp.sum(e, axis=-1, keepdims=True), 1e-30)


def _rope(x, cos, sin):
    xf = x.astype(jnp.float32)
    x1, x2 = jnp.split(xf, 2, axis=-1)
    return jnp.concatenate([x1 * cos - x2 * sin, x2 * cos + x1 * sin], axis=-1).astype(x.dtype)


def _alibi_slopes(n):
    return 2.0 ** (-8.0 * jnp.arange(1, n + 1, dtype=jnp.float32) / n)


def _mla(c_q, c_kv, k_rope, cq_norm, ckv_norm, w_uq, w_ukv, qn_norm, qr_norm, kn_norm, kr_norm, o_norm, cos, sin):
    q = _heads(_rms(c_q, cq_norm) @ w_uq, MLA_HEADS)
    kv = _heads(_rms(c_kv, ckv_norm) @ w_ukv, MLA_HEADS)
    q_nope = _rms(q[..., :MLA_NOPE], qn_norm)
    q_rope = _rope(_rms(q[..., MLA_NOPE:], qr_norm), cos, sin)
    k_nope = _rms(kv[..., :MLA_NOPE], kn_norm)
    v = kv[..., MLA_NOPE:]
    k_r = _rope(_rms(k_rope, kr_norm), cos, sin)
    s = q.shape[2]
    kpos = jnp.arange(s)
    scale = (MLA_NOPE + MLA_ROPE) ** -0.5

    def block(i, qn, qr):
        t = i * QBLOCK + jnp.arange(QBLOCK)
        sc = (jnp.einsum('bhqd,bhkd->bhqk', qn, k_nope).astype(jnp.float32)
              + jnp.einsum('bhqr,bkr->bhqk', qr, k_r).astype(jnp.float32)) * scale
        p = _masked_softmax(sc, kpos[None, :] <= t[:, None])
        return jnp.einsum('bhqk,bhkd->bhqd', p.astype(v.dtype), v)

    o = _sweep(block, q_nope, q_rope)
    return _merge(_rms(o, o_norm))


def _diff(q, k, v, q_norm, k_norm, lq1, lk1, lq2, lk2, subln, slopes, lambda_init):
    b, s, _ = q.shape
    f32 = jnp.float32

    def split2(z):
        z = z.reshape(b, s, DIFF_HEADS, 2, DIFF_QK).transpose(3, 0, 2, 1, 4)
        return z[0], z[1]

    q1, q2 = split2(q)
    k1, k2 = split2(k)
    q1, q2 = _rms(q1, q_norm), _rms(q2, q_norm)
    k1, k2 = _rms(k1, k_norm), _rms(k2, k_norm)
    vh = _heads(v, DIFF_HEADS)
    lam = (jnp.exp(jnp.sum(lq1.astype(f32) * lk1.astype(f32)))
           - jnp.exp(jnp.sum(lq2.astype(f32) * lk2.astype(f32))) + lambda_init)
    kpos = jnp.arange(s)
    scale = DIFF_QK ** -0.5

    def block(i, q1b, q2b):
        t = i * QBLOCK + jnp.arange(QBLOCK)
        mask = kpos[None, :] <= t[:, None]
        bias = -slopes[:, None, None] * (t[:, None] - kpos[None, :]).astype(f32)
        s1 = jnp.einsum('bhqd,bhkd->bhqk', q1b, k1).astype(f32) * scale + bias
        s2 = jnp.einsum('bhqd,bhkd->bhqk', q2b, k2).astype(f32) * scale + bias
        p = _masked_softmax(s1, mask) - lam * _masked_softmax(s2, mask)
        return jnp.einsum('bhqk,bhkd->bhqd', p.astype(vh.dtype), vh)

    o = _sweep(block, q1, q2)
    return _merge(_rms(o, subln) * (1.0 - lambda_init))


def _nsa(q, kc_raw, vc_raw, ks, vs, kw, vw, gate_logits, q_norm, pe_k, w_ck, pe_v, w_cv,
         kc_norm, ks_norm, kw_norm, o_norm, slopes):
    b, s, _ = q.shape
    f32 = jnp.float32
    qh = _rms(_heads(q, NSA_HEADS), q_norm)
    gates = jax.nn.sigmoid(gate_logits.astype(f32)).reshape(b, s, NSA_HEADS, 3).transpose(0, 2, 1, 3)

    n_cmp = (s - NSA_CMP_LEN) // NSA_CMP_STRIDE + 1
    c_start = NSA_CMP_STRIDE * jnp.arange(n_cmp)
    tok = c_start[:, None] + jnp.arange(NSA_CMP_LEN)[None, :]
    c_end = c_start + NSA_CMP_LEN - 1

    def compress(z, pe, w):
        blocks = z[:, tok] + pe
        return blocks.reshape(b, n_cmp, NSA_CMP_LEN * NSA_D) @ w

    kc = _rms(compress(kc_raw, pe_k, w_ck), kc_norm)
    vc = compress(vc_raw, pe_v, w_cv)

    n_sb = s // NSA_SEL_BLOCK
    n_sel = min(NSA_N_SELECT, n_sb)
    ks_b = _rms(ks, ks_norm).reshape(b, n_sb, NSA_SEL_BLOCK, NSA_D)
    vs_b = vs.reshape(b, n_sb, NSA_SEL_BLOCK, NSA_D)
    sel_start = NSA_SEL_BLOCK * jnp.arange(n_sb)
    overlap = ((c_start[:, None] < sel_start[None, :] + NSA_SEL_BLOCK)
               & (c_start[:, None] + NSA_CMP_LEN > sel_start[None, :])).astype(f32)

    kw_p = jnp.pad(_rms(kw, kw_norm), ((0, 0), (NSA_WINDOW, 0), (0, 0)))
    vw_p = jnp.pad(vw, ((0, 0), (NSA_WINDOW, 0), (0, 0)))

    scale = NSA_D ** -0.5
    slope4 = slopes[None, :, None, None]
    jb = jnp.arange(n_sb)

    def block(i, qb, gb):
        t = i * QBLOCK + jnp.arange(QBLOCK)
        tf = t.astype(f32)
        sc = (jnp.einsum('bhqd,bnd->bhqn', qb, kc).astype(f32) * scale
              - slope4 * (tf[:, None] - c_end[None, :].astype(f32)))
        pc = _masked_softmax(sc, c_end[None, :] <= t[:, None])
        oc = jnp.einsum('bhqn,bnd->bhqd', pc.astype(vc.dtype), vc).astype(f32)
        imp = jnp.einsum('bhqn,nj->bqj', pc, overlap)
        cur = t // NSA_SEL_BLOCK
        valid = sel_start[None, :] <= t[:, None]
        forced = (jb[None, :] == 0) | (jb[None, :] == cur[:, None]) | (jb[None, :] == cur[:, None] - 1)
        score = jnp.where(valid, imp + jnp.where(forced, NSA_FORCE_BONUS, 0.0), NEG_INF)
        top_val, top_idx = lax.top_k(score, n_sel)
        picked = top_val > 0.5 * NEG_INF
        kg = jax.vmap(lambda kb_, ib_: kb_[ib_])(ks_b, top_idx)
        vg = jax.vmap(lambda vb_, ib_: vb_[ib_])(vs_b, top_idx)
        pos = top_idx[..., None] * NSA_SEL_BLOCK + jnp.arange(NSA_SEL_BLOCK)
        ms = picked[..., None] & (pos <= t[None, :, None, None])
        ss = (jnp.einsum('bhqd,bqnkd->bhqnk', qb, kg).astype(f32) * scale
              - slope4[..., None] * (tf[None, :, None, None] - pos.astype(f32))[:, None])
        m_tok = n_sel * NSA_SEL_BLOCK
        ps = _masked_softmax(ss.reshape(b, NSA_HEADS, QBLOCK, m_tok), ms.reshape(b, 1, QBLOCK, m_tok))
        osel = jnp.einsum('bhqm,bqmd->bhqd', ps.astype(vg.dtype), vg.reshape(b, QBLOCK, m_tok, NSA_D)).astype(f32)
        kwin = lax.dynamic_slice_in_dim(kw_p, i * QBLOCK, QBLOCK + NSA_WINDOW, axis=1)
        vwin = lax.dynamic_slice_in_dim(vw_p, i * QBLOCK, QBLOCK + NSA_WINDOW, axis=1)
        kp = i * QBLOCK - NSA_WINDOW + jnp.arange(QBLOCK + NSA_WINDOW)
        mw = (kp[None, :] <= t[:, None]) & (t[:, None] - kp[None, :] < NSA_WINDOW) & (kp[None, :] >= 0)
        sw = (jnp.einsum('bhqd,bkd->bhqk', qb, kwin).astype(f32) * scale
              - slope4 * (tf[:, None] - kp[None, :].astype(f32)))
        pw = _masked_softmax(sw, mw)
        ow = jnp.einsum('bhqk,bkd->bhqd', pw.astype(vwin.dtype), vwin).astype(f32)
        g = gb.astype(f32)
        o = g[..., 0:1] * oc + g[..., 1:2] * osel + g[..., 2:3] * ow
        return o.astype(qb.dtype)

    o = _sweep(block, qh, gates)
    return _merge(_rms(o, o_norm))


def _stick_breaking(q, k, v, o_norm):
    qh, kh, vh = _heads(q, SB_HEADS), _heads(k, SB_HEADS), _heads(v, SB_HEADS)
    s = qh.shape[2]
    kpos = jnp.arange(s)
    scale = SB_D ** -0.5

    def block(i, qb):
        t = i * QBLOCK + jnp.arange(QBLOCK)
        z = jnp.einsum('bhqd,bhkd->bhqk', qb, kh).astype(jnp.float32) * scale
        mask = kpos[None, :] < t[:, None]
        log_not = jnp.where(mask, jax.nn.log_sigmoid(-z), 0.0)
        after = lax.cumsum(log_not, axis=3, reverse=True) - log_not
        a = jnp.where(mask, jnp.exp(jax.nn.log_sigmoid(z) + after), 0.0)
        return jnp.einsum('bhqk,bhkd->bhqd', a.astype(vh.dtype), vh)

    o = _sweep(block, qh)
    return _merge(_rms(o, o_norm))


def setup_inputs(seed: int = 0) -> dict:
    key = jax.random.key(seed)
    keys = iter(jax.random.split(key, 64))
    L, D, F = DEPTH, D_MODEL, D_FF
    f32 = jnp.float32

    def dense(shape, fan_in):
        return jax.random.normal(next(keys), shape, f32) * fan_in ** -0.5

    def gain(shape):
        return 1.0 + 0.05 * jax.random.normal(next(keys), shape, f32)

    def small(shape, sc):
        return sc * jax.random.normal(next(keys), shape, f32)

    return {
        'x': jax.random.normal(next(keys), (BATCH, SEQ, D), f32),
        'ffn1_norm': gain((L, D)),
        'ffn1_w_gate': dense((L, D, F), D),
        'ffn1_w_up': dense((L, D, F), D),
        'ffn1_w_down': dense((L, F, D), F),
        'mix_norm': gain((L, D)),
        'w_in': dense((L, D, D_IN), D),
        'mla_cq_norm': gain((L, MLA_Q_RANK)),
        'mla_ckv_norm': gain((L, MLA_KV_RANK)),
        'mla_w_uq': dense((L, MLA_Q_RANK, MLA_HEADS * (MLA_NOPE + MLA_ROPE)), MLA_Q_RANK),
        'mla_w_ukv': dense((L, MLA_KV_RANK, MLA_HEADS * (MLA_NOPE + MLA_V)), MLA_KV_RANK),
        'mla_qn_norm': gain((L, MLA_NOPE)),
        'mla_qr_norm': gain((L, MLA_ROPE)),
        'mla_kn_norm': gain((L, MLA_NOPE)),
        'mla_kr_norm': gain((L, MLA_ROPE)),
        'mla_o_norm': gain((L, MLA_V)),
        'diff_q_norm': gain((L, DIFF_QK)),
        'diff_k_norm': gain((L, DIFF_QK)),
        'diff_lq1': small((L, DIFF_QK), 0.1),
        'diff_lk1': small((L, DIFF_QK), 0.1),
        'diff_lq2': small((L, DIFF_QK), 0.1),
        'diff_lk2': small((L, DIFF_QK), 0.1),
        'diff_subln': gain((L, DIFF_V)),
        'nsa_q_norm': gain((L, NSA_D)),
        'nsa_pe_k': small((L, NSA_CMP_LEN, NSA_D), 0.1),
        'nsa_w_ck': dense((L, NSA_CMP_LEN * NSA_D, NSA_D), NSA_CMP_LEN * NSA_D),
        'nsa_pe_v': small((L, NSA_CMP_LEN, NSA_D), 0.1),
        'nsa_w_cv': dense((L, NSA_CMP_LEN * NSA_D, NSA_D), NSA_CMP_LEN * NSA_D),
        'nsa_kc_norm': gain((L, NSA_D)),
        'nsa_ks_norm': gain((L, NSA_D)),
        'nsa_kw_norm': gain((L, NSA_D)),
        'nsa_o_norm': gain((L, NSA_D)),
        'sb_o_norm': gain((L, SB_D)),
        'w_out': dense((L, MIX_WIDTH, D), MIX_WIDTH),
        'ffn2_norm': gain((L, D)),
        'ffn2_w_gate': dense((L, D, F), D),
        'ffn2_w_up': dense((L, D, F), D),
        'ffn2_w_down': dense((L, F, D), F),
    }


def reference(x, ffn1_norm, ffn1_w_gate, ffn1_w_up, ffn1_w_down, mix_norm, w_in,
              mla_cq_norm, mla_ckv_norm, mla_w_uq, mla_w_ukv, mla_qn_norm, mla_qr_norm,
              mla_kn_norm, mla_kr_norm, mla_o_norm,
              diff_q_norm, diff_k_norm, diff_lq1, diff_lk1, diff_lq2, diff_lk2, diff_subln,
              nsa_q_norm, nsa_pe_k, nsa_w_ck, nsa_pe_v, nsa_w_cv, nsa_kc_norm, nsa_ks_norm,
              nsa_kw_norm, nsa_o_norm, sb_o_norm, w_out,
              ffn2_norm, ffn2_w_gate, ffn2_w_up, ffn2_w_down):
    s = x.shape[1]
    pos = jnp.arange(s, dtype=jnp.float32)
    inv_freq = ROPE_THETA ** (-jnp.arange(0, MLA_ROPE, 2, dtype=jnp.float32) / MLA_ROPE)
    ang = pos[:, None] * inv_freq[None, :]
    cos, sin = jnp.cos(ang), jnp.sin(ang)
    slopes = _alibi_slopes(N_ALIBI_HEADS)
    diff_slopes, nsa_slopes = slopes[0::2], slopes[1::2]
    offsets = _in_offsets()

    h = x
    for l in range(DEPTH):
        h = h + 0.5 * _swiglu(_rms(h, ffn1_norm[l]), ffn1_w_gate[l], ffn1_w_up[l], ffn1_w_down[l])
        u = _rms(h, mix_norm[l]) @ w_in[l]
        (a_cq, a_ckv, a_kr, b_q, b_k, b_v, c_q, c_kc, c_vc, c_ks, c_vs, c_kw, c_vw, c_g,
         d_q, d_k, d_v) = jnp.split(u, offsets, axis=-1)
        lambda_init = 0.8 - 0.6 * math.exp(-0.3 * l)
        o_a = _mla(a_cq, a_ckv, a_kr, mla_cq_norm[l], mla_ckv_norm[l], mla_w_uq[l], mla_w_ukv[l],
                   mla_qn_norm[l], mla_qr_norm[l], mla_kn_norm[l], mla_kr_norm[l], mla_o_norm[l], cos, sin)
        o_b = _diff(b_q, b_k, b_v, diff_q_norm[l], diff_k_norm[l], diff_lq1[l], diff_lk1[l],
                    diff_lq2[l], diff_lk2[l], diff_subln[l], diff_slopes, lambda_init)
        o_c = _nsa(c_q, c_kc, c_vc, c_ks, c_vs, c_kw, c_vw, c_g, nsa_q_norm[l], nsa_pe_k[l], nsa_w_ck[l],
                   nsa_pe_v[l], nsa_w_cv[l], nsa_kc_norm[l], nsa_ks_norm[l], nsa_kw_norm[l],
                   nsa_o_norm[l], nsa_slopes)
        o_d = _stick_breaking(d_q, d_k, d_v, sb_o_norm[l])
        h = h + jnp.concatenate([o_a, o_b, o_c, o_d], axis=-1) @ w_out[l]
        h = h + 0.5 * _swiglu(_rms(h, ffn2_norm[l]), ffn2_w_gate[l], ffn2_w_up[l], ffn2_w_down[l])
    return h
```

```python
import math
from contextlib import ExitStack

import numpy as np
import ml_dtypes

import concourse.bass as bass
import concourse.mybir as mybir
from concourse.bass_utils import run_bass_kernel_spmd

F32 = mybir.dt.float32
BF16 = mybir.dt.bfloat16
AF = mybir.ActivationFunctionType
ALU = mybir.AluOpType
AX = mybir.AxisListType

D = 2048
S = 4096
DEPTH = 2
DFF = 5632
NFC = DFF // 128
DIN = 5196
T = 512
NTT = S // T
EPS = 1e-6
NSLOPE = 8
SLOPES = [2.0 ** (-(i + 1)) for i in range(8)]
DIFF_SLOPES = SLOPES[0::2]
NSA_SLOPES = SLOPES[1::2]

DEBUG = None
ONLY = None


class Buf:
    __slots__ = ("t", "name", "lw", "rd", "sem", "ndma")

    def __init__(self, t, name):
        self.t = t
        self.name = name
        self.lw = None
        self.rd = []
        self.sem = None
        self.ndma = 0

    def __getitem__(self, key):
        return self.t[key]


class Op:
    __slots__ = ("eng", "dma", "calls", "deps", "inc", "count", "sem", "key")

    def __init__(self, eng, dma, calls, key=None):
        self.eng = eng
        self.dma = dma
        self.calls = calls
        self.deps = set()
        self.inc = False
        self.count = 0
        self.sem = None
        self.key = key


class KB:
    CENG = ("pe", "act", "dve", "pool")

    def __init__(self, nc, es):
        self.nc = nc
        self.es = es
        self.engs = {"pe": nc.tensor, "act": nc.scalar, "dve": nc.vector, "pool": nc.gpsimd, "sp": nc.sync}
        self.ops = []
        self.base = 0
        self.csem = {e: es.enter_context(nc.semaphore("cs_" + e)) for e in self.CENG}
        self.ccount = {e: 0 for e in self.CENG}
        self.seen = {e: {} for e in self.engs}
        self.keys = []
        self.bufs = []
        self.n_ins = 0

    def sb(self, es, name, cols, dtype):
        self.uid = getattr(self, "uid", 0) + 1
        name = f"{name}_{self.uid}"
        t = es.enter_context(self.nc.sbuf_tensor(name, [128, cols], dtype))
        b = Buf(t, name)
        self.bufs.append(b)
        return b

    def ps(self, es, name, cols, dtype=F32):
        self.uid = getattr(self, "uid", 0) + 1
        name = f"{name}_{self.uid}"
        t = es.enter_context(self.nc.psum_tensor(name, [128, cols], dtype))
        b = Buf(t, name)
        self.bufs.append(b)
        return b

    def _track(self, op, idx, r, w):
        for b in w:
            if b.lw is not None:
                op.deps.add(b.lw)
            op.deps.update(b.rd)
        for b in r:
            if b.lw is not None:
                op.deps.add(b.lw)
        for b in w:
            b.lw = idx
            b.rd = []
        for b in r:
            if b not in w:
                b.rd.append(idx)
                if len(b.rd) > 64:
                    keep = {}
                    rest = []
                    for i in b.rd:
                        o = self.ops[i]
                        if o.dma:
                            rest.append(i)
                        else:
                            keep[o.eng] = i
                    b.rd = rest + list(keep.values())

    def op(self, eng, method, *args, r=(), w=(), **kw):
        o = Op(eng, False, [(method, args, kw)])
        idx = len(self.ops)
        self.ops.append(o)
        self._track(o, idx, r, w)
        return idx

    def dma(self, eng, out, in_, key, r=(), w=()):
        o = Op(eng, True, [("dma_start", (), {"out": out, "in_": in_})], key=key)
        idx = len(self.ops)
        self.ops.append(o)
        self._track(o, idx, r, w)
        return idx

    def flush(self, final=False):
        nc = self.nc
        ops = self.ops
        n = len(ops)
        for i in range(self.base, n):
            o = ops[i]
            for d in o.deps:
                if d >= self.base:
                    od = ops[d]
                    if not (od.eng == "pe" and o.eng == "pe" and not od.dma and not o.dma):
                        od.inc = True
        last = {}
        for i in range(self.base, n):
            o = ops[i]
            if not o.dma:
                last[o.eng] = i
        for e, i in last.items():
            ops[i].inc = True
        for i in range(self.base, n):
            o = ops[i]
            if o.dma:
                kb = o.key
                if kb.sem is None:
                    pool = self.__dict__.setdefault("sempool", [])
                    if pool:
                        kb.sem, kb.ndma = pool.pop()
                    else:
                        kb.sem = self.es.enter_context(nc.semaphore("ds_" + kb.name))
                        kb.ndma = 0
                    self.keys.append(kb)
                kb.ndma += 1
                o.sem = kb.sem
                o.count = 16 * kb.ndma
            else:
                if o.inc:
                    self.ccount[o.eng] += 1
                o.sem = self.csem[o.eng]
                o.count = self.ccount[o.eng]
        for i in range(self.base, n):
            o = ops[i]
            E = self.engs[o.eng]
            seen = self.seen[o.eng]
            need = {}
            for d in o.deps:
                if d < self.base:
                    continue
                od = ops[d]
                if od.eng == "pe" and o.eng == "pe" and not od.dma and not o.dma:
                    continue
                if not od.dma and not od.inc:
                    raise RuntimeError("dep without inc")
                key = od.sem
                if need.get(key, (None, 0))[1] < od.count:
                    need[key] = (od.sem, od.count)
            for key, (sem, cnt) in need.items():
                if seen.get(key, 0) < cnt:
                    E.wait_ge(sem, cnt)
                    seen[key] = cnt
                    self.n_ins += 1
            ins = None
            for (m, a, kw) in o.calls:
                ins = getattr(E, m)(*a, **kw)
                self.n_ins += 1
            if o.dma:
                ins.then_inc(o.sem, 16)
            elif o.inc:
                ins.then_inc(o.sem, 1)
        targets = [(self.csem[e], self.ccount[e]) for e in self.CENG if self.ccount[e] > 0]
        targets += [(kb.sem, 16 * kb.ndma) for kb in self.keys]
        wait_engs = ["sp"] if final else list(self.engs.keys())
        for e in wait_engs:
            E = self.engs[e]
            seen = self.seen[e]
            for (sem, cnt) in targets:
                if seen.get(sem, 0) < cnt:
                    E.wait_ge(sem, cnt)
                    seen[sem] = cnt
                    self.n_ins += 1
        self.base = n
        pool = self.__dict__.setdefault("sempool", [])
        for kb in self.keys:
            pool.append((kb.sem, kb.ndma))
            kb.sem = None
        self.keys = []
        for b in self.bufs:
            b.lw = None
            b.rd = []


def _consts():
    c = {}
    p = np.arange(128)
    ident = np.eye(128, dtype=np.float32)
    ones = np.ones((128, 128), np.float32)
    blk64 = np.kron(np.eye(2), np.ones((64, 64))).astype(np.float32)
    tri = (p[:, None] <= p[None, :]).astype(np.float32)
    tris = (p[:, None] < p[None, :]).astype(np.float32)
    atri = (p[:, None] > p[None, :]).astype(np.float32)
    ntri_incl = -(p[:, None] >= p[None, :]).astype(np.float32)
    nones = -ones
    prot = np.zeros((128, 128), np.float32)
    for m in range(32):
        prot[m + 32, m] = -1.0
    for m in range(32, 64):
        prot[m - 32, m] = 1.0
    cm = np.zeros((128, 17, 128), np.float32)
    for j in range(17):
        cm[:, j, :] = ((16 * p[:, None] + 31 - p[None, :]) <= 128 * j)
    n = np.arange(256)
    cst = 16 * n
    sel_start = 64 * np.arange(64)
    ov = ((cst[:, None] < sel_start[None, :] + 64) & (cst[:, None] + 32 > sel_start[None, :])).astype(np.float32)
    ov[255] = 0
    ov2 = ov.reshape(2, 128, 64).transpose(1, 0, 2)
    cb = np.concatenate([ident, ones, blk64, tri, tris, atri, ntri_incl, nones, prot,
                         cm.reshape(128, -1), ov2.reshape(128, -1)], axis=1)
    c["cbf"] = cb.astype(ml_dtypes.bfloat16)
    off = {}
    o = 0
    for name, w in [("ident", 128), ("ones", 128), ("blk64", 128), ("tri", 128), ("tris", 128), ("atri", 128),
                    ("ntri", 128), ("nones", 128), ("prot", 128), ("cm", 17 * 128), ("ov", 128)]:
        off[name] = o
        o += w
    c["cbf_off"] = off
    c["cbf_w"] = o
    pos = np.arange(S, dtype=np.float32)
    inv_freq = (10000.0 ** (-np.arange(0, 64, 2, dtype=np.float32) / 64)).astype(np.float32)
    ang = pos[:, None] * inv_freq[None, :]
    cosT = np.concatenate([np.cos(ang), np.cos(ang)], axis=1).T
    sinT = np.concatenate([np.sin(ang), np.sin(ang)], axis=1).T
    c["cosT"] = np.ascontiguousarray(cosT).astype(np.float32)
    c["sinT"] = np.ascontiguousarray(sinT).astype(np.float32)
    j = np.arange(32)
    base = 128.0 * (j[None, :] - 31) + p[:, None] - 127.0
    ab = np.stack([s * base for s in SLOPES], axis=1)
    nt = np.arange(2)
    qbg = np.arange(32)
    cbase = 16.0 * (128 * nt[None, :, None] + p[:, None, None]) + 31 - 128.0 * qbg[None, None, :] - 127.0
    cbias = np.stack([s * cbase for s in NSA_SLOPES], axis=1)
    cbias = np.minimum(cbias, 60.0)
    t = (128 * qbg[None, :] + p[:, None])
    jb = np.arange(64)
    cur = t // 64
    valid = sel_start[None, None, :] <= t[:, :, None]
    forced = (jb[None, None, :] == 0) | (jb[None, None, :] == cur[:, :, None]) | (jb[None, None, :] == cur[:, :, None] - 1)
    bonus = np.where(valid, np.where(forced, 1e4, 0.0), -1e30).astype(np.float32)
    cf = np.concatenate([ab.reshape(128, -1), cbias.reshape(128, -1)], axis=1)
    c["cf32"] = cf.astype(np.float32)
    c["bonus"] = bonus.reshape(128, -1).astype(np.float32)
    c["cf_off"] = {"ab": 0, "cbias": 256}
    c["cf_w"] = cf.shape[1]
    return c


GC = {}


def _gain_cols(inputs, l):
    cols = []

    def add(name, v):
        GC[name] = len(cols)
        cols.append(np.asarray(v, np.float32).reshape(128))

    for c in range(4):
        add(f"cq{c}", inputs["mla_cq_norm"][l][c * 128:(c + 1) * 128])
    for c in range(2):
        add(f"ckv{c}", inputs["mla_ckv_norm"][l][c * 128:(c + 1) * 128])
    add("qn", inputs["mla_qn_norm"][l])
    add("qr", np.tile(inputs["mla_qr_norm"][l], 2))
    add("kn", inputs["mla_kn_norm"][l])
    add("kr", np.tile(inputs["mla_kr_norm"][l], 2))
    add("dq", np.tile(inputs["diff_q_norm"][l], 2))
    add("dk", np.tile(inputs["diff_k_norm"][l], 2))
    add("nq", inputs["nsa_q_norm"][l])
    add("nkc", inputs["nsa_kc_norm"][l])
    add("nks", inputs["nsa_ks_norm"][l])
    add("nkw", inputs["nsa_kw_norm"][l])
    return np.stack(cols, axis=1)


NGC = 16


def build_program(C, stop_after=None, debug_out=False):
    nc = bass.Bass("TRN2", target_bir_lowering=False)

    def din(name, shape, dt=F32):
        return nc.dram_tensor(name, list(shape), dt, kind="ExternalInput").ap()

    skind = "ExternalOutput" if debug_out else "Internal"

    def dscr(name, shape, dt):
        return nc.dram_tensor(name, list(shape), dt, kind=skind).ap()

    x = din("x", [S, D])
    W = {}
    for nm, shp in [("ffn1_w_gate", [DEPTH, D, DFF]), ("ffn1_w_up", [DEPTH, D, DFF]), ("ffn1_w_down", [DEPTH, DFF, D]),
                    ("ffn2_w_gate", [DEPTH, D, DFF]), ("ffn2_w_up", [DEPTH, D, DFF]), ("ffn2_w_down", [DEPTH, DFF, D]),
                    ("w_in", [DEPTH, D, DIN]), ("w_out", [DEPTH, D, D]),
                    ("mla_w_uq", [DEPTH, 512, 768]), ("mla_w_ukv", [DEPTH, 256, 1024]),
                    ("nsa_w_ck", [DEPTH, 4096, 128]), ("nsa_w_cv", [DEPTH, 4096, 128]),
                    ("nsa_pe_k", [DEPTH, 32, 128]), ("nsa_pe_v", [DEPTH, 32, 128]),
                    ("ffn1_norm", [DEPTH, D]), ("mix_norm", [DEPTH, D]), ("ffn2_norm", [DEPTH, D]),
                    ("mla_o_norm", [DEPTH, 128]), ("diff_subln", [DEPTH, 128]), ("nsa_o_norm", [DEPTH, 128]),
                    ("sb_o_norm", [DEPTH, 128]),
                    ("diff_lq1", [DEPTH, 64]), ("diff_lk1", [DEPTH, 64]), ("diff_lq2", [DEPTH, 64]), ("diff_lk2", [DEPTH, 64]),
                    ("gcols", [DEPTH, 128, NGC])]:
        W[nm] = din(nm, shp)
    cbf_d = din("cbf", [128, C["cbf_w"]], BF16)
    cf_d = din("cf32", [128, C["cf_w"]])
    cos_d = din("cosT", [64, S])
    sin_d = din("sinT", [64, S])
    bonus_d = din("bonus", [128, 2048])
    W["bonus"] = bonus_d
    y = nc.dram_tensor("y", [S, D], F32, kind="ExternalOutput").ap()

    hS1 = dscr("hS1", [S, D], F32)
    hS3 = dscr("hS3", [S, D], F32)
    QT = dscr("QT", [20, 128, S], BF16)
    KT = dscr("KT", [18, 128, S], BF16)
    VV = dscr("VV", [14, S, 128], BF16)
    GT = dscr("GT", [S, 12], F32)
    OT = dscr("OT", [16, 128, S], BF16)

    with ExitStack() as es:
        k = KB(nc, es)
        CO = C["cbf_off"]
        cbf = k.sb(es, "cbf", C["cbf_w"], BF16)
        cf = k.sb(es, "cf", C["cf_w"], F32)
        k.dma("sp", cbf[:], cbf_d[:, :], key=cbf, w=[cbf])
        k.dma("sp", cf[:], cf_d[:, :], key=cf, w=[cf])
        if DEBUG == "recompile":
            k.op("dve", "memset", cf[:, 0:1], 0.0, w=[cf])

        def cb(name, rows=128, cols=128, c0=0):
            o = CO[name] + c0
            return cbf[0:rows, o:o + cols]

        for l in range(DEPTH):
            lam_init = 0.8 - 0.6 * math.exp(-0.3 * l)
            src = x if l == 0 else hS3
            with ExitStack() as pa:
                phase_tokens(k, pa, nc, C, W, l, "A", src, hS1, cb, cbf, cf, cos_d, sin_d,
                             QT, KT, VV, GT, OT, lam_init)
                k.flush()
            if stop_after == f"A{l}":
                break
            with ExitStack() as pb:
                phase_attn(k, pb, nc, C, W, l, cb, cbf, cf, QT, KT, VV, GT, OT, lam_init, only=ONLY)
                k.flush()
            if stop_after == f"B{l}":
                break
            dst = y if l == DEPTH - 1 else hS3
            with ExitStack() as pc:
                phase_tokens(k, pc, nc, C, W, l, "C", hS1, dst, cb, cbf, cf, cos_d, sin_d,
                             QT, KT, VV, GT, OT, lam_init)
                k.flush()
            if stop_after == f"C{l}":
                break
        k.flush(final=True)
        print("instructions:", k.n_ins, "ops:", len(k.ops), "dma sems:", len(k.keys))
    return nc


def phase_tokens(k, es, nc, C, W, l, which, src, dst, cb, cbf, cf, cos_d, sin_d, QT, KT, VV, GT, OT, lam_init):
    A = which == "A"
    htile = k.sb(es, "htile", 4 * D, F32)
    xnb = [k.sb(es, f"xnb{i}", D, BF16) for i in range(2)]
    xnT = k.sb(es, "xnT", 16 * T, BF16)
    actT = k.sb(es, "actT", NFC * T, BF16)
    WB = [k.sb(es, f"WB{i}", 8192, BF16) for i in range(2)]
    sg = [k.sb(es, f"sg{i}", T, F32) for i in range(2)]
    gbc = k.sb(es, "gbc", D, F32)
    ss = k.sb(es, "ss", 8, F32)
    PG = [k.ps(es, f"PG{i}", 512) for i in range(2)]
    PU = [k.ps(es, f"PU{i}", 512) for i in range(2)]
    PD = [k.ps(es, f"PD{i}", 512) for i in range(2)]
    PM = k.ps(es, "PM", 512)
    PT = k.ps(es, "PT", 1024, BF16)
    wbi = [0]

    def nextwb():
        b = WB[wbi[0] % 2]
        wbi[0] += 1
        return b

    fn = "ffn1" if A else "ffn2"
    if A:
        gc = k.sb(es, "gc", NGC, F32)
        gcs = k.sb(es, "gcs", NGC, F32)
        k.dma("sp", gc[:], W["gcols"][l], key=gc, w=[gc])
        for nm, sc in [("qn", 192 ** -0.5), ("qr", 192 ** -0.5), ("dq", 64 ** -0.5), ("nq", 128 ** -0.5)]:
            j = GC[nm]
            k.op("dve", "tensor_scalar", out=gcs[:, j:j + 1], in0=gc[:, j:j + 1], scalar1=float(sc), scalar2=None,
                 op0=ALU.mult, r=[gc], w=[gcs])
        wuq = k.sb(es, "wuq", 4 * 768, BF16)
        wukv = k.sb(es, "wukv", 2 * 1024, BF16)
        k.dma("pool", wuq[:].rearrange("p (c n) -> p c n", c=4),
              W["mla_w_uq"][l].rearrange("(c p) n -> p c n", p=128), key=wuq, w=[wuq])
        k.dma("pool", wukv[:].rearrange("p (c n) -> p c n", c=2),
              W["mla_w_ukv"][l].rearrange("(c p) n -> p c n", p=128), key=wukv, w=[wukv])
        cqf = k.sb(es, "cqf", 4 * T, BF16)
        cqn = k.sb(es, "cqn", 4 * T, BF16)
        ckvn = k.sb(es, "ckvn", 2 * T, BF16)
        sqb = [k.sb(es, f"sqb{i}", T, BF16) for i in range(2)]
        rstd = [k.sb(es, f"rstd{i}", T, F32) for i in range(2)]
        yf = k.sb(es, "yf", T, F32)
        ybf = [k.sb(es, f"ybf{i}", T, BF16) for i in range(3)]
        t1 = k.sb(es, "t1", T, F32)
        t2 = k.sb(es, "t2", T, F32)
        cosb = k.sb(es, "cosb", T, F32)
        sinb = k.sb(es, "sinb", T, F32)
        vout = [k.sb(es, f"vout{i}", 4 * 512, BF16) for i in range(2)]
        gout = k.sb(es, "gout", 4 * 12, F32)
        cnt = {"y": 0, "sq": 0, "v": 0}

    def rms_xnT(gname):
        gain_b = gbc
        k.dma("sp", gbc[:], W[gname][l].partition_broadcast(128), key=gbc, w=[gbc])
        for tb in range(4):
            sqj = xnb[(tb + 1) % 2]
            hb = htile[:, tb * D:(tb + 1) * D]
            k.op("dve", "memset", ss[:, tb:tb + 1], 0.0, w=[ss])
            k.op("act", "activation", out=sqj[:], in_=hb, func=AF.Square, accum_out=ss[:, tb:tb + 1],
                 r=[htile], w=[sqj, ss])
            k.op("act", "activation", out=ss[:, 4 + tb:5 + tb], in_=ss[:, tb:tb + 1], func=AF.Sqrt, scale=1.0 / D, bias=EPS,
                 r=[ss], w=[ss])
            k.op("dve", "reciprocal", out=ss[:, 4 + tb:5 + tb], in_=ss[:, 4 + tb:5 + tb], r=[ss], w=[ss])
            xb = xnb[tb % 2]
            k.op("dve", "scalar_tensor_tensor", out=xb[:], in0=hb, scalar=ss[:, 4 + tb:5 + tb], in1=gain_b[:],
                 op0=ALU.mult, op1=ALU.mult, r=[htile, ss, gain_b], w=[xb])
            for half in range(2):
                for c in range(8):
                    dc = half * 8 + c
                    k.op("pe", "transpose", out=PT[:, c * 128:(c + 1) * 128], in_=xb[:, dc * 128:(dc + 1) * 128],
                         identity=cb("ident"), r=[xb, cbf], w=[PT])
                dstv = xnT[:].rearrange("p (c t) -> p c t", c=16)[:, half * 8:(half + 1) * 8, tb * 128:(tb + 1) * 128]
                k.op("act", "copy", out=dstv, in_=PT[:].rearrange("p (c t) -> p c t", c=8), r=[PT], w=[xnT])

    def ffn(wg, wu, wd):
        xv = xnT[:].rearrange("p (c t) -> p c t", c=16)
        av = actT[:].rearrange("p (c t) -> p c t", c=NFC)
        pi = 0
        for fg in range(NFC // 2):
            wb = nextwb()
            wv = wb[:, 0:8192].rearrange("p (g c n) -> p g c n", g=2, c=16)
            k.dma("pool", wv[:, 0], wg[:, fg * 256:(fg + 1) * 256].rearrange("(c p) n -> p c n", p=128), key=wb, w=[wb])
            k.dma("pool", wv[:, 1], wu[:, fg * 256:(fg + 1) * 256].rearrange("(c p) n -> p c n", p=128), key=wb, w=[wb])
            for fc in range(2):
                pg, pu, sgb = PG[pi % 2], PU[pi % 2], sg[pi % 2]
                pi += 1
                for dc in range(16):
                    k.op("pe", "matmul", pg[:], lhsT=wv[:, 0, dc, fc * 128:(fc + 1) * 128], rhs=xv[:, dc, :],
                         start=(dc == 0), stop=(dc == 15), r=[wb, xnT], w=[pg])
                for dc in range(16):
                    k.op("pe", "matmul", pu[:], lhsT=wv[:, 1, dc, fc * 128:(fc + 1) * 128], rhs=xv[:, dc, :],
                         start=(dc == 0), stop=(dc == 15), r=[wb, xnT], w=[pu])
                k.op("act", "activation", out=sgb[:], in_=pg[:], func=AF.Silu, r=[pg], w=[sgb])
                k.op("dve", "tensor_tensor", out=av[:, fg * 2 + fc, :], in0=sgb[:], in1=pu[:], op=ALU.mult,
                     r=[sgb, pu], w=[actT])
        for cg in range(16):
            wb = nextwb()
            wv = wb[:, 0:NFC * 128].rearrange("p (c n) -> p c n", c=NFC)
            for hh in range(2):
                k.dma("pool", wv[:, hh * 22:(hh + 1) * 22, :],
                      wd[hh * 2816:(hh + 1) * 2816, cg * 128:(cg + 1) * 128].rearrange("(c p) n -> p c n", p=128),
                      key=wb, w=[wb])
            for tb in range(4):
                pd = PD[(cg * 4 + tb) % 2]
                for fc in range(NFC):
                    k.op("pe", "matmul", pd[:, 0:128], lhsT=av[:, fc, tb * 128:(tb + 1) * 128], rhs=wv[:, fc, :],
                         start=(fc == 0), stop=(fc == NFC - 1), r=[wb, actT], w=[pd])
                hv = htile[:, tb * D + cg * 128: tb * D + (cg + 1) * 128]
                k.op("dve", "scalar_tensor_tensor", out=hv, in0=pd[:, 0:128], scalar=0.5, in1=hv,
                     op0=ALU.mult, op1=ALU.add, r=[pd], w=[htile])

    def load_h(tt, srcap):
        k.dma("sp", htile[:].rearrange("p (b d) -> p b d", b=4),
              srcap[tt * T:(tt + 1) * T, :].rearrange("(b p) d -> p b d", p=128), key=htile, w=[htile])

    def store_h(tt, dstap):
        k.dma("sp", dstap[tt * T:(tt + 1) * T, :].rearrange("(b p) d -> p b d", p=128),
              htile[:].rearrange("p (b d) -> p b d", b=4), key=htile, r=[htile])

    def fm_finish(ps, n, tt, norm, gcol, scale, rope, dest, keep=None):
        yb = ybf[cnt["y"] % 3]
        cnt["y"] += 1
        if norm is None:
            k.op("act", "activation", out=yb[0:n, :], in_=ps[0:n, 0:T], func=AF.Copy, scale=float(scale), r=[ps], w=[yb])
        else:
            sq = sqb[cnt["sq"] % 2]
            rs = rstd[cnt["sq"] % 2]
            cnt["sq"] += 1
            k.op("act", "activation", out=sq[0:n, :], in_=ps[0:n, 0:T], func=AF.Square, r=[ps], w=[sq])
            onesm = cb("ones", n, n) if norm == 128 else cb("blk64", n, n)
            k.op("pe", "matmul", PM[0:n, 0:T], lhsT=onesm, rhs=sq[0:n, :], start=True, stop=True, r=[sq, cbf], w=[PM])
            k.op("act", "activation", out=rs[0:n, :], in_=PM[0:n, 0:T], func=AF.Sqrt, scale=1.0 / norm, bias=EPS,
                 r=[PM], w=[rs])
            k.op("dve", "reciprocal", out=rs[0:n, :], in_=rs[0:n, :], r=[rs], w=[rs])
            if not rope:
                k.op("dve", "scalar_tensor_tensor", out=yb[0:n, :], in0=ps[0:n, 0:T], scalar=gcol[0:n, :], in1=rs[0:n, :],
                     op0=ALU.mult, op1=ALU.mult, r=[ps, rs, gc, gcs], w=[yb])
            else:
                k.op("dve", "scalar_tensor_tensor", out=yf[0:n, :], in0=ps[0:n, 0:T], scalar=gcol[0:n, :], in1=rs[0:n, :],
                     op0=ALU.mult, op1=ALU.mult, r=[ps, rs, gc, gcs], w=[yf])
                yb2 = ybf[cnt["y"] % 3]
                cnt["y"] += 1
                k.op("act", "copy", out=yb2[0:n, :], in_=yf[0:n, :], r=[yf], w=[yb2])
                k.op("pe", "matmul", PM[0:n, 0:T], lhsT=cb("prot", n, n), rhs=yb2[0:n, :], start=True, stop=True,
                     r=[yb2, cbf], w=[PM])
                k.op("dve", "tensor_tensor", out=t1[0:n, :], in0=yf[0:n, :], in1=cosb[0:n, :], op=ALU.mult,
                     r=[yf, cosb], w=[t1])
                k.op("dve", "tensor_tensor", out=t2[0:n, :], in0=PM[0:n, 0:T], in1=sinb[0:n, :], op=ALU.mult,
                     r=[PM, sinb], w=[t2])
                k.op("dve", "tensor_tensor", out=yb[0:n, :], in0=t1[0:n, :], in1=t2[0:n, :], op=ALU.add,
                     r=[t1, t2], w=[yb])
        k.dma("sp", dest, yb[0:n, :], key=yb, r=[yb])

    def proj_mm_fm(ps, wv, nin, c0, n, inv):
        for c in range(nin):
            k.op("pe", "matmul", ps[0:n, 0:T], lhsT=wv[:, c, c0:c0 + n], rhs=inv[:, c, :], start=(c == 0),
                 stop=(c == nin - 1), r=list(proj_r), w=[ps])

    proj_r = []

    def load_win(c0, n):
        wb = nextwb()
        wv = wb[:, 0:16 * n].rearrange("p (c n) -> p c n", c=16)
        k.dma("pool", wv, W["w_in"][l][:, c0:c0 + n].rearrange("(c p) n -> p c n", p=128), key=wb, w=[wb])
        return wb, wv

    def tm_group(wb, wv, nin, c0, n, inv, tt, dest_list, sigmoid=False):
        vo = vout[cnt["v"] % 2]
        cnt["v"] += 1
        vov = vo[:].rearrange("p (b n) -> p b n", b=4)
        for tb in range(4):
            pd = PD[tb % 2]
            for c in range(nin):
                k.op("pe", "matmul", pd[:, 0:n], lhsT=inv[:, c, tb * 128:(tb + 1) * 128], rhs=wv[:, c, c0:c0 + n],
                     start=(c == 0), stop=(c == nin - 1), r=list(proj_r), w=[pd])
            if sigmoid:
                k.op("act", "activation", out=gout[:, tb * 12:(tb + 1) * 12], in_=pd[:, 0:n], func=AF.Sigmoid,
                     r=[pd], w=[gout])
            else:
                k.op("act", "copy", out=vov[:, tb, 0:n], in_=pd[:, 0:n], r=[pd], w=[vo])
        if sigmoid:
            k.dma("sp", GT[tt * T:(tt + 1) * T, :].rearrange("(b p) n -> p b n", p=128),
                  gout[:].rearrange("p (b n) -> p b n", b=4), key=gout, r=[gout])
        else:
            for (co, wd_, dap) in dest_list:
                k.dma("sp", dap[tt * T:(tt + 1) * T, :].rearrange("(b p) n -> p b n", p=128),
                      vov[:, :, co:co + wd_], key=vo, r=[vo])

    def projections(tt):
        xv = xnT[:].rearrange("p (c t) -> p c t", c=16)
        tsl = slice(tt * T, (tt + 1) * T)
        k.dma("sp", cosb[0:64, :], cos_d[:, tsl], key=cosb, w=[cosb])
        k.dma("sp", sinb[0:64, :], sin_d[:, tsl], key=sinb, w=[sinb])
        gcol = lambda nm: gc[:, GC[nm]:GC[nm] + 1]
        gscol = lambda nm: gcs[:, GC[nm]:GC[nm] + 1]
        wb, wv = load_win(0, 512)
        proj_r[:] = [wb, xnT]
        cqv = cqf[:].rearrange("p (c t) -> p c t", c=4)
        for c in range(4):
            ps = PG[c % 2]
            proj_mm_fm(ps, wv, 16, c * 128, 128, xv)
            k.op("act", "copy", out=cqv[:, c, :], in_=ps[:, 0:T], r=[ps], w=[cqf])
            sq = sqb[c % 2]
            k.op("act", "activation", out=sq[:], in_=ps[:, 0:T], func=AF.Square, r=[ps], w=[sq])
            k.op("pe", "matmul", PM[:, 0:T], lhsT=cb("ones"), rhs=sq[:], start=(c == 0), stop=(c == 3),
                 r=[sq, cbf], w=[PM])
        rs = rstd[0]
        k.op("act", "activation", out=rs[:], in_=PM[:, 0:T], func=AF.Sqrt, scale=1.0 / 512, bias=EPS, r=[PM], w=[rs])
        k.op("dve", "reciprocal", out=rs[:], in_=rs[:], r=[rs], w=[rs])
        cqnv = cqn[:].rearrange("p (c t) -> p c t", c=4)
        for c in range(4):
            k.op("dve", "scalar_tensor_tensor", out=cqnv[:, c, :], in0=cqv[:, c, :], scalar=gcol(f"cq{c}"), in1=rs[:],
                 op0=ALU.mult, op1=ALU.mult, r=[cqf, rs, gc], w=[cqn])
        wb, wv = load_win(512, 256 + 64)
        proj_r[:] = [wb, xnT]
        for c in range(2):
            ps = PG[c % 2]
            proj_mm_fm(ps, wv, 16, c * 128, 128, xv)
            k.op("act", "copy", out=cqv[:, c, :], in_=ps[:, 0:T], r=[ps], w=[cqf])
            sq = sqb[c % 2]
            k.op("act", "activation", out=sq[:], in_=ps[:, 0:T], func=AF.Square, r=[ps], w=[sq])
            k.op("pe", "matmul", PM[:, 0:T], lhsT=cb("ones"), rhs=sq[:], start=(c == 0), stop=(c == 1),
                 r=[sq, cbf], w=[PM])
        rs = rstd[1]
        k.op("act", "activation", out=rs[:], in_=PM[:, 0:T], func=AF.Sqrt, scale=1.0 / 256, bias=EPS, r=[PM], w=[rs])
        k.op("dve", "reciprocal", out=rs[:], in_=rs[:], r=[rs], w=[rs])
        ckvv = ckvn[:].rearrange("p (c t) -> p c t", c=2)
        for c in range(2):
            k.op("dve", "scalar_tensor_tensor", out=ckvv[:, c, :], in0=cqv[:, c, :], scalar=gcol(f"ckv{c}"), in1=rs[:],
                 op0=ALU.mult, op1=ALU.mult, r=[cqf, rs, gc], w=[ckvn])
        ps = PU[0]
        proj_mm_fm(ps, wv, 16, 256, 64, xv)
        fm_finish(ps, 64, tt, 64, gcol("kr"), 1.0, True, KT[4][0:64, tsl])
        wuqv = wuq[:].rearrange("p (c n) -> p c n", c=4)
        wukvv = wukv[:].rearrange("p (c n) -> p c n", c=2)
        for h in range(4):
            proj_r[:] = [wuq, cqn]
            ps = PG[h % 2]
            proj_mm_fm(ps, wuqv, 4, h * 192, 128, cqnv)
            fm_finish(ps, 128, tt, 128, gscol("qn"), 1.0, False, QT[h][:, tsl])
            ps = PU[h % 2]
            proj_mm_fm(ps, wuqv, 4, h * 192 + 128, 64, cqnv)
            fm_finish(ps, 64, tt, 64, gscol("qr"), 1.0, True, QT[4 + h][0:64, tsl])
            proj_r[:] = [wukv, ckvn]
            ps = PG[(h + 1) % 2]
            proj_mm_fm(ps, wukvv, 2, h * 256, 128, ckvv)
            fm_finish(ps, 128, tt, 128, gcol("kn"), 1.0, False, KT[h][:, tsl])
        proj_r[:] = [wukv, ckvn]
        for h in range(4):
            tm_group(wukv, wukvv, 2, h * 256 + 128, 128, ckvv, tt, [(0, 128, VV[h])])
        wb, wv = load_win(832, 512)
        proj_r[:] = [wb, xnT]
        for h in range(4):
            ps = PG[h % 2]
            proj_mm_fm(ps, wv, 16, h * 128, 128, xv)
            fm_finish(ps, 128, tt, 64, gscol("dq"), 1.0, False, QT[8 + h][:, tsl])
        wb, wv = load_win(1344, 512)
        proj_r[:] = [wb, xnT]
        for h in range(4):
            ps = PU[h % 2]
            proj_mm_fm(ps, wv, 16, h * 128, 128, xv)
            fm_finish(ps, 128, tt, 64, gcol("dk"), 1.0, False, KT[5 + h][:, tsl])
        wb, wv = load_win(1856, 512)
        proj_r[:] = [wb, xnT]
        tm_group(wb, wv, 16, 0, 512, xv, tt, [(h * 128, 128, VV[4 + h]) for h in range(4)])
        wb, wv = load_win(2368, 512)
        proj_r[:] = [wb, xnT]
        for h in range(4):
            ps = PG[h % 2]
            proj_mm_fm(ps, wv, 16, h * 128, 128, xv)
            fm_finish(ps, 128, tt, 128, gscol("nq"), 1.0, False, QT[12 + h][:, tsl])
        wb, wv = load_win(2880, 512)
        proj_r[:] = [wb, xnT]
        ps = PU[0]
        proj_mm_fm(ps, wv, 16, 0, 128, xv)
        fm_finish(ps, 128, tt, None, None, 1.0, False, KT[9][:, tsl])
        ps = PU[1]
        proj_mm_fm(ps, wv, 16, 128, 128, xv)
        fm_finish(ps, 128, tt, None, None, 1.0, False, KT[10][:, tsl])
        ps = PU[0]
        proj_mm_fm(ps, wv, 16, 256, 128, xv)
        fm_finish(ps, 128, tt, 128, gcol("nks"), 1.0, False, KT[11][:, tsl])
        tm_group(wb, wv, 16, 384, 128, xv, tt, [(0, 128, VV[8])])
        wb, wv = load_win(3392, 268)
        proj_r[:] = [wb, xnT]
        ps = PU[1]
        proj_mm_fm(ps, wv, 16, 0, 128, xv)
        fm_finish(ps, 128, tt, 128, gcol("nkw"), 1.0, False, KT[12][:, tsl])
        tm_group(wb, wv, 16, 128, 128, xv, tt, [(0, 128, VV[9])])
        tm_group(wb, wv, 16, 256, 12, xv, tt, None, sigmoid=True)
        wb, wv = load_win(3660, 512)
        proj_r[:] = [wb, xnT]
        for h in range(4):
            ps = PG[h % 2]
            proj_mm_fm(ps, wv, 16, h * 128, 128, xv)
            fm_finish(ps, 128, tt, None, None, 128 ** -0.5, False, QT[16 + h][:, tsl])
        wb, wv = load_win(4172, 512)
        proj_r[:] = [wb, xnT]
        for h in range(4):
            ps = PU[h % 2]
            proj_mm_fm(ps, wv, 16, h * 128, 128, xv)
            fm_finish(ps, 128, tt, None, None, 1.0, False, KT[13 + h][:, tsl])
        wb, wv = load_win(4684, 512)
        proj_r[:] = [wb, xnT]
        tm_group(wb, wv, 16, 0, 512, xv, tt, [(h * 128, 128, VV[10 + h]) for h in range(4)])

    def wout_add(tt):
        ot = xnT
        ov = ot[:].rearrange("p (c t) -> p c t", c=16)
        k.dma("sp", ov, OT[:, :, tt * T:(tt + 1) * T].rearrange("c p t -> p c t"), key=ot, w=[ot])
        for cg in range(4):
            wb = nextwb()
            wv = wb[:, 0:8192].rearrange("p (c n) -> p c n", c=16)
            k.dma("pool", wv, W["w_out"][l][:, cg * 512:(cg + 1) * 512].rearrange("(c p) n -> p c n", p=128), key=wb, w=[wb])
            for tb in range(4):
                pd = PD[tb % 2]
                for c in range(16):
                    k.op("pe", "matmul", pd[:], lhsT=ov[:, c, tb * 128:(tb + 1) * 128], rhs=wv[:, c, :],
                         start=(c == 0), stop=(c == 15), r=[wb, ot], w=[pd])
                hv = htile[:, tb * D + cg * 512: tb * D + (cg + 1) * 512]
                k.op("dve", "tensor_tensor", out=hv, in0=pd[:], in1=hv, op=ALU.add, r=[pd], w=[htile])

    for tt in range(NTT):
        load_h(tt, src)
        if A:
            rms_xnT("ffn1_norm")
            ffn(W["ffn1_w_gate"][l], W["ffn1_w_up"][l], W["ffn1_w_down"][l])
            store_h(tt, dst)
            rms_xnT("mix_norm")
            projections(tt)
        else:
            wout_add(tt)
            rms_xnT("ffn2_norm")
            ffn(W["ffn2_w_gate"][l], W["ffn2_w_up"][l], W["ffn2_w_down"][l])
            store_h(tt, dst)


def phase_attn(k, es, nc, C, W, l, cb, cbf, cf, QT, KT, VV, GT, OT, lam_init, only=None):
    NG = S // 512

    def common(ms):
        d = {}
        d["S"] = [k.ps(ms, f"S{i}", 512) for i in range(3)]
        d["O"] = [k.ps(ms, f"O{i}", 512) for i in range(4)]
        d["PT"] = k.ps(ms, "PTt", 1024, BF16)
        d["pt"] = [k.sb(ms, f"pt{i}", 512, BF16) for i in range(3)]
        d["on"] = [k.sb(ms, f"on{i}", 512, F32) for i in range(3)]
        d["sm"] = k.sb(ms, "sm", 16, F32)
        d["ob"] = [k.sb(ms, f"ob{i}", 128, BF16) for i in range(2)]
        d["ot"] = [k.sb(ms, f"ot{i}", 512, BF16) for i in range(2)]
        d["gbc"] = k.sb(ms, "gbco", 128, F32)
        d["junk"] = k.sb(ms, "junk", 128, F32)
        d["cnt"] = {"s": 0, "p": 0, "ob": 0, "ot": 0}
        return d

    def causal_mode(kt, qbg):
        if kt > qbg:
            return "skip"
        return "tri" if kt == qbg else "full"

    def std_mask(m, kt, qb):
        return cb(m), [cbf]

    def soft_sub(cm, g, pairs, vfn, nv, ktlist, modefn, biasfn, maskfn, fin, preads):
        ktlist = list(ktlist)
        valid = {qb: [kt for kt in ktlist if modefn(kt, 4 * g + qb) != "skip"] for qb in range(4)}
        for kt in ktlist:
            qbs = [qb for qb in range(4) if modefn(kt, 4 * g + qb) != "skip"]
            if not qbs:
                continue
            lo, hi = qbs[0], qbs[-1] + 1
            Sb = cm["S"][cm["cnt"]["s"] % 3]
            cm["cnt"]["s"] += 1
            P = cm["pt"][cm["cnt"]["p"] % 3]
            cm["cnt"]["p"] += 1
            for i, (lf, rf) in enumerate(pairs):
                k.op("pe", "matmul", Sb[:, lo * 128:hi * 128], lhsT=lf(kt), rhs=rf(g * 512 + lo * 128, g * 512 + hi * 128),
                     start=(i == 0), stop=(i == len(pairs) - 1), r=preads, w=[Sb])
            if biasfn is None:
                k.op("act", "activation", out=P[:, lo * 128:hi * 128], in_=Sb[:, lo * 128:hi * 128], func=AF.Exp,
                     r=[Sb], w=[P])
            else:
                for qb in qbs:
                    k.op("act", "activation", out=P[:, qb * 128:(qb + 1) * 128], in_=Sb[:, qb * 128:(qb + 1) * 128],
                         func=AF.Exp, bias=biasfn(kt, 4 * g + qb), r=[Sb, cf], w=[P])
            for qb in qbs:
                m = modefn(kt, 4 * g + qb)
                if m != "full":
                    map_, mr = maskfn(m, kt, qb)
                    k.op("dve", "tensor_tensor", out=P[:, qb * 128:(qb + 1) * 128], in0=P[:, qb * 128:(qb + 1) * 128],
                         in1=map_, op=ALU.mult, r=mr, w=[P])
            for qb in qbs:
                k.op("pe", "matmul", cm["O"][qb][:, 0:nv], lhsT=P[:, qb * 128:(qb + 1) * 128], rhs=vfn(kt),
                     start=(kt == valid[qb][0]), stop=(kt == valid[qb][-1]), r=[P] + preads, w=[cm["O"][qb]])
        for qb in range(4):
            fin(qb, cm["O"][qb])

    def fin_norm(cm, onb):
        def fin(qb, Ob):
            sm = cm["sm"]
            k.op("dve", "tensor_scalar", out=sm[:, qb:qb + 1], in0=Ob[:, 128:129], scalar1=1e-30, scalar2=None,
                 op0=ALU.max, r=[Ob], w=[sm])
            k.op("dve", "reciprocal", out=sm[:, qb:qb + 1], in_=sm[:, qb:qb + 1], r=[sm], w=[sm])
            k.op("dve", "tensor_scalar", out=onb[:, qb * 128:(qb + 1) * 128], in0=Ob[:, 0:128], scalar1=sm[:, qb:qb + 1],
                 scalar2=None, op0=ALU.mult, r=[Ob, sm], w=[onb])
        return fin

    def finish_head(cm, slot, g, onb, extra):
        sm = cm["sm"]
        otb = cm["ot"][cm["cnt"]["ot"] % 2]
        cm["cnt"]["ot"] += 1
        for qb in range(4):
            ov = onb[:, qb * 128:(qb + 1) * 128]
            k.op("dve", "memset", sm[:, 4 + qb:5 + qb], 0.0, w=[sm])
            k.op("act", "activation", out=cm["junk"][:], in_=ov, func=AF.Square, accum_out=sm[:, 4 + qb:5 + qb],
                 r=[onb], w=[cm["junk"], sm])
            k.op("act", "activation", out=sm[:, 8 + qb:9 + qb], in_=sm[:, 4 + qb:5 + qb], func=AF.Sqrt, scale=1.0 / 128,
                 bias=EPS, r=[sm], w=[sm])
            k.op("dve", "reciprocal", out=sm[:, 8 + qb:9 + qb], in_=sm[:, 8 + qb:9 + qb], r=[sm], w=[sm])
            if extra != 1.0:
                k.op("dve", "tensor_scalar", out=sm[:, 8 + qb:9 + qb], in0=sm[:, 8 + qb:9 + qb], scalar1=float(extra),
                     scalar2=None, op0=ALU.mult, r=[sm], w=[sm])
            ob = cm["ob"][cm["cnt"]["ob"] % 2]
            cm["cnt"]["ob"] += 1
            k.op("dve", "scalar_tensor_tensor", out=ob[:], in0=ov, scalar=sm[:, 8 + qb:9 + qb], in1=cm["gbc"][:],
                 op0=ALU.mult, op1=ALU.mult, r=[onb, sm, cm["gbc"]], w=[ob])
            k.op("pe", "transpose", out=cm["PT"][:, qb * 128:(qb + 1) * 128], in_=ob[:], identity=cb("ident"),
                 r=[ob, cbf], w=[cm["PT"]])
        k.op("act", "copy", out=otb[:], in_=cm["PT"][:, 0:512], r=[cm["PT"]], w=[otb])
        k.dma("sp", OT[slot][:, g * 512:(g + 1) * 512], otb[:], key=otb, r=[otb])

    def load_v(v, src):
        vv = v[:].rearrange("p (t c) -> p t c", c=132)
        k.dma("sp", vv[:, :, 0:128], src.rearrange("(t p) d -> p t d", p=128), key=v, w=[v])
        return vv

    def ab_bias(si):
        return lambda kt, qbg: cf[:, si * 32 + (kt - qbg + 31): si * 32 + (kt - qbg + 31) + 1]

    mixers = only if only is not None else ("mla", "diff", "nsa", "sb")

    if "mla" in mixers:
        with ExitStack() as ms:
            cm = common(ms)
            aq = [k.sb(ms, f"aq{i}", S, BF16) for i in range(2)]
            aqr = [k.sb(ms, f"aqr{i}", S, BF16) for i in range(2)]
            ak = [k.sb(ms, f"ak{i}", S, BF16) for i in range(2)]
            akr = k.sb(ms, "akr", S, BF16)
            av = [k.sb(ms, f"av{i}", 32 * 132, BF16) for i in range(2)]
            k.dma("sp", cm["gbc"][:], W["mla_o_norm"][l].partition_broadcast(128), key=cm["gbc"], w=[cm["gbc"]])
            k.dma("sp", akr[0:64, :], KT[4][0:64, :], key=akr, w=[akr])
            for i in range(2):
                k.op("dve", "memset", av[i][:], 1.0, w=[av[i]])
            for h in range(4):
                q, qr, kk, v = aq[h % 2], aqr[h % 2], ak[h % 2], av[h % 2]
                k.dma("sp", q[:], QT[h], key=q, w=[q])
                k.dma("sp", qr[0:64, :], QT[4 + h][0:64, :], key=qr, w=[qr])
                k.dma("sp", kk[:], KT[h], key=kk, w=[kk])
                vv = load_v(v, VV[h])
                pairs = [(lambda kt, kk=kk: kk[:, kt * 128:(kt + 1) * 128], lambda a, b, q=q: q[:, a:b]),
                         (lambda kt: akr[0:64, kt * 128:(kt + 1) * 128], lambda a, b, qr=qr: qr[0:64, a:b])]
                for g in range(NG):
                    onb = cm["on"][0]
                    soft_sub(cm, g, pairs, lambda kt, vv=vv: vv[:, kt, 0:129], 129, range(0, 4 * g + 4), causal_mode,
                             None, std_mask, fin_norm(cm, onb), [q, qr, kk, akr, v])
                    finish_head(cm, h, g, onb, 1.0)
            k.flush()

    if "diff" in mixers:
        with ExitStack() as ms:
            cm = common(ms)
            aq = [k.sb(ms, f"dq{i}", S, BF16) for i in range(2)]
            ak = [k.sb(ms, f"dk{i}", S, BF16) for i in range(2)]
            av = [k.sb(ms, f"dv{i}", 32 * 132, BF16) for i in range(2)]
            lt = k.sb(ms, "lt", 256, F32)
            ltmp = k.sb(ms, "ltmp", 64, F32)
            ls = k.sb(ms, "ls", 8, F32)
            k.dma("sp", cm["gbc"][:], W["diff_subln"][l].partition_broadcast(128), key=cm["gbc"], w=[cm["gbc"]])
            for i, nm in enumerate(["diff_lq1", "diff_lk1", "diff_lq2", "diff_lk2"]):
                k.dma("sp", lt[:, i * 64:(i + 1) * 64], W[nm][l].partition_broadcast(128), key=lt, w=[lt])
            for j in range(2):
                k.op("dve", "tensor_tensor", out=ltmp[:], in0=lt[:, j * 128:j * 128 + 64], in1=lt[:, j * 128 + 64:j * 128 + 128],
                     op=ALU.mult, r=[lt], w=[ltmp])
                k.op("dve", "reduce_sum", out=ls[:, j:j + 1], in_=ltmp[:], axis=AX.X, r=[ltmp], w=[ls])
                k.op("act", "activation", out=ls[:, 2 + j:3 + j], in_=ls[:, j:j + 1], func=AF.Exp, r=[ls], w=[ls])
            k.op("dve", "tensor_tensor", out=ls[:, 4:5], in0=ls[:, 3:4], in1=ls[:, 2:3], op=ALU.subtract, r=[ls], w=[ls])
            k.op("dve", "tensor_scalar", out=ls[:, 5:6], in0=ls[:, 4:5], scalar1=float(-lam_init), scalar2=None, op0=ALU.add,
                 r=[ls], w=[ls])
            for i in range(2):
                k.op("dve", "memset", av[i][:], 1.0, w=[av[i]])
            for h in range(4):
                q, kk, v = aq[h % 2], ak[h % 2], av[h % 2]
                k.dma("sp", q[:], QT[8 + h], key=q, w=[q])
                k.dma("sp", kk[:], KT[5 + h], key=kk, w=[kk])
                vv = load_v(v, VV[4 + h])
                for g in range(NG):
                    for sub in range(2):
                        r0, r1 = sub * 64, (sub + 1) * 64
                        pairs = [(lambda kt, kk=kk, r0=r0, r1=r1: kk[r0:r1, kt * 128:(kt + 1) * 128],
                                  lambda a, b, q=q, r0=r0, r1=r1: q[r0:r1, a:b])]
                        soft_sub(cm, g, pairs, lambda kt, vv=vv: vv[:, kt, 0:129], 129, range(0, 4 * g + 4), causal_mode,
                                 ab_bias(2 * h), std_mask, fin_norm(cm, cm["on"][sub]), [q, kk, v])
                    on0, on1 = cm["on"][0], cm["on"][1]
                    k.op("dve", "scalar_tensor_tensor", out=on0[:], in0=on1[:], scalar=ls[:, 5:6], in1=on0[:],
                         op0=ALU.mult, op1=ALU.add, r=[on1, ls], w=[on0])
                    finish_head(cm, 4 + h, g, on0, 1.0 - lam_init)
            k.flush()

    if "nsa" in mixers:
        with ExitStack() as ms:
            cm = common(ms)
            nq = [k.sb(ms, f"nq{i}", S, BF16) for i in range(4)]
            nks = k.sb(ms, "nks", S, BF16)
            nkw = k.sb(ms, "nkw", S, BF16)
            nvs = k.sb(ms, "nvs", 32 * 132, BF16)
            nvw = k.sb(ms, "nvw", 32 * 132, BF16)
            raw = [k.sb(ms, f"raw{i}", S, BF16) for i in range(2)]
            wc = [k.sb(ms, f"wc{i}", 32 * 128, BF16) for i in range(2)]
            kcT = k.sb(ms, "kcT", 256, BF16)
            vcx = k.sb(ms, "vcx", 2 * 196, BF16)
            gts = k.sb(ms, "gts", 32 * 12, F32)
            bonus = k.sb(ms, "bonus", 2048, F32)
            maskT = k.sb(ms, "maskT", 4 * 32 * 128, BF16)
            selexp = k.sb(ms, "selexp", 4096, BF16)
            selb = k.sb(ms, "selb", 64, BF16)
            scb = k.sb(ms, "scb", 64, F32)
            sc2 = k.sb(ms, "sc2", 64, F32)
            imp = [k.sb(ms, f"imp{i}", 64, F32) for i in range(4)]
            m8 = k.sb(ms, "m8", 16, F32)
            oc = k.sb(ms, "oc", 4 * 4 * 128, F32)
            pe_f = k.sb(ms, "pe_f", 128, F32)
            pe_b = k.sb(ms, "pe_b", 128, BF16)
            peT = k.sb(ms, "peT", 32, BF16)
            crow = k.sb(ms, "crow", 128, BF16)
            kcn = k.sb(ms, "kcn", 128, BF16)
            gcn = k.sb(ms, "gcn", NGC, F32)
            sm = cm["sm"]
            k.dma("sp", cm["gbc"][:], W["nsa_o_norm"][l].partition_broadcast(128), key=cm["gbc"], w=[cm["gbc"]])
            k.dma("sp", gcn[:], W["gcols"][l], key=gcn, w=[gcn])
            k.dma("sp", bonus[:], W["bonus"], key=bonus, w=[bonus])
            k.dma("sp", gts[:].rearrange("p (b n) -> p b n", n=12), GT.rearrange("(b p) n -> p b n", p=128), key=gts, w=[gts])
            for h in range(4):
                k.dma("sp", nq[h][:], QT[12 + h], key=nq[h], w=[nq[h]])
            k.dma("sp", nks[:], KT[11], key=nks, w=[nks])
            k.dma("sp", nkw[:], KT[12], key=nkw, w=[nkw])
            k.op("dve", "memset", nvs[:], 1.0, w=[nvs])
            k.op("dve", "memset", nvw[:], 1.0, w=[nvw])
            vsv = load_v(nvs, VV[8])
            vwv = load_v(nvw, VV[9])
            k.dma("sp", raw[0][:], KT[9], key=raw[0], w=[raw[0]])
            k.dma("sp", raw[1][:], KT[10], key=raw[1], w=[raw[1]])
            k.dma("pool", wc[0][:].rearrange("p (l n) -> p l n", n=128), W["nsa_w_ck"][l].rearrange("(l p) n -> p l n", p=128),
                  key=wc[0], w=[wc[0]])
            k.dma("pool", wc[1][:].rearrange("p (l n) -> p l n", n=128), W["nsa_w_cv"][l].rearrange("(l p) n -> p l n", p=128),
                  key=wc[1], w=[wc[1]])
            k.op("dve", "memset", kcT[:], 0.0, w=[kcT])
            k.op("dve", "memset", vcx[:], 0.0, w=[vcx])
            vcv = vcx[:].rearrange("p (t c) -> p t c", c=196)
            k.op("dve", "memset", vcv[:, :, 128:129], 1.0, w=[vcx])
            k.op("dve", "tensor_copy", out=vcv[:, :, 129:193], in_=cb("ov").rearrange("p (t c) -> p t c", c=64), r=[cbf], w=[vcx])
            for which in range(2):
                pen = "nsa_pe_k" if which == 0 else "nsa_pe_v"
                wv = wc[which][:].rearrange("p (l n) -> p l n", n=128)
                k.dma("sp", pe_f[0:32, :], W[pen][l], key=pe_f, w=[pe_f])
                k.op("act", "copy", out=pe_b[0:32, :], in_=pe_f[0:32, :], r=[pe_f], w=[pe_b])
                k.op("pe", "transpose", out=cm["PT"][:, 0:32], in_=pe_b[0:32, :], identity=cb("ident", 32, 32),
                     r=[pe_b, cbf], w=[cm["PT"]])
                k.op("act", "copy", out=peT[:], in_=cm["PT"][:, 0:32], r=[cm["PT"]], w=[peT])
                Oc = cm["O"][0]
                for li in range(32):
                    k.op("pe", "matmul", Oc[0:1, 0:128], lhsT=peT[:, li:li + 1], rhs=wv[:, li, :], start=(li == 0),
                         stop=(li == 31), r=[peT, wc[which]], w=[Oc])
                k.op("act", "copy", out=crow[0:1, :], in_=Oc[0:1, 0:128], r=[Oc], w=[crow])
                for nt in range(2):
                    nr = 128 if nt == 0 else 127
                    Ok = cm["O"][1 + nt]
                    for li in range(32):
                        a0 = nt * 2048 + li
                        k.op("pe", "matmul", Ok[0:nr, 0:128], lhsT=raw[which][:, a0:a0 + 16 * (nr - 1) + 1:16], rhs=wv[:, li, :],
                             start=(li == 0), stop=False, r=[raw[which], wc[which]], w=[Ok])
                    k.op("pe", "matmul", Ok[0:nr, 0:128], lhsT=cb("ones", 1, nr), rhs=crow[0:1, :], start=False, stop=True,
                         r=[crow, cbf], w=[Ok])
                    if which == 0:
                        k.op("dve", "memset", sm[:, 0:1], 0.0, w=[sm])
                        k.op("act", "activation", out=cm["junk"][0:nr, :], in_=Ok[0:nr, 0:128], func=AF.Square,
                             accum_out=sm[0:nr, 0:1], r=[Ok], w=[cm["junk"], sm])
                        k.op("act", "activation", out=sm[0:nr, 1:2], in_=sm[0:nr, 0:1], func=AF.Sqrt, scale=1.0 / 128, bias=EPS,
                             r=[sm], w=[sm])
                        k.op("dve", "reciprocal", out=sm[0:nr, 1:2], in_=sm[0:nr, 1:2], r=[sm], w=[sm])
                        k.op("dve", "memset", kcn[:], 0.0, w=[kcn])
                        k.op("dve", "tensor_scalar", out=kcn[0:nr, :], in0=Ok[0:nr, 0:128], scalar1=sm[0:nr, 1:2], scalar2=None,
                             op0=ALU.mult, r=[Ok, sm], w=[kcn])
                        k.op("pe", "transpose", out=cm["PT"][:, 128:256], in_=kcn[:], identity=cb("ident"), r=[kcn, cbf],
                             w=[cm["PT"]])
                        k.op("dve", "tensor_scalar", out=kcT[:, nt * 128:(nt + 1) * 128], in0=cm["PT"][:, 128:256],
                             scalar1=gcn[:, GC["nkc"]:GC["nkc"] + 1], scalar2=None, op0=ALU.mult, r=[cm["PT"], gcn], w=[kcT])
                    else:
                        k.op("act", "copy", out=vcv[0:nr, nt, 0:128], in_=Ok[0:nr, 0:128], r=[Ok], w=[vcx])
            ocv = oc[:].rearrange("p (q h d) -> p q h d", q=4, h=4)
            mkv = maskT[:].rearrange("p (q t d) -> p q t d", q=4, t=32)

            def cmp_mode(nt, qbg):
                j = qbg - 16 * nt
                if j < 0:
                    return "skip"
                if j >= 17:
                    return "full"
                return ("cm", j)

            def cmp_mask(m, kt, qb):
                return cb("cm", 128, 128, m[1] * 128), [cbf]

            def sel_mode(kt, qbg):
                return "skip" if kt > qbg else "m"

            def win_mode(kt, qbg):
                rel = kt - qbg
                if rel > 0 or rel < -4:
                    return "skip"
                if rel == 0:
                    return "tri"
                if rel == -4:
                    return "atri"
                return "full"

            for g in range(NG):
                for h in range(4):
                    def fin_cmp(qb, Ob, h=h):
                        k.op("dve", "tensor_scalar", out=sm[:, qb:qb + 1], in0=Ob[:, 128:129], scalar1=1e-30, scalar2=None,
                             op0=ALU.max, r=[Ob], w=[sm])
                        k.op("dve", "reciprocal", out=sm[:, qb:qb + 1], in_=sm[:, qb:qb + 1], r=[sm], w=[sm])
                        k.op("dve", "tensor_scalar", out=ocv[:, qb, h, :], in0=Ob[:, 0:128], scalar1=sm[:, qb:qb + 1],
                             scalar2=None, op0=ALU.mult, r=[Ob, sm], w=[oc])
                        if h == 0:
                            k.op("dve", "tensor_scalar", out=imp[qb][:], in0=Ob[:, 129:193], scalar1=sm[:, qb:qb + 1],
                                 scalar2=None, op0=ALU.mult, r=[Ob, sm], w=[imp[qb]])
                        else:
                            k.op("dve", "scalar_tensor_tensor", out=imp[qb][:], in0=Ob[:, 129:193], scalar=sm[:, qb:qb + 1],
                                 in1=imp[qb][:], op0=ALU.mult, op1=ALU.add, r=[Ob, sm], w=[imp[qb]])
                    pairs = [(lambda nt: kcT[:, nt * 128:(nt + 1) * 128], lambda a, b, h=h: nq[h][:, a:b])]
                    cbias = lambda nt, qbg, h=h: cf[:, 256 + h * 64 + nt * 32 + qbg: 256 + h * 64 + nt * 32 + qbg + 1]
                    soft_sub(cm, g, pairs, lambda nt: vcv[:, nt, 0:193], 193, range(2), cmp_mode, cbias, cmp_mask, fin_cmp,
                             [kcT, nq[h], vcx])
                for qb in range(4):
                    qbg = 4 * g + qb
                    k.op("dve", "tensor_tensor", out=scb[:], in0=imp[qb][:], in1=bonus[:, qbg * 64:(qbg + 1) * 64], op=ALU.add,
                         r=[imp[qb], bonus], w=[scb])
                    k.op("dve", "max", out=m8[:, 0:8], in_=scb[:], r=[scb], w=[m8])
                    k.op("dve", "match_replace", out=sc2[:], in_to_replace=m8[:, 0:8], in_values=scb[:], imm_value=-3.0e38,
                         r=[m8, scb], w=[sc2])
                    k.op("dve", "max", out=m8[:, 8:16], in_=sc2[:], r=[sc2], w=[m8])
                    k.op("dve", "tensor_scalar", out=sm[:, 12:13], in0=m8[:, 15:16], scalar1=-1e29, scalar2=None, op0=ALU.max,
                         r=[m8], w=[sm])
                    k.op("dve", "tensor_scalar", out=selb[:], in0=scb[:], scalar1=sm[:, 12:13], scalar2=None, op0=ALU.is_ge,
                         r=[scb, sm], w=[selb])
                    k.op("dve", "tensor_copy", out=selexp[:].rearrange("p (j s) -> p j s", s=64),
                         in_=selb[:, 0:64].unsqueeze(2).to_broadcast([128, 64, 64]), r=[selb], w=[selexp])
                    for kt0 in range(0, qbg + 1, 4):
                        n = min(4, qbg + 1 - kt0)
                        Sb = cm["S"][cm["cnt"]["s"] % 3]
                        cm["cnt"]["s"] += 1
                        for j in range(n):
                            k.op("pe", "matmul", Sb[:, j * 128:(j + 1) * 128], lhsT=selexp[:, (kt0 + j) * 128:(kt0 + j + 1) * 128],
                                 rhs=cb("ident"), start=True, stop=True, r=[selexp, cbf], w=[Sb])
                        k.op("act", "copy", out=mkv[:, qb, kt0:kt0 + n, :], in_=Sb[:, 0:n * 128].rearrange("p (t d) -> p t d", d=128),
                             r=[Sb], w=[maskT])
                    k.op("dve", "tensor_tensor", out=mkv[:, qb, qbg, :], in0=mkv[:, qb, qbg, :], in1=cb("tri"), op=ALU.mult,
                         r=[cbf], w=[maskT])
                for h in range(4):
                    si = 2 * h + 1
                    pairs = [(lambda kt: nks[:, kt * 128:(kt + 1) * 128], lambda a, b, h=h: nq[h][:, a:b])]
                    soft_sub(cm, g, pairs, lambda kt: vsv[:, kt, 0:129], 129, range(0, 4 * g + 4), sel_mode, ab_bias(si),
                             lambda m, kt, qb: (mkv[:, qb, kt, :], [maskT]), fin_norm(cm, cm["on"][0]), [nks, nq[h], nvs])
                    pairs = [(lambda kt: nkw[:, kt * 128:(kt + 1) * 128], lambda a, b, h=h: nq[h][:, a:b])]
                    soft_sub(cm, g, pairs, lambda kt: vwv[:, kt, 0:129], 129, range(max(0, 4 * g - 4), 4 * g + 4), win_mode,
                             ab_bias(si), std_mask, fin_norm(cm, cm["on"][1]), [nkw, nq[h], nvw])
                    onc = cm["on"][2]
                    for qb in range(4):
                        qbg = 4 * g + qb
                        gcol = lambda i: gts[:, qbg * 12 + 3 * h + i: qbg * 12 + 3 * h + i + 1]
                        dst = onc[:, qb * 128:(qb + 1) * 128]
                        k.op("dve", "tensor_scalar", out=dst, in0=ocv[:, qb, h, :], scalar1=gcol(0), scalar2=None, op0=ALU.mult,
                             r=[oc, gts], w=[onc])
                        k.op("dve", "scalar_tensor_tensor", out=dst, in0=cm["on"][0][:, qb * 128:(qb + 1) * 128], scalar=gcol(1),
                             in1=dst, op0=ALU.mult, op1=ALU.add, r=[cm["on"][0], gts], w=[onc])
                        k.op("dve", "scalar_tensor_tensor", out=dst, in0=cm["on"][1][:, qb * 128:(qb + 1) * 128], scalar=gcol(2),
                             in1=dst, op0=ALU.mult, op1=ALU.add, r=[cm["on"][1], gts], w=[onc])
                    finish_head(cm, 8 + h, g, onc, 1.0)
            k.flush()

    if "sb" in mixers:
        with ExitStack() as ms:
            cm = common(ms)
            aq = [k.sb(ms, f"sq{i}", S, BF16) for i in range(2)]
            ak = [k.sb(ms, f"sk{i}", S, BF16) for i in range(2)]
            av = [k.sb(ms, f"sv{i}", 32 * 132, BF16) for i in range(2)]
            Ef = [k.sb(ms, f"Ef{i}", 512, F32) for i in range(2)]
            Lb = [k.sb(ms, f"Lb{i}", 512, BF16) for i in range(2)]
            Lf = k.sb(ms, "Lf", 512, F32)
            Lsb = [k.sb(ms, f"Lsb{i}", 512, BF16) for i in range(2)]
            k.dma("sp", cm["gbc"][:], W["sb_o_norm"][l].partition_broadcast(128), key=cm["gbc"], w=[cm["gbc"]])
            step = 0
            for h in range(4):
                q, kk, v = aq[h % 2], ak[h % 2], av[h % 2]
                k.dma("sp", q[:], QT[16 + h], key=q, w=[q])
                k.dma("sp", kk[:], KT[13 + h], key=kk, w=[kk])
                vv = load_v(v, VV[10 + h])
                for g in range(NG):
                    k.op("dve", "memset", Lf[:], 0.0, w=[Lf])
                    ktl = list(range(4 * g + 3, -1, -1))
                    for idx, kt in enumerate(ktl):
                        lo = max(0, kt - 4 * g)
                        c0, c1 = lo * 128, 512
                        Sb = cm["S"][step % 2]
                        Cb = cm["S"][2]
                        E, L, P = Ef[step % 2], Lb[step % 2], cm["pt"][step % 3]
                        Ls_prev = Lsb[(step + 1) % 2]
                        Ls_new = Lsb[step % 2]
                        step += 1
                        lhs = kk[:, kt * 128:(kt + 1) * 128]
                        rhs = q[:, g * 512 + c0: g * 512 + c1]
                        k.op("pe", "matmul", Sb[:, c0:c1], lhsT=lhs, rhs=rhs, start=True, stop=True, r=[kk, q], w=[Sb])
                        k.op("act", "activation", out=E[:, c0:c1], in_=Sb[:, c0:c1], func=AF.Exp, r=[Sb], w=[E])
                        k.op("act", "activation", out=L[:, c0:c1], in_=E[:, c0:c1], func=AF.Ln, bias=1.0, r=[E], w=[L])
                        if kt >= 4 * g:
                            k.op("dve", "tensor_tensor", out=L[:, c0:c0 + 128], in0=L[:, c0:c0 + 128], in1=cb("tris"), op=ALU.mult,
                                 r=[cbf], w=[L])
                        k.op("pe", "matmul", Cb[:, c0:c1], lhsT=cb("ntri"), rhs=L[:, c0:c1], start=True, stop=False,
                             r=[L, cbf], w=[Cb])
                        if idx > 0:
                            k.op("pe", "matmul", Cb[:, c0:c1], lhsT=cb("nones"), rhs=Ls_prev[:, c0:c1], start=False, stop=False,
                                 r=[Ls_prev, cbf], w=[Cb])
                        k.op("pe", "matmul", Cb[:, c0:c1], lhsT=lhs, rhs=rhs, start=False, stop=True, r=[kk, q], w=[Cb])
                        k.op("act", "activation", out=P[:, c0:c1], in_=Cb[:, c0:c1], func=AF.Exp, r=[Cb], w=[P])
                        if kt >= 4 * g:
                            k.op("dve", "tensor_tensor", out=P[:, c0:c0 + 128], in0=P[:, c0:c0 + 128], in1=cb("tris"), op=ALU.mult,
                                 r=[cbf], w=[P])
                        for qb in range(lo, 4):
                            k.op("pe", "matmul", cm["O"][qb][:, 0:128], lhsT=P[:, qb * 128:(qb + 1) * 128], rhs=vv[:, kt, 0:128],
                                 start=(kt == 4 * g + qb), stop=(kt == 0), r=[P, v], w=[cm["O"][qb]])
                        if kt > 0:
                            k.op("dve", "tensor_tensor", out=Lf[:, c0:c1], in0=Lf[:, c0:c1], in1=L[:, c0:c1], op=ALU.add,
                                 r=[L], w=[Lf])
                            k.op("dve", "tensor_copy", out=Ls_new[:], in_=Lf[:], r=[Lf], w=[Ls_new])
                    onb = cm["on"][0]
                    for qb in range(4):
                        k.op("act", "copy", out=onb[:, qb * 128:(qb + 1) * 128], in_=cm["O"][qb][:, 0:128], r=[cm["O"][qb]], w=[onb])
                    finish_head(cm, 12 + h, g, onb, 1.0)
            k.flush()


_CACHE = {}


def _host_inputs(inputs, C):
    common = {}
    for nm in ["ffn1_w_gate", "ffn1_w_up", "ffn1_w_down", "ffn2_w_gate", "ffn2_w_up", "ffn2_w_down", "w_in", "w_out",
               "mla_w_uq", "mla_w_ukv", "nsa_w_ck", "nsa_w_cv", "nsa_pe_k", "nsa_pe_v", "ffn1_norm", "mix_norm",
               "ffn2_norm", "mla_o_norm", "diff_subln", "nsa_o_norm", "sb_o_norm", "diff_lq1", "diff_lk1", "diff_lq2",
               "diff_lk2"]:
        common[nm] = np.ascontiguousarray(np.asarray(inputs[nm], np.float32))
    common["gcols"] = np.stack([_gain_cols(inputs, l) for l in range(DEPTH)], axis=0).astype(np.float32)
    common["cbf"] = C["cbf"]
    common["cf32"] = C["cf32"]
    common["cosT"] = C["cosT"]
    common["sinT"] = C["sinT"]
    common["bonus"] = C["bonus"]
    return common


def kernel(**inputs):
    C = _consts()
    inputs = {kk: np.asarray(v) for kk, v in inputs.items()}
    _gain_cols(inputs, 0)
    nc = build_program(C)
    common = _host_inputs(inputs, C)
    x = np.asarray(inputs["x"], np.float32)
    in_maps = []
    for c in range(4):
        m = dict(common)
        m["x"] = np.ascontiguousarray(x[c])
        in_maps.append(m)
    res = run_bass_kernel_spmd(nc, in_maps, core_ids=list(range(4)))
    out = np.stack([res.results[c]["y"] for c in range(4)], axis=0)
    return out.astype(np.float32)
```

```python
import math
from contextlib import ExitStack

import numpy as np
import ml_dtypes

import concourse.bass as bass
import concourse.mybir as mybir
from concourse.bass_utils import run_bass_kernel_spmd

F32 = mybir.dt.float32
BF16 = mybir.dt.bfloat16
AF = mybir.ActivationFunctionType
ALU = mybir.AluOpType
AX = mybir.AxisListType

D = 2048
S = 4096
DEPTH = 2
DFF = 5632
NFC = DFF // 128
DIN = 5196
T = 512
NTT = S // T
EPS = 1e-6
NSLOPE = 8
SLOPES = [2.0 ** (-(i + 1)) for i in range(8)]
DIFF_SLOPES = SLOPES[0::2]
NSA_SLOPES = SLOPES[1::2]

DEBUG = None
ONLY = None


class Buf:
    __slots__ = ("t", "name", "lw", "rd", "sem", "ndma")

    def __init__(self, t, name):
        self.t = t
        self.name = name
        self.lw = None
        self.rd = []
        self.sem = None
        self.ndma = 0

    def __getitem__(self, key):
        return self.t[key]


class Op:
    __slots__ = ("eng", "dma", "calls", "deps", "inc", "count", "sem", "key")

    def __init__(self, eng, dma, calls, key=None):
        self.eng = eng
        self.dma = dma
        self.calls = calls
        self.deps = set()
        self.inc = False
        self.count = 0
        self.sem = None
        self.key = key


class KB:
    CENG = ("pe", "act", "dve", "pool")

    def __init__(self, nc, es):
        self.nc = nc
        self.es = es
        self.engs = {"pe": nc.tensor, "act": nc.scalar, "dve": nc.vector, "pool": nc.gpsimd, "sp": nc.sync}
        self.ops = []
        self.base = 0
        self.csem = {e: es.enter_context(nc.semaphore("cs_" + e)) for e in self.CENG}
        self.ccount = {e: 0 for e in self.CENG}
        self.seen = {e: {} for e in self.engs}
        self.keys = []
        self.bufs = []
        self.n_ins = 0

    def sb(self, es, name, cols, dtype):
        self.uid = getattr(self, "uid", 0) + 1
        name = f"{name}_{self.uid}"
        t = es.enter_context(self.nc.sbuf_tensor(name, [128, cols], dtype))
        b = Buf(t, name)
        self.bufs.append(b)
        return b

    def ps(self, es, name, cols, dtype=F32):
        self.uid = getattr(self, "uid", 0) + 1
        name = f"{name}_{self.uid}"
        t = es.enter_context(self.nc.psum_tensor(name, [128, cols], dtype))
        b = Buf(t, name)
        self.bufs.append(b)
        return b

    def _track(self, op, idx, r, w):
        for b in w:
            if b.lw is not None:
                op.deps.add(b.lw)
            op.deps.update(b.rd)
        for b in r:
            if b.lw is not None:
                op.deps.add(b.lw)
        for b in w:
            b.lw = idx
            b.rd = []
        for b in r:
            if b not in w:
                b.rd.append(idx)
                if len(b.rd) > 64:
                    keep = {}
                    rest = []
                    for i in b.rd:
                        o = self.ops[i]
                        if o.dma:
                            rest.append(i)
                        else:
                            keep[o.eng] = i
                    b.rd = rest + list(keep.values())

    def op(self, eng, method, *args, r=(), w=(), **kw):
        o = Op(eng, False, [(method, args, kw)])
        idx = len(self.ops)
        self.ops.append(o)
        self._track(o, idx, r, w)
        return idx

    def dma(self, eng, out, in_, key, r=(), w=()):
        o = Op(eng, True, [("dma_start", (), {"out": out, "in_": in_})], key=key)
        idx = len(self.ops)
        self.ops.append(o)
        self._track(o, idx, r, w)
        return idx

    def flush(self, final=False):
        nc = self.nc
        ops = self.ops
        n = len(ops)
        for i in range(self.base, n):
            o = ops[i]
            for d in o.deps:
                if d >= self.base:
                    od = ops[d]
                    if not (od.eng == "pe" and o.eng == "pe" and not od.dma and not o.dma):
                        od.inc = True
        last = {}
        for i in range(self.base, n):
            o = ops[i]
            if not o.dma:
                last[o.eng] = i
        for e, i in last.items():
            ops[i].inc = True
        for i in range(self.base, n):
            o = ops[i]
            if o.dma:
                kb = o.key
                if kb.sem is None:
                    pool = self.__dict__.setdefault("sempool", [])
                    if pool:
                        kb.sem, kb.ndma = pool.pop()
                    else:
                        kb.sem = self.es.enter_context(nc.semaphore("ds_" + kb.name))
                        kb.ndma = 0
                    self.keys.append(kb)
                kb.ndma += 1
                o.sem = kb.sem
                o.count = 16 * kb.ndma
            else:
                if o.inc:
                    self.ccount[o.eng] += 1
                o.sem = self.csem[o.eng]
                o.count = self.ccount[o.eng]
        for i in range(self.base, n):
            o = ops[i]
            E = self.engs[o.eng]
            seen = self.seen[o.eng]
            need = {}
            for d in o.deps:
                if d < self.base:
                    continue
                od = ops[d]
                if od.eng == "pe" and o.eng == "pe" and not od.dma and not o.dma:
                    continue
                if not od.dma and not od.inc:
                    raise RuntimeError("dep without inc")
                key = od.sem
                if need.get(key, (None, 0))[1] < od.count:
                    need[key] = (od.sem, od.count)
            for key, (sem, cnt) in need.items():
                if seen.get(key, 0) < cnt:
                    E.wait_ge(sem, cnt)
                    seen[key] = cnt
                    self.n_ins += 1
            ins = None
            for (m, a, kw) in o.calls:
                ins = getattr(E, m)(*a, **kw)
                self.n_ins += 1
            if o.dma:
                ins.then_inc(o.sem, 16)
            elif o.inc:
                ins.then_inc(o.sem, 1)
        targets = [(self.csem[e], self.ccount[e]) for e in self.CENG if self.ccount[e] > 0]
        targets += [(kb.sem, 16 * kb.ndma) for kb in self.keys]
        wait_engs = ["sp"] if final else list(self.engs.keys())
        for e in wait_engs:
            E = self.engs[e]
            seen = self.seen[e]
            for (sem, cnt) in targets:
                if seen.get(sem, 0) < cnt:
                    E.wait_ge(sem, cnt)
                    seen[sem] = cnt
                    self.n_ins += 1
        self.base = n
        pool = self.__dict__.setdefault("sempool", [])
        for kb in self.keys:
            pool.append((kb.sem, kb.ndma))
            kb.sem = None
        self.keys = []
        for b in self.bufs:
            b.lw = None
            b.rd = []


def _consts():
    c = {}
    p = np.arange(128)
    ident = np.eye(128, dtype=np.float32)
    ones = np.ones((128, 128), np.float32)
    blk64 = np.kron(np.eye(2), np.ones((64, 64))).astype(np.float32)
    tri = (p[:, None] <= p[None, :]).astype(np.float32)
    tris = (p[:, None] < p[None, :]).astype(np.float32)
    atri = (p[:, None] > p[None, :]).astype(np.float32)
    ntri_incl = -(p[:, None] >= p[None, :]).astype(np.float32)
    nones = -ones
    prot = np.zeros((128, 128), np.float32)
    for m in range(32):
        prot[m + 32, m] = -1.0
    for m in range(32, 64):
        prot[m - 32, m] = 1.0
    cm = np.zeros((128, 17, 128), np.float32)
    for j in range(17):
        cm[:, j, :] = ((16 * p[:, None] + 31 - p[None, :]) <= 128 * j)
    n = np.arange(256)
    cst = 16 * n
    sel_start = 64 * np.arange(64)
    ov = ((cst[:, None] < sel_start[None, :] + 64) & (cst[:, None] + 32 > sel_start[None, :])).astype(np.float32)
    ov[255] = 0
    ov2 = ov.reshape(2, 128, 64).transpose(1, 0, 2)
    cb = np.concatenate([ident, ones, blk64, tri, tris, atri, ntri_incl, nones, prot,
                         cm.reshape(128, -1), ov2.reshape(128, -1)], axis=1)
    c["cbf"] = cb.astype(ml_dtypes.bfloat16)
    off = {}
    o = 0
    for name, w in [("ident", 128), ("ones", 128), ("blk64", 128), ("tri", 128), ("tris", 128), ("atri", 128),
                    ("ntri", 128), ("nones", 128), ("prot", 128), ("cm", 17 * 128), ("ov", 128)]:
        off[name] = o
        o += w
    c["cbf_off"] = off
    c["cbf_w"] = o
    pos = np.arange(S, dtype=np.float32)
    inv_freq = (10000.0 ** (-np.arange(0, 64, 2, dtype=np.float32) / 64)).astype(np.float32)
    ang = pos[:, None] * inv_freq[None, :]
    cosT = np.concatenate([np.cos(ang), np.cos(ang)], axis=1).T
    sinT = np.concatenate([np.sin(ang), np.sin(ang)], axis=1).T
    c["cosT"] = np.ascontiguousarray(cosT).astype(np.float32)
    c["sinT"] = np.ascontiguousarray(sinT).astype(np.float32)
    j = np.arange(32)
    base = 128.0 * (j[None, :] - 31) + p[:, None] - 127.0
    ab = np.stack([s * base for s in SLOPES], axis=1)
    nt = np.arange(2)
    qbg = np.arange(32)
    cbase = 16.0 * (128 * nt[None, :, None] + p[:, None, None]) + 31 - 128.0 * qbg[None, None, :] - 127.0
    cbias = np.stack([s * cbase for s in NSA_SLOPES], axis=1)
    cbias = np.minimum(cbias, 60.0)
    t = (128 * qbg[None, :] + p[:, None])
    jb = np.arange(64)
    cur = t // 64
    valid = sel_start[None, None, :] <= t[:, :, None]
    forced = (jb[None, None, :] == 0) | (jb[None, None, :] == cur[:, :, None]) | (jb[None, None, :] == cur[:, :, None] - 1)
    bonus = np.where(valid, np.where(forced, 1e4, 0.0), -1e30).astype(np.float32)
    cf = np.concatenate([ab.reshape(128, -1), cbias.reshape(128, -1)], axis=1)
    c["cf32"] = cf.astype(np.float32)
    c["bonus"] = bonus.reshape(128, -1).astype(np.float32)
    c["cf_off"] = {"ab": 0, "cbias": 256}
    c["cf_w"] = cf.shape[1]
    return c


GC = {}


def _gain_cols(inputs, l):
    cols = []

    def add(name, v):
        GC[name] = len(cols)
        cols.append(np.asarray(v, np.float32).reshape(128))

    for c in range(4):
        add(f"cq{c}", inputs["mla_cq_norm"][l][c * 128:(c + 1) * 128])
    for c in range(2):
        add(f"ckv{c}", inputs["mla_ckv_norm"][l][c * 128:(c + 1) * 128])
    add("qn", inputs["mla_qn_norm"][l])
    add("qr", np.tile(inputs["mla_qr_norm"][l], 2))
    add("kn", inputs["mla_kn_norm"][l])
    add("kr", np.tile(inputs["mla_kr_norm"][l], 2))
    add("dq", np.tile(inputs["diff_q_norm"][l], 2))
    add("dk", np.tile(inputs["diff_k_norm"][l], 2))
    add("nq", inputs["nsa_q_norm"][l])
    add("nkc", inputs["nsa_kc_norm"][l])
    add("nks", inputs["nsa_ks_norm"][l])
    add("nkw", inputs["nsa_kw_norm"][l])
    return np.stack(cols, axis=1)


NGC = 16


def build_program(C, stop_after=None, debug_out=False):
    nc = bass.Bass("TRN2", target_bir_lowering=False)

    def din(name, shape, dt=F32):
        return nc.dram_tensor(name, list(shape), dt, kind="ExternalInput").ap()

    skind = "ExternalOutput" if debug_out else "Internal"

    def dscr(name, shape, dt):
        return nc.dram_tensor(name, list(shape), dt, kind=skind).ap()

    x = din("x", [S, D])
    W = {}
    for nm, shp in [("ffn1_w_gate", [DEPTH, D, DFF]), ("ffn1_w_up", [DEPTH, D, DFF]), ("ffn1_w_down", [DEPTH, DFF, D]),
                    ("ffn2_w_gate", [DEPTH, D, DFF]), ("ffn2_w_up", [DEPTH, D, DFF]), ("ffn2_w_down", [DEPTH, DFF, D]),
                    ("w_in", [DEPTH, D, DIN]), ("w_out", [DEPTH, D, D]),
                    ("mla_w_uq", [DEPTH, 512, 768]), ("mla_w_ukv", [DEPTH, 256, 1024]),
                    ("nsa_w_ck", [DEPTH, 4096, 128]), ("nsa_w_cv", [DEPTH, 4096, 128]),
                    ("nsa_pe_k", [DEPTH, 32, 128]), ("nsa_pe_v", [DEPTH, 32, 128]),
                    ("ffn1_norm", [DEPTH, D]), ("mix_norm", [DEPTH, D]), ("ffn2_norm", [DEPTH, D]),
                    ("mla_o_norm", [DEPTH, 128]), ("diff_subln", [DEPTH, 128]), ("nsa_o_norm", [DEPTH, 128]),
                    ("sb_o_norm", [DEPTH, 128]),
                    ("diff_lq1", [DEPTH, 64]), ("diff_lk1", [DEPTH, 64]), ("diff_lq2", [DEPTH, 64]), ("diff_lk2", [DEPTH, 64]),
                    ("gcols", [DEPTH, 128, NGC])]:
        W[nm] = din(nm, shp)
    cbf_d = din("cbf", [128, C["cbf_w"]], BF16)
    cf_d = din("cf32", [128, C["cf_w"]])
    cos_d = din("cosT", [64, S])
    sin_d = din("sinT", [64, S])
    bonus_d = din("bonus", [128, 2048])
    W["bonus"] = bonus_d
    y = nc.dram_tensor("y", [S, D], F32, kind="ExternalOutput").ap()

    hS1 = dscr("hS1", [S, D], F32)
    hS3 = dscr("hS3", [S, D], F32)
    QT = dscr("QT", [20, 128, S], BF16)
    KT = dscr("KT", [18, 128, S], BF16)
    VV = dscr("VV", [14, S, 128], BF16)
    GT = dscr("GT", [S, 12], F32)
    OT = dscr("OT", [16, 128, S], BF16)

    with ExitStack() as es:
        k = KB(nc, es)
        CO = C["cbf_off"]
        cbf = k.sb(es, "cbf", C["cbf_w"], BF16)
        cf = k.sb(es, "cf", C["cf_w"], F32)
        k.dma("sp", cbf[:], cbf_d[:, :], key=cbf, w=[cbf])
        k.dma("sp", cf[:], cf_d[:, :], key=cf, w=[cf])
        if DEBUG == "recompile":
            k.op("dve", "memset", cf[:, 0:1], 0.0, w=[cf])

        def cb(name, rows=128, cols=128, c0=0):
            o = CO[name] + c0
            return cbf[0:rows, o:o + cols]

        for l in range(DEPTH):
            lam_init = 0.8 - 0.6 * math.exp(-0.3 * l)
            src = x if l == 0 else hS3
            with ExitStack() as pa:
                phase_tokens(k, pa, nc, C, W, l, "A", src, hS1, cb, cbf, cf, cos_d, sin_d,
                             QT, KT, VV, GT, OT, lam_init)
                k.flush()
            if stop_after == f"A{l}":
                break
            with ExitStack() as pb:
                phase_attn(k, pb, nc, C, W, l, cb, cbf, cf, QT, KT, VV, GT, OT, lam_init, only=ONLY)
                k.flush()
            if stop_after == f"B{l}":
                break
            dst = y if l == DEPTH - 1 else hS3
            with ExitStack() as pc:
                phase_tokens(k, pc, nc, C, W, l, "C", hS1, dst, cb, cbf, cf, cos_d, sin_d,
                             QT, KT, VV, GT, OT, lam_init)
                k.flush()
            if stop_after == f"C{l}":
                break
        k.flush(final=True)
        print("instructions:", k.n_ins, "ops:", len(k.ops), "dma sems:", len(k.keys))
    return nc


def phase_tokens(k, es, nc, C, W, l, which, src, dst, cb, cbf, cf, cos_d, sin_d, QT, KT, VV, GT, OT, lam_init):
    A = which == "A"
    htile = k.sb(es, "htile", 4 * D, F32)
    xnb = [k.sb(es, f"xnb{i}", D, BF16) for i in range(2)]
    xnT = k.sb(es, "xnT", 16 * T, BF16)
    actT = k.sb(es, "actT", NFC * T, BF16)
    WB = [k.sb(es, f"WB{i}", 8192, BF16) for i in range(2)]
    sg = [k.sb(es, f"sg{i}", T, F32) for i in range(2)]
    gbc = k.sb(es, "gbc", D, F32)
    ss = k.sb(es, "ss", 8, F32)
    PG = [k.ps(es, f"PG{i}", 512) for i in range(2)]
    PU = [k.ps(es, f"PU{i}", 512) for i in range(2)]
    PD = [k.ps(es, f"PD{i}", 512) for i in range(2)]
    PM = k.ps(es, "PM", 512)
    PT = k.ps(es, "PT", 1024, BF16)
    wbi = [0]

    def nextwb():
        b = WB[wbi[0] % 2]
        wbi[0] += 1
        return b

    fn = "ffn1" if A else "ffn2"
    if A:
        gc = k.sb(es, "gc", NGC, F32)
        gcs = k.sb(es, "gcs", NGC, F32)
        k.dma("sp", gc[:], W["gcols"][l], key=gc, w=[gc])
        for nm, sc in [("qn", 192 ** -0.5), ("qr", 192 ** -0.5), ("dq", 64 ** -0.5), ("nq", 128 ** -0.5)]:
            j = GC[nm]
            k.op("dve", "tensor_scalar", out=gcs[:, j:j + 1], in0=gc[:, j:j + 1], scalar1=float(sc), scalar2=None,
                 op0=ALU.mult, r=[gc], w=[gcs])
        wuq = k.sb(es, "wuq", 4 * 768, BF16)
        wukv = k.sb(es, "wukv", 2 * 1024, BF16)
        k.dma("pool", wuq[:].rearrange("p (c n) -> p c n", c=4),
              W["mla_w_uq"][l].rearrange("(c p) n -> p c n", p=128), key=wuq, w=[wuq])
        k.dma("pool", wukv[:].rearrange("p (c n) -> p c n", c=2),
              W["mla_w_ukv"][l].rearrange("(c p) n -> p c n", p=128), key=wukv, w=[wukv])
        cqf = k.sb(es, "cqf", 4 * T, BF16)
        cqn = k.sb(es, "cqn", 4 * T, BF16)
        ckvn = k.sb(es, "ckvn", 2 * T, BF16)
        sqb = [k.sb(es, f"sqb{i}", T, BF16) for i in range(2)]
        rstd = [k.sb(es, f"rstd{i}", T, F32) for i in range(2)]
        yf = k.sb(es, "yf", T, F32)
        ybf = [k.sb(es, f"ybf{i}", T, BF16) for i in range(3)]
        t1 = k.sb(es, "t1", T, F32)
        t2 = k.sb(es, "t2", T, F32)
        cosb = k.sb(es, "cosb", T, F32)
        sinb = k.sb(es, "sinb", T, F32)
        vout = [k.sb(es, f"vout{i}", 4 * 512, BF16) for i in range(2)]
        gout = k.sb(es, "gout", 4 * 12, F32)
        cnt = {"y": 0, "sq": 0, "v": 0}

    def rms_xnT(gname):
        gain_b = gbc
        k.dma("sp", gbc[:], W[gname][l].partition_broadcast(128), key=gbc, w=[gbc])
        for tb in range(4):
            sqj = xnb[(tb + 1) % 2]
            hb = htile[:, tb * D:(tb + 1) * D]
            k.op("dve", "memset", ss[:, tb:tb + 1], 0.0, w=[ss])
            k.op("act", "activation", out=sqj[:], in_=hb, func=AF.Square, accum_out=ss[:, tb:tb + 1],
                 r=[htile], w=[sqj, ss])
            k.op("act", "activation", out=ss[:, 4 + tb:5 + tb], in_=ss[:, tb:tb + 1], func=AF.Sqrt, scale=1.0 / D, bias=EPS,
                 r=[ss], w=[ss])
            k.op("dve", "reciprocal", out=ss[:, 4 + tb:5 + tb], in_=ss[:, 4 + tb:5 + tb], r=[ss], w=[ss])
            xb = xnb[tb % 2]
            k.op("dve", "scalar_tensor_tensor", out=xb[:], in0=hb, scalar=ss[:, 4 + tb:5 + tb], in1=gain_b[:],
                 op0=ALU.mult, op1=ALU.mult, r=[htile, ss, gain_b], w=[xb])
            for half in range(2):
                for c in range(8):
                    dc = half * 8 + c
                    k.op("pe", "transpose", out=PT[:, c * 128:(c + 1) * 128], in_=xb[:, dc * 128:(dc + 1) * 128],
                         identity=cb("ident"), r=[xb, cbf], w=[PT])
                dstv = xnT[:].rearrange("p (c t) -> p c t", c=16)[:, half * 8:(half + 1) * 8, tb * 128:(tb + 1) * 128]
                k.op("act", "copy", out=dstv, in_=PT[:].rearrange("p (c t) -> p c t", c=8), r=[PT], w=[xnT])

    def ffn(wg, wu, wd):
        xv = xnT[:].rearrange("p (c t) -> p c t", c=16)
        av = actT[:].rearrange("p (c t) -> p c t", c=NFC)
        pi = 0
        for fg in range(NFC // 2):
            wb = nextwb()
            wv = wb[:, 0:8192].rearrange("p (g c n) -> p g c n", g=2, c=16)
            k.dma("pool", wv[:, 0], wg[:, fg * 256:(fg + 1) * 256].rearrange("(c p) n -> p c n", p=128), key=wb, w=[wb])
            k.dma("pool", wv[:, 1], wu[:, fg * 256:(fg + 1) * 256].rearrange("(c p) n -> p c n", p=128), key=wb, w=[wb])
            for fc in range(2):
                pg, pu, sgb = PG[pi % 2], PU[pi % 2], sg[pi % 2]
                pi += 1
                for dc in range(16):
                    k.op("pe", "matmul", pg[:], lhsT=wv[:, 0, dc, fc * 128:(fc + 1) * 128], rhs=xv[:, dc, :],
                         start=(dc == 0), stop=(dc == 15), r=[wb, xnT], w=[pg])
                for dc in range(16):
                    k.op("pe", "matmul", pu[:], lhsT=wv[:, 1, dc, fc * 128:(fc + 1) * 128], rhs=xv[:, dc, :],
                         start=(dc == 0), stop=(dc == 15), r=[wb, xnT], w=[pu])
                k.op("act", "activation", out=sgb[:], in_=pg[:], func=AF.Silu, r=[pg], w=[sgb])
                k.op("dve", "tensor_tensor", out=av[:, fg * 2 + fc, :], in0=sgb[:], in1=pu[:], op=ALU.mult,
                     r=[sgb, pu], w=[actT])
        for cg in range(16):
            wb = nextwb()
            wv = wb[:, 0:NFC * 128].rearrange("p (c n) -> p c n", c=NFC)
            for hh in range(2):
                k.dma("pool", wv[:, hh * 22:(hh + 1) * 22, :],
                      wd[hh * 2816:(hh + 1) * 2816, cg * 128:(cg + 1) * 128].rearrange("(c p) n -> p c n", p=128),
                      key=wb, w=[wb])
            for tb in range(4):
                pd = PD[(cg * 4 + tb) % 2]
                for fc in range(NFC):
                    k.op("pe", "matmul", pd[:, 0:128], lhsT=av[:, fc, tb * 128:(tb + 1) * 128], rhs=wv[:, fc, :],
                         start=(fc == 0), stop=(fc == NFC - 1), r=[wb, actT], w=[pd])
                hv = htile[:, tb * D + cg * 128: tb * D + (cg + 1) * 128]
                k.op("dve", "scalar_tensor_tensor", out=hv, in0=pd[:, 0:128], scalar=0.5, in1=hv,
                     op0=ALU.mult, op1=ALU.add, r=[pd], w=[htile])

    def load_h(tt, srcap):
        k.dma("sp", htile[:].rearrange("p (b d) -> p b d", b=4),
              srcap[tt * T:(tt + 1) * T, :].rearrange("(b p) d -> p b d", p=128), key=htile, w=[htile])

    def store_h(tt, dstap):
        k.dma("sp", dstap[tt * T:(tt + 1) * T, :].rearrange("(b p) d -> p b d", p=128),
              htile[:].rearrange("p (b d) -> p b d", b=4), key=htile, r=[htile])

    def fm_finish(ps, n, tt, norm, gcol, scale, rope, dest, keep=None):
        yb = ybf[cnt["y"] % 3]
        cnt["y"] += 1
        if norm is None:
            k.op("act", "activation", out=yb[0:n, :], in_=ps[0:n, 0:T], func=AF.Copy, scale=float(scale), r=[ps], w=[yb])
        else:
            sq = sqb[cnt["sq"] % 2]
            rs = rstd[cnt["sq"] % 2]
            cnt["sq"] += 1
            k.op("act", "activation", out=sq[0:n, :], in_=ps[0:n, 0:T], func=AF.Square, r=[ps], w=[sq])
            onesm = cb("ones", n, n) if norm == 128 else cb("blk64", n, n)
            k.op("pe", "matmul", PM[0:n, 0:T], lhsT=onesm, rhs=sq[0:n, :], start=True, stop=True, r=[sq, cbf], w=[PM])
            k.op("act", "activation", out=rs[0:n, :], in_=PM[0:n, 0:T], func=AF.Sqrt, scale=1.0 / norm, bias=EPS,
                 r=[PM], w=[rs])
            k.op("dve", "reciprocal", out=rs[0:n, :], in_=rs[0:n, :], r=[rs], w=[rs])
            if not rope:
                k.op("dve", "scalar_tensor_tensor", out=yb[0:n, :], in0=ps[0:n, 0:T], scalar=gcol[0:n, :], in1=rs[0:n, :],
                     op0=ALU.mult, op1=ALU.mult, r=[ps, rs, gc, gcs], w=[yb])
            else:
                k.op("dve", "scalar_tensor_tensor", out=yf[0:n, :], in0=ps[0:n, 0:T], scalar=gcol[0:n, :], in1=rs[0:n, :],
                     op0=ALU.mult, op1=ALU.mult, r=[ps, rs, gc, gcs], w=[yf])
                yb2 = ybf[cnt["y"] % 3]
                cnt["y"] += 1
                k.op("act", "copy", out=yb2[0:n, :], in_=yf[0:n, :], r=[yf], w=[yb2])
                k.op("pe", "matmul", PM[0:n, 0:T], lhsT=cb("prot", n, n), rhs=yb2[0:n, :], start=True, stop=True,
                     r=[yb2, cbf], w=[PM])
                k.op("dve", "tensor_tensor", out=t1[0:n, :], in0=yf[0:n, :], in1=cosb[0:n, :], op=ALU.mult,
                     r=[yf, cosb], w=[t1])
                k.op("dve", "tensor_tensor", out=t2[0:n, :], in0=PM[0:n, 0:T], in1=sinb[0:n, :], op=ALU.mult,
                     r=[PM, sinb], w=[t2])
                k.op("dve", "tensor_tensor", out=yb[0:n, :], in0=t1[0:n, :], in1=t2[0:n, :], op=ALU.add,
                     r=[t1, t2], w=[yb])
        k.dma("sp", dest, yb[0:n, :], key=yb, r=[yb])

    def proj_mm_fm(ps, wv, nin, c0, n, inv):
        for c in range(nin):
            k.op("pe", "matmul", ps[0:n, 0:T], lhsT=wv[:, c, c0:c0 + n], rhs=inv[:, c, :], start=(c == 0),
                 stop=(c == nin - 1), r=list(proj_r), w=[ps])

    proj_r = []

    def load_win(c0, n):
        wb = nextwb()
        wv = wb[:, 0:16 * n].rearrange("p (c n) -> p c n", c=16)
        k.dma("pool", wv, W["w_in"][l][:, c0:c0 + n].rearrange("(c p) n -> p c n", p=128), key=wb, w=[wb])
        return wb, wv

    def tm_group(wb, wv, nin, c0, n, inv, tt, dest_list, sigmoid=False):
        vo = vout[cnt["v"] % 2]
        cnt["v"] += 1
        vov = vo[:].rearrange("p (b n) -> p b n", b=4)
        for tb in range(4):
            pd = PD[tb % 2]
            for c in range(nin):
                k.op("pe", "matmul", pd[:, 0:n], lhsT=inv[:, c, tb * 128:(tb + 1) * 128], rhs=wv[:, c, c0:c0 + n],
                     start=(c == 0), stop=(c == nin - 1), r=list(proj_r), w=[pd])
            if sigmoid:
                k.op("act", "activation", out=gout[:, tb * 12:(tb + 1) * 12], in_=pd[:, 0:n], func=AF.Sigmoid,
                     r=[pd], w=[gout])
            else:
                k.op("act", "copy", out=vov[:, tb, 0:n], in_=pd[:, 0:n], r=[pd], w=[vo])
        if sigmoid:
            k.dma("sp", GT[tt * T:(tt + 1) * T, :].rearrange("(b p) n -> p b n", p=128),
                  gout[:].rearrange("p (b n) -> p b n", b=4), key=gout, r=[gout])
        else:
            for (co, wd_, dap) in dest_list:
                k.dma("sp", dap[tt * T:(tt + 1) * T, :].rearrange("(b p) n -> p b n", p=128),
                      vov[:, :, co:co + wd_], key=vo, r=[vo])

    def projections(tt):
        xv = xnT[:].rearrange("p (c t) -> p c t", c=16)
        tsl = slice(tt * T, (tt + 1) * T)
        k.dma("sp", cosb[0:64, :], cos_d[:, tsl], key=cosb, w=[cosb])
        k.dma("sp", sinb[0:64, :], sin_d[:, tsl], key=sinb, w=[sinb])
        gcol = lambda nm: gc[:, GC[nm]:GC[nm] + 1]
        gscol = lambda nm: gcs[:, GC[nm]:GC[nm] + 1]
        wb, wv = load_win(0, 512)
        proj_r[:] = [wb, xnT]
        cqv = cqf[:].rearrange("p (c t) -> p c t", c=4)
        for c in range(4):
            ps = PG[c % 2]
            proj_mm_fm(ps, wv, 16, c * 128, 128, xv)
            k.op("act", "copy", out=cqv[:, c, :], in_=ps[:, 0:T], r=[ps], w=[cqf])
            sq = sqb[c % 2]
            k.op("act", "activation", out=sq[:], in_=ps[:, 0:T], func=AF.Square, r=[ps], w=[sq])
            k.op("pe", "matmul", PM[:, 0:T], lhsT=cb("ones"), rhs=sq[:], start=(c == 0), stop=(c == 3),
                 r=[sq, cbf], w=[PM])
        rs = rstd[0]
        k.op("act", "activation", out=rs[:], in_=PM[:, 0:T], func=AF.Sqrt, scale=1.0 / 512, bias=EPS, r=[PM], w=[rs])
        k.op("dve", "reciprocal", out=rs[:], in_=rs[:], r=[rs], w=[rs])
        cqnv = cqn[:].rearrange("p (c t) -> p c t", c=4)
        for c in range(4):
            k.op("dve", "scalar_tensor_tensor", out=cqnv[:, c, :], in0=cqv[:, c, :], scalar=gcol(f"cq{c}"), in1=rs[:],
                 op0=ALU.mult, op1=ALU.mult, r=[cqf, rs, gc], w=[cqn])
        wb, wv = load_win(512, 256 + 64)
        proj_r[:] = [wb, xnT]
        for c in range(2):
            ps = PG[c % 2]
            proj_mm_fm(ps, wv, 16, c * 128, 128, xv)
            k.op("act", "copy", out=cqv[:, c, :], in_=ps[:, 0:T], r=[ps], w=[cqf])
            sq = sqb[c % 2]
            k.op("act", "activation", out=sq[:], in_=ps[:, 0:T], func=AF.Square, r=[ps], w=[sq])
            k.op("pe", "matmul", PM[:, 0:T], lhsT=cb("ones"), rhs=sq[:], start=(c == 0), stop=(c == 1),
                 r=[sq, cbf], w=[PM])
        rs = rstd[1]
        k.op("act", "activation", out=rs[:], in_=PM[:, 0:T], func=AF.Sqrt, scale=1.0 / 256, bias=EPS, r=[PM], w=[rs])
        k.op("dve", "reciprocal", out=rs[:], in_=rs[:], r=[rs], w=[rs])
        ckvv = ckvn[:].rearrange("p (c t) -> p c t", c=2)
        for c in range(2):
            k.op("dve", "scalar_tensor_tensor", out=ckvv[:, c, :], in0=cqv[:, c, :], scalar=gcol(f"ckv{c}"), in1=rs[:],
                 op0=ALU.mult, op1=ALU.mult, r=[cqf, rs, gc], w=[ckvn])
        ps = PU[0]
        proj_mm_fm(ps, wv, 16, 256, 64, xv)
        fm_finish(ps, 64, tt, 64, gcol("kr"), 1.0, True, KT[4][0:64, tsl])
        wuqv = wuq[:].rearrange("p (c n) -> p c n", c=4)
        wukvv = wukv[:].rearrange("p (c n) -> p c n", c=2)
        for h in range(4):
            proj_r[:] = [wuq, cqn]
            ps = PG[h % 2]
            proj_mm_fm(ps, wuqv, 4, h * 192, 128, cqnv)
            fm_finish(ps, 128, tt, 128, gscol("qn"), 1.0, False, QT[h][:, tsl])
            ps = PU[h % 2]
            proj_mm_fm(ps, wuqv, 4, h * 192 + 128, 64, cqnv)
            fm_finish(ps, 64, tt, 64, gscol("qr"), 1.0, True, QT[4 + h][0:64, tsl])
            proj_r[:] = [wukv, ckvn]
            ps = PG[(h + 1) % 2]
            proj_mm_fm(ps, wukvv, 2, h * 256, 128, ckvv)
            fm_finish(ps, 128, tt, 128, gcol("kn"), 1.0, False, KT[h][:, tsl])
        proj_r[:] = [wukv, ckvn]
        for h in range(4):
            tm_group(wukv, wukvv, 2, h * 256 + 128, 128, ckvv, tt, [(0, 128, VV[h])])
        wb, wv = load_win(832, 512)
        proj_r[:] = [wb, xnT]
        for h in range(4):
            ps = PG[h % 2]
            proj_mm_fm(ps, wv, 16, h * 128, 128, xv)
            fm_finish(ps, 128, tt, 64, gscol("dq"), 1.0, False, QT[8 + h][:, tsl])
        wb, wv = load_win(1344, 512)
        proj_r[:] = [wb, xnT]
        for h in range(4):
            ps = PU[h % 2]
            proj_mm_fm(ps, wv, 16, h * 128, 128, xv)
            fm_finish(ps, 128, tt, 64, gcol("dk"), 1.0, False, KT[5 + h][:, tsl])
        wb, wv = load_win(1856, 512)
        proj_r[:] = [wb, xnT]
        tm_group(wb, wv, 16, 0, 512, xv, tt, [(h * 128, 128, VV[4 + h]) for h in range(4)])
        wb, wv = load_win(2368, 512)
        proj_r[:] = [wb, xnT]
        for h in range(4):
            ps = PG[h % 2]
            proj_mm_fm(ps, wv, 16, h * 128, 128, xv)
            fm_finish(ps, 128, tt, 128, gscol("nq"), 1.0, False, QT[12 + h][:, tsl])
        wb, wv = load_win(2880, 512)
        proj_r[:] = [wb, xnT]
        ps = PU[0]
        proj_mm_fm(ps, wv, 16, 0, 128, xv)
        fm_finish(ps, 128, tt, None, None, 1.0, False, KT[9][:, tsl])
        ps = PU[1]
        proj_mm_fm(ps, wv, 16, 128, 128, xv)
        fm_finish(ps, 128, tt, None, None, 1.0, False, KT[10][:, tsl])
        ps = PU[0]
        proj_mm_fm(ps, wv, 16, 256, 128, xv)
        fm_finish(ps, 128, tt, 128, gcol("nks"), 1.0, False, KT[11][:, tsl])
        tm_group(wb, wv, 16, 384, 128, xv, tt, [(0, 128, VV[8])])
        wb, wv = load_win(3392, 268)
        proj_r[:] = [wb, xnT]
        ps = PU[1]
        proj_mm_fm(ps, wv, 16, 0, 128, xv)
        fm_finish(ps, 128, tt, 128, gcol("nkw"), 1.0, False, KT[12][:, tsl])
        tm_group(wb, wv, 16, 128, 128, xv, tt, [(0, 128, VV[9])])
        tm_group(wb, wv, 16, 256, 12, xv, tt, None, sigmoid=True)
        wb, wv = load_win(3660, 512)
        proj_r[:] = [wb, xnT]
        for h in range(4):
            ps = PG[h % 2]
            proj_mm_fm(ps, wv, 16, h * 128, 128, xv)
            fm_finish(ps, 128, tt, None, None, 128 ** -0.5, False, QT[16 + h][:, tsl])
        wb, wv = load_win(4172, 512)
        proj_r[:] = [wb, xnT]
        for h in range(4):
            ps = PU[h % 2]
            proj_mm_fm(ps, wv, 16, h * 128, 128, xv)
            fm_finish(ps, 128, tt, None, None, 1.0, False, KT[13 + h][:, tsl])
        wb, wv = load_win(4684, 512)
        proj_r[:] = [wb, xnT]
        tm_group(wb, wv, 16, 0, 512, xv, tt, [(h * 128, 128, VV[10 + h]) for h in range(4)])

    def wout_add(tt):
        ot = xnT
        ov = ot[:].rearrange("p (c t) -> p c t", c=16)
        k.dma("sp", ov, OT[:, :, tt * T:(tt + 1) * T].rearrange("c p t -> p c t"), key=ot, w=[ot])
        for cg in range(4):
            wb = nextwb()
            wv = wb[:, 0:8192].rearrange("p (c n) -> p c n", c=16)
            k.dma("pool", wv, W["w_out"][l][:, cg * 512:(cg + 1) * 512].rearrange("(c p) n -> p c n", p=128), key=wb, w=[wb])
            for tb in range(4):
                pd = PD[tb % 2]
                for c in range(16):
                    k.op("pe", "matmul", pd[:], lhsT=ov[:, c, tb * 128:(tb + 1) * 128], rhs=wv[:, c, :],
                         start=(c == 0), stop=(c == 15), r=[wb, ot], w=[pd])
                hv = htile[:, tb * D + cg * 512: tb * D + (cg + 1) * 512]
                k.op("dve", "tensor_tensor", out=hv, in0=pd[:], in1=hv, op=ALU.add, r=[pd], w=[htile])

    for tt in range(NTT):
        load_h(tt, src)
        if A:
            rms_xnT("ffn1_norm")
            ffn(W["ffn1_w_gate"][l], W["ffn1_w_up"][l], W["ffn1_w_down"][l])
            store_h(tt, dst)
            rms_xnT("mix_norm")
            projections(tt)
        else:
            wout_add(tt)
            rms_xnT("ffn2_norm")
            ffn(W["ffn2_w_gate"][l], W["ffn2_w_up"][l], W["ffn2_w_down"][l])
            store_h(tt, dst)


def phase_attn(k, es, nc, C, W, l, cb, cbf, cf, QT, KT, VV, GT, OT, lam_init, only=None):
    NG = S // 512

    def common(ms):
        d = {}
        d["S"] = [k.ps(ms, f"S{i}", 512) for i in range(3)]
        d["O"] = [k.ps(ms, f"O{i}", 512) for i in range(4)]
        d["PT"] = k.ps(ms, "PTt", 1024, BF16)
        d["pt"] = [k.sb(ms, f"pt{i}", 512, BF16) for i in range(3)]
        d["on"] = [k.sb(ms, f"on{i}", 512, F32) for i in range(3)]
        d["sm"] = k.sb(ms, "sm", 16, F32)
        d["ob"] = [k.sb(ms, f"ob{i}", 128, BF16) for i in range(2)]
        d["ot"] = [k.sb(ms, f"ot{i}", 512, BF16) for i in range(2)]
        d["gbc"] = k.sb(ms, "gbco", 128, F32)
        d["junk"] = k.sb(ms, "junk", 128, F32)
        d["cnt"] = {"s": 0, "p": 0, "ob": 0, "ot": 0}
        return d

    def causal_mode(kt, qbg):
        if kt > qbg:
            return "skip"
        return "tri" if kt == qbg else "full"

    def std_mask(m, kt, qb):
        return cb(m), [cbf]

    def soft_sub(cm, g, pairs, vfn, nv, ktlist, modefn, biasfn, maskfn, fin, preads, nchunk=4):
        ktlist = list(ktlist)
        valid = {qb: [kt for kt in ktlist if modefn(kt, 4 * g + qb) != "skip"] for qb in range(4)}
        steps = []
        for kt in ktlist:
            qbs = [qb for qb in range(4) if modefn(kt, 4 * g + qb) != "skip"]
            if qbs:
                steps.append((kt, qbs))

        def emit_qk(st):
            kt, qbs = st["kt"], st["qbs"]
            lo, hi = qbs[0], qbs[-1] + 1
            Sb = cm["S"][cm["cnt"]["s"] % 3]
            cm["cnt"]["s"] += 1
            P = cm["pt"][cm["cnt"]["p"] % 3]
            cm["cnt"]["p"] += 1
            st["S"], st["P"] = Sb, P
            for i, (lf, rf) in enumerate(pairs):
                k.op("pe", "matmul", Sb[:, lo * 128:hi * 128], lhsT=lf(kt), rhs=rf(g * 512 + lo * 128, g * 512 + hi * 128),
                     start=(i == 0), stop=(i == len(pairs) - 1), r=preads, w=[Sb])

        def emit_act(st):
            kt, qbs, Sb, P = st["kt"], st["qbs"], st["S"], st["P"]
            lo, hi = qbs[0], qbs[-1] + 1
            if biasfn is None:
                k.op("act", "activation", out=P[:, lo * 128:hi * 128], in_=Sb[:, lo * 128:hi * 128], func=AF.Exp,
                     r=[Sb], w=[P])
            else:
                cs = 4 // nchunk
                for c in range(nchunk):
                    a, b = max(lo, c * cs), min(hi, (c + 1) * cs)
                    if a >= b:
                        continue
                    k.op("act", "activation", out=P[:, a * 128:b * 128], in_=Sb[:, a * 128:b * 128],
                         func=AF.Exp, bias=biasfn(kt, 4 * g + (c + 1) * cs - 1), r=[Sb, cf], w=[P])
            for qb in qbs:
                m = modefn(kt, 4 * g + qb)
                if m != "full":
                    map_, mr = maskfn(m, kt, qb)
                    k.op("dve", "tensor_tensor", out=P[:, qb * 128:(qb + 1) * 128], in0=P[:, qb * 128:(qb + 1) * 128],
                         in1=map_, op=ALU.mult, r=mr, w=[P])

        def emit_pv(st):
            kt, qbs, P = st["kt"], st["qbs"], st["P"]
            for qb in qbs:
                k.op("pe", "matmul", cm["O"][qb][:, 0:nv], lhsT=P[:, qb * 128:(qb + 1) * 128], rhs=vfn(kt),
                     start=(kt == valid[qb][0]), stop=(kt == valid[qb][-1]), r=[P] + preads, w=[cm["O"][qb]])

        prev = None
        for (kt, qbs) in steps:
            st = {"kt": kt, "qbs": qbs}
            emit_qk(st)
            if prev is not None:
                emit_pv(prev)
            emit_act(st)
            prev = st
        if prev is not None:
            emit_pv(prev)
        for qb in range(4):
            fin(qb, cm["O"][qb])

    def fin_norm(cm, onb):
        def fin(qb, Ob):
            sm = cm["sm"]
            k.op("dve", "tensor_scalar", out=sm[:, qb:qb + 1], in0=Ob[:, 128:129], scalar1=1e-37, scalar2=None,
                 op0=ALU.max, r=[Ob], w=[sm])
            k.op("dve", "reciprocal", out=sm[:, qb:qb + 1], in_=sm[:, qb:qb + 1], r=[sm], w=[sm])
            k.op("dve", "tensor_scalar", out=onb[:, qb * 128:(qb + 1) * 128], in0=Ob[:, 0:128], scalar1=sm[:, qb:qb + 1],
                 scalar2=None, op0=ALU.mult, r=[Ob, sm], w=[onb])
        return fin

    def finish_head(cm, slot, g, onb, extra):
        sm = cm["sm"]
        otb = cm["ot"][cm["cnt"]["ot"] % 2]
        cm["cnt"]["ot"] += 1
        for qb in range(4):
            ov = onb[:, qb * 128:(qb + 1) * 128]
            k.op("dve", "memset", sm[:, 4 + qb:5 + qb], 0.0, w=[sm])
            k.op("act", "activation", out=cm["junk"][:], in_=ov, func=AF.Square, accum_out=sm[:, 4 + qb:5 + qb],
                 r=[onb], w=[cm["junk"], sm])
            k.op("act", "activation", out=sm[:, 8 + qb:9 + qb], in_=sm[:, 4 + qb:5 + qb], func=AF.Sqrt, scale=1.0 / 128,
                 bias=EPS, r=[sm], w=[sm])
            k.op("dve", "reciprocal", out=sm[:, 8 + qb:9 + qb], in_=sm[:, 8 + qb:9 + qb], r=[sm], w=[sm])
            if extra != 1.0:
                k.op("dve", "tensor_scalar", out=sm[:, 8 + qb:9 + qb], in0=sm[:, 8 + qb:9 + qb], scalar1=float(extra),
                     scalar2=None, op0=ALU.mult, r=[sm], w=[sm])
            ob = cm["ob"][cm["cnt"]["ob"] % 2]
            cm["cnt"]["ob"] += 1
            k.op("dve", "scalar_tensor_tensor", out=ob[:], in0=ov, scalar=sm[:, 8 + qb:9 + qb], in1=cm["gbc"][:],
                 op0=ALU.mult, op1=ALU.mult, r=[onb, sm, cm["gbc"]], w=[ob])
            k.op("pe", "transpose", out=cm["PT"][:, qb * 128:(qb + 1) * 128], in_=ob[:], identity=cb("ident"),
                 r=[ob, cbf], w=[cm["PT"]])
        k.op("act", "copy", out=otb[:], in_=cm["PT"][:, 0:512], r=[cm["PT"]], w=[otb])
        k.dma("sp", OT[slot][:, g * 512:(g + 1) * 512], otb[:], key=otb, r=[otb])

    def load_v(v, src):
        vv = v[:].rearrange("p (t c) -> p t c", c=132)
        k.dma("sp", vv[:, :, 0:128], src.rearrange("(t p) d -> p t d", p=128), key=v, w=[v])
        return vv

    def ab_bias(si):
        return lambda kt, qbg: cf[:, si * 32 + (kt - qbg + 31): si * 32 + (kt - qbg + 31) + 1]

    def ab_chunks(si):
        sl = SLOPES[si]
        return 4 if sl * 255 > 70 else (2 if sl * 511 > 70 else 1)

    def far_skip(si, modefn):
        sl = SLOPES[si]

        def f(kt, qbg):
            rel = kt - qbg
            if rel < 0 and sl * 128.0 * (-rel - 1) >= 110.0:
                return "skip"
            return modefn(kt, qbg)
        return f

    mixers = only if only is not None else ("mla", "diff", "nsa", "sb")

    if "mla" in mixers:
        with ExitStack() as ms:
            cm = common(ms)
            aq = [k.sb(ms, f"aq{i}", S, BF16) for i in range(2)]
            aqr = [k.sb(ms, f"aqr{i}", S, BF16) for i in range(2)]
            ak = [k.sb(ms, f"ak{i}", S, BF16) for i in range(2)]
            akr = k.sb(ms, "akr", S, BF16)
            av = [k.sb(ms, f"av{i}", 32 * 132, BF16) for i in range(2)]
            k.dma("sp", cm["gbc"][:], W["mla_o_norm"][l].partition_broadcast(128), key=cm["gbc"], w=[cm["gbc"]])
            k.dma("sp", akr[0:64, :], KT[4][0:64, :], key=akr, w=[akr])
            for i in range(2):
                k.op("dve", "memset", av[i][:], 1.0, w=[av[i]])
            for h in range(4):
                q, qr, kk, v = aq[h % 2], aqr[h % 2], ak[h % 2], av[h % 2]
                k.dma("sp", q[:], QT[h], key=q, w=[q])
                k.dma("sp", qr[0:64, :], QT[4 + h][0:64, :], key=qr, w=[qr])
                k.dma("sp", kk[:], KT[h], key=kk, w=[kk])
                vv = load_v(v, VV[h])
                pairs = [(lambda kt, kk=kk: kk[:, kt * 128:(kt + 1) * 128], lambda a, b, q=q: q[:, a:b]),
                         (lambda kt: akr[0:64, kt * 128:(kt + 1) * 128], lambda a, b, qr=qr: qr[0:64, a:b])]
                for g in range(NG):
                    onb = cm["on"][0]
                    soft_sub(cm, g, pairs, lambda kt, vv=vv: vv[:, kt, 0:129], 129, range(0, 4 * g + 4), causal_mode,
                             None, std_mask, fin_norm(cm, onb), [q, qr, kk, akr, v])
                    finish_head(cm, h, g, onb, 1.0)
            k.flush()

    if "diff" in mixers:
        with ExitStack() as ms:
            cm = common(ms)
            aq = [k.sb(ms, f"dq{i}", S, BF16) for i in range(2)]
            ak = [k.sb(ms, f"dk{i}", S, BF16) for i in range(2)]
            av = [k.sb(ms, f"dv{i}", 32 * 132, BF16) for i in range(2)]
            lt = k.sb(ms, "lt", 256, F32)
            ltmp = k.sb(ms, "ltmp", 64, F32)
            ls = k.sb(ms, "ls", 8, F32)
            k.dma("sp", cm["gbc"][:], W["diff_subln"][l].partition_broadcast(128), key=cm["gbc"], w=[cm["gbc"]])
            for i, nm in enumerate(["diff_lq1", "diff_lk1", "diff_lq2", "diff_lk2"]):
                k.dma("sp", lt[:, i * 64:(i + 1) * 64], W[nm][l].partition_broadcast(128), key=lt, w=[lt])
            for j in range(2):
                k.op("dve", "tensor_tensor", out=ltmp[:], in0=lt[:, j * 128:j * 128 + 64], in1=lt[:, j * 128 + 64:j * 128 + 128],
                     op=ALU.mult, r=[lt], w=[ltmp])
                k.op("dve", "reduce_sum", out=ls[:, j:j + 1], in_=ltmp[:], axis=AX.X, r=[ltmp], w=[ls])
                k.op("act", "activation", out=ls[:, 2 + j:3 + j], in_=ls[:, j:j + 1], func=AF.Exp, r=[ls], w=[ls])
            k.op("dve", "tensor_tensor", out=ls[:, 4:5], in0=ls[:, 3:4], in1=ls[:, 2:3], op=ALU.subtract, r=[ls], w=[ls])
            k.op("dve", "tensor_scalar", out=ls[:, 5:6], in0=ls[:, 4:5], scalar1=float(-lam_init), scalar2=None, op0=ALU.add,
                 r=[ls], w=[ls])
            for i in range(2):
                k.op("dve", "memset", av[i][:], 1.0, w=[av[i]])
            for h in range(4):
                q, kk, v = aq[h % 2], ak[h % 2], av[h % 2]
                k.dma("sp", q[:], QT[8 + h], key=q, w=[q])
                k.dma("sp", kk[:], KT[5 + h], key=kk, w=[kk])
                vv = load_v(v, VV[4 + h])
                for g in range(NG):
                    for sub in range(2):
                        r0, r1 = sub * 64, (sub + 1) * 64
                        pairs = [(lambda kt, kk=kk, r0=r0, r1=r1: kk[r0:r1, kt * 128:(kt + 1) * 128],
                                  lambda a, b, q=q, r0=r0, r1=r1: q[r0:r1, a:b])]
                        soft_sub(cm, g, pairs, lambda kt, vv=vv: vv[:, kt, 0:129], 129, range(0, 4 * g + 4),
                                 far_skip(2 * h, causal_mode), ab_bias(2 * h), std_mask, fin_norm(cm, cm["on"][sub]),
                                 [q, kk, v], nchunk=ab_chunks(2 * h))
                    on0, on1 = cm["on"][0], cm["on"][1]
                    k.op("dve", "scalar_tensor_tensor", out=on0[:], in0=on1[:], scalar=ls[:, 5:6], in1=on0[:],
                         op0=ALU.mult, op1=ALU.add, r=[on1, ls], w=[on0])
                    finish_head(cm, 4 + h, g, on0, 1.0 - lam_init)
            k.flush()

    if "nsa" in mixers:
        with ExitStack() as ms:
            cm = common(ms)
            nq = [k.sb(ms, f"nq{i}", S, BF16) for i in range(4)]
            nks = k.sb(ms, "nks", S, BF16)
            nkw = k.sb(ms, "nkw", S, BF16)
            nvs = k.sb(ms, "nvs", 32 * 132, BF16)
            nvw = k.sb(ms, "nvw", 32 * 132, BF16)
            raw = [k.sb(ms, f"raw{i}", S, BF16) for i in range(2)]
            wc = [k.sb(ms, f"wc{i}", 32 * 128, BF16) for i in range(2)]
            kcT = k.sb(ms, "kcT", 256, BF16)
            vcx = k.sb(ms, "vcx", 2 * 196, BF16)
            gts = k.sb(ms, "gts", 32 * 12, F32)
            bonus = k.sb(ms, "bonus", 2048, F32)
            maskT = k.sb(ms, "maskT", 4 * 32 * 128, BF16)
            selexp = k.sb(ms, "selexp", 4096, BF16)
            selb = k.sb(ms, "selb", 64, BF16)
            scb = k.sb(ms, "scb", 64, F32)
            sc2 = k.sb(ms, "sc2", 64, F32)
            imp = [k.sb(ms, f"imp{i}", 64, F32) for i in range(4)]
            m8 = k.sb(ms, "m8", 16, F32)
            oc = k.sb(ms, "oc", 4 * 4 * 128, F32)
            pe_f = k.sb(ms, "pe_f", 128, F32)
            pe_b = k.sb(ms, "pe_b", 128, BF16)
            peT = k.sb(ms, "peT", 32, BF16)
            crow = k.sb(ms, "crow", 128, BF16)
            kcn = k.sb(ms, "kcn", 128, BF16)
            gcn = k.sb(ms, "gcn", NGC, F32)
            sm = cm["sm"]
            k.dma("sp", cm["gbc"][:], W["nsa_o_norm"][l].partition_broadcast(128), key=cm["gbc"], w=[cm["gbc"]])
            k.dma("sp", gcn[:], W["gcols"][l], key=gcn, w=[gcn])
            k.dma("sp", bonus[:], W["bonus"], key=bonus, w=[bonus])
            k.dma("sp", gts[:].rearrange("p (b n) -> p b n", n=12), GT.rearrange("(b p) n -> p b n", p=128), key=gts, w=[gts])
            for h in range(4):
                k.dma("sp", nq[h][:], QT[12 + h], key=nq[h], w=[nq[h]])
            k.dma("sp", nks[:], KT[11], key=nks, w=[nks])
            k.dma("sp", nkw[:], KT[12], key=nkw, w=[nkw])
            k.op("dve", "memset", nvs[:], 1.0, w=[nvs])
            k.op("dve", "memset", nvw[:], 1.0, w=[nvw])
            vsv = load_v(nvs, VV[8])
            vwv = load_v(nvw, VV[9])
            k.dma("sp", raw[0][:], KT[9], key=raw[0], w=[raw[0]])
            k.dma("sp", raw[1][:], KT[10], key=raw[1], w=[raw[1]])
            k.dma("pool", wc[0][:].rearrange("p (l n) -> p l n", n=128), W["nsa_w_ck"][l].rearrange("(l p) n -> p l n", p=128),
                  key=wc[0], w=[wc[0]])
            k.dma("pool", wc[1][:].rearrange("p (l n) -> p l n", n=128), W["nsa_w_cv"][l].rearrange("(l p) n -> p l n", p=128),
                  key=wc[1], w=[wc[1]])
            k.op("dve", "memset", kcT[:], 0.0, w=[kcT])
            k.op("dve", "memset", vcx[:], 0.0, w=[vcx])
            vcv = vcx[:].rearrange("p (t c) -> p t c", c=196)
            k.op("dve", "memset", vcv[:, :, 128:129], 1.0, w=[vcx])
            k.op("dve", "tensor_copy", out=vcv[:, :, 129:193], in_=cb("ov").rearrange("p (t c) -> p t c", c=64), r=[cbf], w=[vcx])
            for which in range(2):
                pen = "nsa_pe_k" if which == 0 else "nsa_pe_v"
                wv = wc[which][:].rearrange("p (l n) -> p l n", n=128)
                k.dma("sp", pe_f[0:32, :], W[pen][l], key=pe_f, w=[pe_f])
                k.op("act", "copy", out=pe_b[0:32, :], in_=pe_f[0:32, :], r=[pe_f], w=[pe_b])
                k.op("pe", "transpose", out=cm["PT"][:, 0:32], in_=pe_b[0:32, :], identity=cb("ident", 32, 32),
                     r=[pe_b, cbf], w=[cm["PT"]])
                k.op("act", "copy", out=peT[:], in_=cm["PT"][:, 0:32], r=[cm["PT"]], w=[peT])
                Oc = cm["O"][0]
                for li in range(32):
                    k.op("pe", "matmul", Oc[0:1, 0:128], lhsT=peT[:, li:li + 1], rhs=wv[:, li, :], start=(li == 0),
                         stop=(li == 31), r=[peT, wc[which]], w=[Oc])
                k.op("act", "copy", out=crow[0:1, :], in_=Oc[0:1, 0:128], r=[Oc], w=[crow])
                for nt in range(2):
                    nr = 128 if nt == 0 else 127
                    Ok = cm["O"][1 + nt]
                    for li in range(32):
                        a0 = nt * 2048 + li
                        k.op("pe", "matmul", Ok[0:nr, 0:128], lhsT=raw[which][:, a0:a0 + 16 * (nr - 1) + 1:16], rhs=wv[:, li, :],
                             start=(li == 0), stop=False, r=[raw[which], wc[which]], w=[Ok])
                    k.op("pe", "matmul", Ok[0:nr, 0:128], lhsT=cb("ones", 1, nr), rhs=crow[0:1, :], start=False, stop=True,
                         r=[crow, cbf], w=[Ok])
                    if which == 0:
                        k.op("dve", "memset", sm[:, 0:1], 0.0, w=[sm])
                        k.op("act", "activation", out=cm["junk"][0:nr, :], in_=Ok[0:nr, 0:128], func=AF.Square,
                             accum_out=sm[0:nr, 0:1], r=[Ok], w=[cm["junk"], sm])
                        k.op("act", "activation", out=sm[0:nr, 1:2], in_=sm[0:nr, 0:1], func=AF.Sqrt, scale=1.0 / 128, bias=EPS,
                             r=[sm], w=[sm])
                        k.op("dve", "reciprocal", out=sm[0:nr, 1:2], in_=sm[0:nr, 1:2], r=[sm], w=[sm])
                        k.op("dve", "memset", kcn[:], 0.0, w=[kcn])
                        k.op("dve", "tensor_scalar", out=kcn[0:nr, :], in0=Ok[0:nr, 0:128], scalar1=sm[0:nr, 1:2], scalar2=None,
                             op0=ALU.mult, r=[Ok, sm], w=[kcn])
                        k.op("pe", "transpose", out=cm["PT"][:, 128:256], in_=kcn[:], identity=cb("ident"), r=[kcn, cbf],
                             w=[cm["PT"]])
                        k.op("dve", "tensor_scalar", out=kcT[:, nt * 128:(nt + 1) * 128], in0=cm["PT"][:, 128:256],
                             scalar1=gcn[:, GC["nkc"]:GC["nkc"] + 1], scalar2=None, op0=ALU.mult, r=[cm["PT"], gcn], w=[kcT])
                    else:
                        k.op("act", "copy", out=vcv[0:nr, nt, 0:128], in_=Ok[0:nr, 0:128], r=[Ok], w=[vcx])
            ocv = oc[:].rearrange("p (q h d) -> p q h d", q=4, h=4)
            mkv = maskT[:].rearrange("p (q t d) -> p q t d", q=4, t=32)

            def cmp_mode(nt, qbg):
                j = qbg - 16 * nt
                if j < 0:
                    return "skip"
                if j >= 17:
                    return "full"
                return ("cm", j)

            def cmp_mask(m, kt, qb):
                return cb("cm", 128, 128, m[1] * 128), [cbf]

            def sel_mode(kt, qbg):
                return "skip" if kt > qbg else "m"

            def win_mode(kt, qbg):
                rel = kt - qbg
                if rel > 0 or rel < -4:
                    return "skip"
                if rel == 0:
                    return "tri"
                if rel == -4:
                    return "atri"
                return "full"

            for g in range(NG):
                for h in range(4):
                    def fin_cmp(qb, Ob, h=h):
                        k.op("dve", "tensor_scalar", out=sm[:, qb:qb + 1], in0=Ob[:, 128:129], scalar1=1e-37, scalar2=None,
                             op0=ALU.max, r=[Ob], w=[sm])
                        k.op("dve", "reciprocal", out=sm[:, qb:qb + 1], in_=sm[:, qb:qb + 1], r=[sm], w=[sm])
                        k.op("dve", "tensor_scalar", out=ocv[:, qb, h, :], in0=Ob[:, 0:128], scalar1=sm[:, qb:qb + 1],
                             scalar2=None, op0=ALU.mult, r=[Ob, sm], w=[oc])
                        if h == 0:
                            k.op("dve", "tensor_scalar", out=imp[qb][:], in0=Ob[:, 129:193], scalar1=sm[:, qb:qb + 1],
                                 scalar2=None, op0=ALU.mult, r=[Ob, sm], w=[imp[qb]])
                        else:
                            k.op("dve", "scalar_tensor_tensor", out=imp[qb][:], in0=Ob[:, 129:193], scalar=sm[:, qb:qb + 1],
                                 in1=imp[qb][:], op0=ALU.mult, op1=ALU.add, r=[Ob, sm], w=[imp[qb]])
                    pairs = [(lambda nt: kcT[:, nt * 128:(nt + 1) * 128], lambda a, b, h=h: nq[h][:, a:b])]
                    cbias = lambda nt, qbg, h=h: cf[:, 256 + h * 64 + nt * 32 + qbg: 256 + h * 64 + nt * 32 + qbg + 1]
                    soft_sub(cm, g, pairs, lambda nt: vcv[:, nt, 0:193], 193, range(2), cmp_mode, cbias, cmp_mask, fin_cmp,
                             [kcT, nq[h], vcx])
                for qb in range(4):
                    qbg = 4 * g + qb
                    k.op("dve", "tensor_tensor", out=scb[:], in0=imp[qb][:], in1=bonus[:, qbg * 64:(qbg + 1) * 64], op=ALU.add,
                         r=[imp[qb], bonus], w=[scb])
                    k.op("dve", "max", out=m8[:, 0:8], in_=scb[:], r=[scb], w=[m8])
                    k.op("dve", "match_replace", out=sc2[:], in_to_replace=m8[:, 0:8], in_values=scb[:], imm_value=-3.0e38,
                         r=[m8, scb], w=[sc2])
                    k.op("dve", "max", out=m8[:, 8:16], in_=sc2[:], r=[sc2], w=[m8])
                    k.op("dve", "tensor_scalar", out=sm[:, 12:13], in0=m8[:, 15:16], scalar1=-1e29, scalar2=None, op0=ALU.max,
                         r=[m8], w=[sm])
                    k.op("dve", "tensor_scalar", out=selb[:], in0=scb[:], scalar1=sm[:, 12:13], scalar2=None, op0=ALU.is_ge,
                         r=[scb, sm], w=[selb])
                    k.op("dve", "tensor_copy", out=selexp[:].rearrange("p (j s) -> p j s", s=64),
                         in_=selb[:, 0:64].unsqueeze(2).to_broadcast([128, 64, 64]), r=[selb], w=[selexp])
                    for kt0 in range(0, qbg + 1, 4):
                        n = min(4, qbg + 1 - kt0)
                        Sb = cm["S"][cm["cnt"]["s"] % 3]
                        cm["cnt"]["s"] += 1
                        for j in range(n):
                            k.op("pe", "matmul", Sb[:, j * 128:(j + 1) * 128], lhsT=selexp[:, (kt0 + j) * 128:(kt0 + j + 1) * 128],
                                 rhs=cb("ident"), start=True, stop=True, r=[selexp, cbf], w=[Sb])
                        k.op("act", "copy", out=mkv[:, qb, kt0:kt0 + n, :], in_=Sb[:, 0:n * 128].rearrange("p (t d) -> p t d", d=128),
                             r=[Sb], w=[maskT])
                    k.op("dve", "tensor_tensor", out=mkv[:, qb, qbg, :], in0=mkv[:, qb, qbg, :], in1=cb("tri"), op=ALU.mult,
                         r=[cbf], w=[maskT])
                for h in range(4):
                    si = 2 * h + 1
                    pairs = [(lambda kt: nks[:, kt * 128:(kt + 1) * 128], lambda a, b, h=h: nq[h][:, a:b])]
                    soft_sub(cm, g, pairs, lambda kt: vsv[:, kt, 0:129], 129, range(0, 4 * g + 4), far_skip(si, sel_mode),
                             ab_bias(si), lambda m, kt, qb: (mkv[:, qb, kt, :], [maskT]), fin_norm(cm, cm["on"][0]),
                             [nks, nq[h], nvs], nchunk=ab_chunks(si))
                    pairs = [(lambda kt: nkw[:, kt * 128:(kt + 1) * 128], lambda a, b, h=h: nq[h][:, a:b])]
                    soft_sub(cm, g, pairs, lambda kt: vwv[:, kt, 0:129], 129, range(max(0, 4 * g - 4), 4 * g + 4),
                             far_skip(si, win_mode), ab_bias(si), std_mask, fin_norm(cm, cm["on"][1]), [nkw, nq[h], nvw],
                             nchunk=ab_chunks(si))
                    onc = cm["on"][2]
                    for qb in range(4):
                        qbg = 4 * g + qb
                        gcol = lambda i: gts[:, qbg * 12 + 3 * h + i: qbg * 12 + 3 * h + i + 1]
                        dst = onc[:, qb * 128:(qb + 1) * 128]
                        k.op("dve", "tensor_scalar", out=dst, in0=ocv[:, qb, h, :], scalar1=gcol(0), scalar2=None, op0=ALU.mult,
                             r=[oc, gts], w=[onc])
                        k.op("dve", "scalar_tensor_tensor", out=dst, in0=cm["on"][0][:, qb * 128:(qb + 1) * 128], scalar=gcol(1),
                             in1=dst, op0=ALU.mult, op1=ALU.add, r=[cm["on"][0], gts], w=[onc])
                        k.op("dve", "scalar_tensor_tensor", out=dst, in0=cm["on"][1][:, qb * 128:(qb + 1) * 128], scalar=gcol(2),
                             in1=dst, op0=ALU.mult, op1=ALU.add, r=[cm["on"][1], gts], w=[onc])
                    finish_head(cm, 8 + h, g, onc, 1.0)
            k.flush()

    if "sb" in mixers:
        with ExitStack() as ms:
            cm = common(ms)
            aq = [k.sb(ms, f"sq{i}", S, BF16) for i in range(2)]
            ak = [k.sb(ms, f"sk{i}", S, BF16) for i in range(2)]
            av = [k.sb(ms, f"sv{i}", 32 * 132, BF16) for i in range(2)]
            Ef = [k.sb(ms, f"Ef{i}", 512, F32) for i in range(2)]
            Lb = [k.sb(ms, f"Lb{i}", 512, BF16) for i in range(3)]
            Lf = k.sb(ms, "Lf", 512, F32)
            Lsb = [k.sb(ms, f"Lsb{i}", 512, BF16) for i in range(3)]
            k.dma("sp", cm["gbc"][:], W["sb_o_norm"][l].partition_broadcast(128), key=cm["gbc"], w=[cm["gbc"]])
            step = [0]
            for h in range(4):
                q, kk, v = aq[h % 2], ak[h % 2], av[h % 2]
                k.dma("sp", q[:], QT[16 + h], key=q, w=[q])
                k.dma("sp", kk[:], KT[13 + h], key=kk, w=[kk])
                vv = load_v(v, VV[10 + h])
                for g in range(NG):
                    k.op("dve", "memset", Lf[:], 0.0, w=[Lf])
                    ktl = list(range(4 * g + 3, -1, -1))

                    def stage_a(idx, kt, g=g, q=q, kk=kk):
                        st = {}
                        lo = max(0, kt - 4 * g)
                        c0, c1 = lo * 128, 512
                        sidx = step[0]
                        step[0] += 1
                        Sb = cm["S"][sidx % 2]
                        E, L = Ef[sidx % 2], Lb[sidx % 3]
                        st.update(kt=kt, idx=idx, lo=lo, c0=c0, c1=c1, L=L, P=cm["pt"][sidx % 3],
                                  Ls_prev=Lsb[(sidx + 2) % 3], Ls_new=Lsb[sidx % 3])
                        lhs = kk[:, kt * 128:(kt + 1) * 128]
                        rhs = q[:, g * 512 + c0: g * 512 + c1]
                        st["lhs"], st["rhs"] = lhs, rhs
                        k.op("pe", "matmul", Sb[:, c0:c1], lhsT=lhs, rhs=rhs, start=True, stop=True, r=[kk, q], w=[Sb])
                        k.op("act", "activation", out=E[:, c0:c1], in_=Sb[:, c0:c1], func=AF.Exp, r=[Sb], w=[E])
                        k.op("act", "activation", out=L[:, c0:c1], in_=E[:, c0:c1], func=AF.Ln, bias=1.0, r=[E], w=[L])
                        if kt >= 4 * g:
                            k.op("dve", "tensor_tensor", out=L[:, c0:c0 + 128], in0=L[:, c0:c0 + 128], in1=cb("tris"), op=ALU.mult,
                                 r=[cbf], w=[L])
                        if kt > 0:
                            k.op("dve", "tensor_tensor", out=Lf[:, c0:c1], in0=Lf[:, c0:c1], in1=L[:, c0:c1], op=ALU.add,
                                 r=[L], w=[Lf])
                            k.op("dve", "tensor_copy", out=st["Ls_new"][:], in_=Lf[:], r=[Lf], w=[st["Ls_new"]])
                        return st

                    def stage_b(st, g=g, q=q, kk=kk, v=v, vv=vv):
                        kt, idx, lo, c0, c1, L, P = st["kt"], st["idx"], st["lo"], st["c0"], st["c1"], st["L"], st["P"]
                        Cb = cm["S"][2]
                        k.op("pe", "matmul", Cb[:, c0:c1], lhsT=cb("ntri"), rhs=L[:, c0:c1], start=True, stop=False,
                             r=[L, cbf], w=[Cb])
                        if idx > 0:
                            k.op("pe", "matmul", Cb[:, c0:c1], lhsT=cb("nones"), rhs=st["Ls_prev"][:, c0:c1], start=False, stop=False,
                                 r=[st["Ls_prev"], cbf], w=[Cb])
                        k.op("pe", "matmul", Cb[:, c0:c1], lhsT=st["lhs"], rhs=st["rhs"], start=False, stop=True, r=[kk, q], w=[Cb])
                        k.op("act", "activation", out=P[:, c0:c1], in_=Cb[:, c0:c1], func=AF.Exp, r=[Cb], w=[P])
                        if kt >= 4 * g:
                            k.op("dve", "tensor_tensor", out=P[:, c0:c0 + 128], in0=P[:, c0:c0 + 128], in1=cb("tris"), op=ALU.mult,
                                 r=[cbf], w=[P])
                        for qb in range(lo, 4):
                            k.op("pe", "matmul", cm["O"][qb][:, 0:128], lhsT=P[:, qb * 128:(qb + 1) * 128], rhs=vv[:, kt, 0:128],
                                 start=(kt == 4 * g + qb), stop=(kt == 0), r=[P, v], w=[cm["O"][qb]])

                    prev = None
                    for idx, kt in enumerate(ktl):
                        st = stage_a(idx, kt)
                        if prev is not None:
                            stage_b(prev)
                        prev = st
                    stage_b(prev)
                    onb = cm["on"][0]
                    for qb in range(4):
                        k.op("act", "copy", out=onb[:, qb * 128:(qb + 1) * 128], in_=cm["O"][qb][:, 0:128], r=[cm["O"][qb]], w=[onb])
                    finish_head(cm, 12 + h, g, onb, 1.0)
            k.flush()


_CACHE = {}


def _host_inputs(inputs, C):
    common = {}
    for nm in ["ffn1_w_gate", "ffn1_w_up", "ffn1_w_down", "ffn2_w_gate", "ffn2_w_up", "ffn2_w_down", "w_in", "w_out",
               "mla_w_uq", "mla_w_ukv", "nsa_w_ck", "nsa_w_cv", "nsa_pe_k", "nsa_pe_v", "ffn1_norm", "mix_norm",
               "ffn2_norm", "mla_o_norm", "diff_subln", "nsa_o_norm", "sb_o_norm", "diff_lq1", "diff_lk1", "diff_lq2",
               "diff_lk2"]:
        common[nm] = np.ascontiguousarray(np.asarray(inputs[nm], np.float32))
    common["gcols"] = np.stack([_gain_cols(inputs, l) for l in range(DEPTH)], axis=0).astype(np.float32)
    common["cbf"] = C["cbf"]
    common["cf32"] = C["cf32"]
    common["cosT"] = C["cosT"]
    common["sinT"] = C["sinT"]
    common["bonus"] = C["bonus"]
    return common


def kernel(**inputs):
    C = _consts()
    inputs = {kk: np.asarray(v) for kk, v in inputs.items()}
    _gain_cols(inputs, 0)
    nc = build_program(C)
    common = _host_inputs(inputs, C)
    x = np.asarray(inputs["x"], np.float32)
    in_maps = []
    for c in range(4):
        m = dict(common)
        m["x"] = np.ascontiguousarray(x[c])
        in_maps.append(m)
    res = run_bass_kernel_spmd(nc, in_maps, core_ids=list(range(4)))
    out = np.stack([res.results[c]["y"] for c in range(4)], axis=0)
    return out.astype(np.float32)
```

```python
import math
from contextlib import ExitStack

import numpy as np
import ml_dtypes

import concourse.bass as bass
import concourse.mybir as mybir
from concourse.bass_utils import run_bass_kernel_spmd

F32 = mybir.dt.float32
BF16 = mybir.dt.bfloat16
AF = mybir.ActivationFunctionType
ALU = mybir.AluOpType
AX = mybir.AxisListType

D = 2048
S = 4096
DEPTH = 2
DFF = 5632
NFC = DFF // 128
DIN = 5196
SL = 2048
NQB = SL // 128
T = 512
NTT = SL // T
EPS = 1e-6
NSLOPE = 8
SLOPES = [2.0 ** (-(i + 1)) for i in range(8)]
DIFF_SLOPES = SLOPES[0::2]
NSA_SLOPES = SLOPES[1::2]

DEBUG = None
ONLY = None


class Buf:
    __slots__ = ("t", "name", "lw", "rd", "sem", "ndma")

    def __init__(self, t, name):
        self.t = t
        self.name = name
        self.lw = None
        self.rd = []
        self.sem = None
        self.ndma = 0

    def __getitem__(self, key):
        return self.t[key]


class Op:
    __slots__ = ("eng", "dma", "calls", "deps", "inc", "count", "sem", "key")

    def __init__(self, eng, dma, calls, key=None):
        self.eng = eng
        self.dma = dma
        self.calls = calls
        self.deps = set()
        self.inc = False
        self.count = 0
        self.sem = None
        self.key = key


class KB:
    CENG = ("pe", "act", "dve", "pool")

    def __init__(self, nc, es):
        self.nc = nc
        self.es = es
        self.engs = {"pe": nc.tensor, "act": nc.scalar, "dve": nc.vector, "pool": nc.gpsimd, "sp": nc.sync}
        self.ops = []
        self.base = 0
        self.csem = {e: es.enter_context(nc.semaphore("cs_" + e)) for e in self.CENG}
        self.ccount = {e: 0 for e in self.CENG}
        self.seen = {e: {} for e in self.engs}
        self.keys = []
        self.bufs = []
        self.n_ins = 0

    def sb(self, es, name, cols, dtype):
        self.uid = getattr(self, "uid", 0) + 1
        name = f"{name}_{self.uid}"
        t = es.enter_context(self.nc.sbuf_tensor(name, [128, cols], dtype))
        b = Buf(t, name)
        self.bufs.append(b)
        return b

    def ps(self, es, name, cols, dtype=F32):
        self.uid = getattr(self, "uid", 0) + 1
        name = f"{name}_{self.uid}"
        t = es.enter_context(self.nc.psum_tensor(name, [128, cols], dtype))
        b = Buf(t, name)
        self.bufs.append(b)
        return b

    def _track(self, op, idx, r, w):
        for b in w:
            if b.lw is not None:
                op.deps.add(b.lw)
            op.deps.update(b.rd)
        for b in r:
            if b.lw is not None:
                op.deps.add(b.lw)
        for b in w:
            b.lw = idx
            b.rd = []
        for b in r:
            if b not in w:
                b.rd.append(idx)
                if len(b.rd) > 64:
                    keep = {}
                    rest = []
                    for i in b.rd:
                        o = self.ops[i]
                        if o.dma:
                            rest.append(i)
                        else:
                            keep[o.eng] = i
                    b.rd = rest + list(keep.values())

    def op(self, eng, method, *args, r=(), w=(), **kw):
        o = Op(eng, False, [(method, args, kw)])
        idx = len(self.ops)
        self.ops.append(o)
        self._track(o, idx, r, w)
        return idx

    def dma(self, eng, out, in_, key, r=(), w=()):
        o = Op(eng, True, [("dma_start", (), {"out": out, "in_": in_})], key=key)
        idx = len(self.ops)
        self.ops.append(o)
        self._track(o, idx, r, w)
        return idx

    def flush(self, final=False):
        nc = self.nc
        ops = self.ops
        n = len(ops)
        for i in range(self.base, n):
            o = ops[i]
            for d in o.deps:
                if d >= self.base:
                    od = ops[d]
                    if not (od.eng == "pe" and o.eng == "pe" and not od.dma and not o.dma):
                        od.inc = True
        last = {}
        for i in range(self.base, n):
            o = ops[i]
            if not o.dma:
                last[o.eng] = i
        for e, i in last.items():
            ops[i].inc = True
        for i in range(self.base, n):
            o = ops[i]
            if o.dma:
                kb = o.key
                if kb.sem is None:
                    pool = self.__dict__.setdefault("sempool", [])
                    if pool:
                        kb.sem, kb.ndma = pool.pop()
                    else:
                        kb.sem = self.es.enter_context(nc.semaphore("ds_" + kb.name))
                        kb.ndma = 0
                    self.keys.append(kb)
                kb.ndma += 1
                o.sem = kb.sem
                o.count = 16 * kb.ndma
            else:
                if o.inc:
                    self.ccount[o.eng] += 1
                o.sem = self.csem[o.eng]
                o.count = self.ccount[o.eng]
        for i in range(self.base, n):
            o = ops[i]
            E = self.engs[o.eng]
            seen = self.seen[o.eng]
            need = {}
            for d in o.deps:
                if d < self.base:
                    continue
                od = ops[d]
                if od.eng == "pe" and o.eng == "pe" and not od.dma and not o.dma:
                    continue
                if not od.dma and not od.inc:
                    raise RuntimeError("dep without inc")
                key = od.sem
                if need.get(key, (None, 0))[1] < od.count:
                    need[key] = (od.sem, od.count)
            for key, (sem, cnt) in need.items():
                if seen.get(key, 0) < cnt:
                    E.wait_ge(sem, cnt)
                    seen[key] = cnt
                    self.n_ins += 1
            ins = None
            for (m, a, kw) in o.calls:
                ins = getattr(E, m)(*a, **kw)
                self.n_ins += 1
            if o.dma:
                ins.then_inc(o.sem, 16)
            elif o.inc:
                ins.then_inc(o.sem, 1)
        targets = [(self.csem[e], self.ccount[e]) for e in self.CENG if self.ccount[e] > 0]
        targets += [(kb.sem, 16 * kb.ndma) for kb in self.keys]
        wait_engs = ["sp"] if final else list(self.engs.keys())
        for e in wait_engs:
            E = self.engs[e]
            seen = self.seen[e]
            for (sem, cnt) in targets:
                if seen.get(sem, 0) < cnt:
                    E.wait_ge(sem, cnt)
                    seen[sem] = cnt
                    self.n_ins += 1
        self.base = n
        pool = self.__dict__.setdefault("sempool", [])
        for kb in self.keys:
            pool.append((kb.sem, kb.ndma))
            kb.sem = None
        self.keys = []
        for b in self.bufs:
            b.lw = None
            b.rd = []


def _consts(r=0):
    c = {}
    p = np.arange(128)
    ident = np.eye(128, dtype=np.float32)
    ones = np.ones((128, 128), np.float32)
    zeros = np.zeros((128, 128), np.float32)
    blk64 = np.kron(np.eye(2), np.ones((64, 64))).astype(np.float32)
    tri = (p[:, None] <= p[None, :]).astype(np.float32)
    tris = (p[:, None] < p[None, :]).astype(np.float32)
    atri = (p[:, None] > p[None, :]).astype(np.float32)
    ntri_incl = -(p[:, None] >= p[None, :]).astype(np.float32)
    nones = -ones
    prot = np.zeros((128, 128), np.float32)
    for m in range(32):
        prot[m + 32, m] = -1.0
    for m in range(32, 64):
        prot[m - 32, m] = 1.0
    if r == 0:
        m0, m1, ms0, ms1 = tri, zeros, tris, zeros
        wm = {-4: atri, -3: ones, 0: tri, 1: zeros}
    else:
        m0, m1, ms0, ms1 = ones, tri, ones, tris
        wm = {-4: zeros, -3: atri, 0: ones, 1: tri}
    cm = []
    for (nt, j) in [(0, j) for j in range(9)] + [(1, j) for j in range(8, 16)]:
        G = 2 * j + r
        cm.append(((16 * p[:, None] + 31 - p[None, :]) <= 128 * G - 2048 * nt).astype(np.float32))
    cm = np.concatenate(cm, axis=1)
    n = np.arange(256)
    cst = 16 * n
    sel_start = 64 * np.arange(64)
    ov = ((cst[:, None] < sel_start[None, :] + 64) & (cst[:, None] + 32 > sel_start[None, :])).astype(np.float32)
    ov[255] = 0
    ov2 = ov.reshape(2, 128, 64).transpose(1, 0, 2)
    parts = [("ident", ident), ("ones", ones), ("blk64", blk64), ("tri", tri), ("tris", tris), ("atri", atri),
             ("ntri", ntri_incl), ("nones", nones), ("prot", prot), ("m0", m0), ("m1", m1), ("ms0", ms0), ("ms1", ms1),
             ("w-4", wm[-4]), ("w-3", wm[-3]), ("w0", wm[0]), ("w1", wm[1]), ("cm", cm), ("ov", ov2.reshape(128, -1))]
    c["cbf"] = np.concatenate([a for _, a in parts], axis=1).astype(ml_dtypes.bfloat16)
    off = {}
    o = 0
    for name, a in parts:
        off[name] = o
        o += a.shape[1]
    c["cbf_off"] = off
    c["cbf_w"] = o
    jl = np.arange(NQB)
    posl = ((2 * jl[:, None] + r) * 128 + p[None, :]).reshape(-1).astype(np.float32)
    inv_freq = (10000.0 ** (-np.arange(0, 64, 2, dtype=np.float32) / 64)).astype(np.float32)
    ang = posl[:, None] * inv_freq[None, :]
    c["cosT"] = np.ascontiguousarray(np.concatenate([np.cos(ang), np.cos(ang)], axis=1).T).astype(np.float32)
    c["sinT"] = np.ascontiguousarray(np.concatenate([np.sin(ang), np.sin(ang)], axis=1).T).astype(np.float32)
    e = np.arange(64) - 62
    base = 128.0 * (e[None, :] - r) + p[:, None] - 127.0
    ab = np.stack([sl * base for sl in SLOPES], axis=1)
    nt = np.arange(2)
    G = 2 * jl + r
    cbase = 16.0 * (128 * nt[None, :, None] + p[:, None, None]) + 31 - 128.0 * G[None, None, :] - 127.0
    cbias = np.stack([sl * cbase for sl in NSA_SLOPES], axis=1)
    cbias = np.minimum(cbias, 60.0)
    t = (128 * G[None, :] + p[:, None])
    jb = np.arange(64)
    cur = t // 64
    valid = sel_start[None, None, :] <= t[:, :, None]
    forced = (jb[None, None, :] == 0) | (jb[None, None, :] == cur[:, :, None]) | (jb[None, None, :] == cur[:, :, None] - 1)
    bonus = np.where(valid, np.where(forced, 1e4, 0.0), -1e30).astype(np.float32)
    cf = np.concatenate([ab.reshape(128, -1), cbias.reshape(128, -1)], axis=1)
    c["cf32"] = cf.astype(np.float32)
    c["bonus"] = bonus.reshape(128, -1).astype(np.float32)
    c["cf_off"] = {"ab": 0, "cbias": 512}
    c["cf_w"] = cf.shape[1]
    return c


GC = {}


def _gain_cols(inputs, l):
    cols = []

    def add(name, v):
        GC[name] = len(cols)
        cols.append(np.asarray(v, np.float32).reshape(128))

    for c in range(4):
        add(f"cq{c}", inputs["mla_cq_norm"][l][c * 128:(c + 1) * 128])
    for c in range(2):
        add(f"ckv{c}", inputs["mla_ckv_norm"][l][c * 128:(c + 1) * 128])
    add("qn", inputs["mla_qn_norm"][l])
    add("qr", np.tile(inputs["mla_qr_norm"][l], 2))
    add("kn", inputs["mla_kn_norm"][l])
    add("kr", np.tile(inputs["mla_kr_norm"][l], 2))
    add("dq", np.tile(inputs["diff_q_norm"][l], 2))
    add("dk", np.tile(inputs["diff_k_norm"][l], 2))
    add("nq", inputs["nsa_q_norm"][l])
    add("nkc", inputs["nsa_kc_norm"][l])
    add("nks", inputs["nsa_ks_norm"][l])
    add("nkw", inputs["nsa_kw_norm"][l])
    return np.stack(cols, axis=1)


NGC = 16


def build_program(C, stop_after=None, debug_out=False):
    nc = bass.Bass("TRN2", target_bir_lowering=False, num_devices=8)

    def din(name, shape, dt=F32):
        return nc.dram_tensor(name, list(shape), dt, kind="ExternalInput").ap()

    skind = "ExternalOutput" if debug_out else "Internal"

    def dscr(name, shape, dt):
        return nc.dram_tensor(name, list(shape), dt, kind=skind).ap()

    x = din("x", [SL, D])
    W = {}
    for nm, shp in [("ffn1_w_gate", [DEPTH, D, DFF]), ("ffn1_w_up", [DEPTH, D, DFF]), ("ffn1_w_down", [DEPTH, DFF, D]),
                    ("ffn2_w_gate", [DEPTH, D, DFF]), ("ffn2_w_up", [DEPTH, D, DFF]), ("ffn2_w_down", [DEPTH, DFF, D]),
                    ("w_in", [DEPTH, D, DIN]), ("w_out", [DEPTH, D, D]),
                    ("mla_w_uq", [DEPTH, 512, 768]), ("mla_w_ukv", [DEPTH, 256, 1024]),
                    ("nsa_w_ck", [DEPTH, 4096, 128]), ("nsa_w_cv", [DEPTH, 4096, 128]),
                    ("nsa_pe_k", [DEPTH, 32, 128]), ("nsa_pe_v", [DEPTH, 32, 128]),
                    ("ffn1_norm", [DEPTH, D]), ("mix_norm", [DEPTH, D]), ("ffn2_norm", [DEPTH, D]),
                    ("mla_o_norm", [DEPTH, 128]), ("diff_subln", [DEPTH, 128]), ("nsa_o_norm", [DEPTH, 128]),
                    ("sb_o_norm", [DEPTH, 128]),
                    ("diff_lq1", [DEPTH, 64]), ("diff_lk1", [DEPTH, 64]), ("diff_lq2", [DEPTH, 64]), ("diff_lk2", [DEPTH, 64]),
                    ("gcols", [DEPTH, 128, NGC])]:
        W[nm] = din(nm, shp)
    cbf_d = din("cbf", [128, C["cbf_w"]], BF16)
    cf_d = din("cf32", [128, C["cf_w"]])
    cos_d = din("cosT", [64, SL])
    sin_d = din("sinT", [64, SL])
    bonus_d = din("bonus", [128, NQB * 64])
    W["bonus"] = bonus_d
    y = nc.dram_tensor("y", [SL, D], F32, kind="ExternalOutput").ap()

    hS1 = dscr("hS1", [SL, D], F32)
    hS3 = dscr("hS3", [SL, D], F32)
    QT = dscr("QT", [20, 128, SL], BF16)
    KVs = nc.dram_tensor("KVs", [2 * 4096, SL], BF16, kind="Internal", addr_space="Shared").ap()
    pid = nc.partition_id()
    rk = pid % 2
    kv_mine = KVs[bass.ts(rk, 4096), :]

    KVloc = dscr("KVloc", [4096, SL], BF16)
    kv_dyn = [kv_mine[i * 512:(i + 1) * 512, :] for i in range(8)]

    class _KTW:
        def __getitem__(self, slot):
            return KVloc[slot * 128:(slot + 1) * 128, :]

    class _VVW:
        def __getitem__(self, slot):
            return KVloc[2304 + slot * 128:2304 + (slot + 1) * 128, :].rearrange("r (a d) -> (r a) d", d=128)

    KT = _KTW()
    VV = _VVW()

    def kt_read(rr, slot):
        return KVs[rr * 4096 + slot * 128: rr * 4096 + (slot + 1) * 128, :]

    def vv_read(rr, slot):
        return KVs[rr * 4096 + 2304 + slot * 128: rr * 4096 + 2304 + (slot + 1) * 128, :].rearrange("r (a d) -> (r a) d", d=128)

    GT = dscr("GT", [SL, 12], F32)
    OT = dscr("OT", [16, 128, SL], BF16)

    with ExitStack() as es:
        k = KB(nc, es)
        xsem = es.enter_context(nc.semaphore("xsem"))
        xcnt = [0]
        CO = C["cbf_off"]
        cbf = k.sb(es, "cbf", C["cbf_w"], BF16)
        cf = k.sb(es, "cf", C["cf_w"], F32)
        k.dma("sp", cbf[:], cbf_d[:, :], key=cbf, w=[cbf])
        k.dma("sp", cf[:], cf_d[:, :], key=cf, w=[cf])
        if DEBUG == "recompile":
            k.op("dve", "memset", cf[:, 0:1], 0.0, w=[cf])

        def cb(name, rows=128, cols=128, c0=0):
            o = CO[name] + c0
            return cbf[0:rows, o:o + cols]

        for l in range(DEPTH):
            lam_init = 0.8 - 0.6 * math.exp(-0.3 * l)
            src = x if l == 0 else hS3
            with ExitStack() as pa:
                phase_tokens(k, pa, nc, C, W, l, "A", src, hS1, cb, cbf, cf, cos_d, sin_d,
                             QT, KT, VV, GT, OT, lam_init)
                k.flush()
            for i in range(8):
                nc.sync.dma_start(out=kv_dyn[i], in_=KVloc[i * 512:(i + 1) * 512, :]).then_inc(xsem, 16)
            xcnt[0] += 8 * 16
            nc.sync.wait_ge(xsem, xcnt[0])
            nc.all_core_barrier()
            if stop_after == f"A{l}":
                break
            with ExitStack() as pb:
                phase_attn(k, pb, nc, C, W, l, cb, cbf, cf, QT, (kt_read, vv_read), None, GT, OT, lam_init, only=ONLY)
                k.flush()
            nc.all_core_barrier()
            if stop_after == f"B{l}":
                break
            dst = y if l == DEPTH - 1 else hS3
            with ExitStack() as pc:
                phase_tokens(k, pc, nc, C, W, l, "C", hS1, dst, cb, cbf, cf, cos_d, sin_d,
                             QT, KT, VV, GT, OT, lam_init)
                k.flush()
            if stop_after == f"C{l}":
                break
        k.flush(final=True)
        print("instructions:", k.n_ins, "ops:", len(k.ops), "dma sems:", len(k.keys))
    return nc


def phase_tokens(k, es, nc, C, W, l, which, src, dst, cb, cbf, cf, cos_d, sin_d, QT, KT, VV, GT, OT, lam_init):
    A = which == "A"
    htile = k.sb(es, "htile", 4 * D, F32)
    xnb = [k.sb(es, f"xnb{i}", D, BF16) for i in range(2)]
    xnT = k.sb(es, "xnT", 16 * T, BF16)
    actT = k.sb(es, "actT", NFC * T, BF16)
    WB = [k.sb(es, f"WB{i}", 8192, BF16) for i in range(2)]
    sg = [k.sb(es, f"sg{i}", T, F32) for i in range(2)]
    gbc = k.sb(es, "gbc", D, F32)
    ss = k.sb(es, "ss", 8, F32)
    PG = [k.ps(es, f"PG{i}", 512) for i in range(2)]
    PU = [k.ps(es, f"PU{i}", 512) for i in range(2)]
    PD = [k.ps(es, f"PD{i}", 512) for i in range(2)]
    PM = k.ps(es, "PM", 512)
    PT = k.ps(es, "PT", 1024, BF16)
    wbi = [0]

    def nextwb():
        b = WB[wbi[0] % 2]
        wbi[0] += 1
        return b

    fn = "ffn1" if A else "ffn2"
    if A:
        gc = k.sb(es, "gc", NGC, F32)
        gcs = k.sb(es, "gcs", NGC, F32)
        k.dma("sp", gc[:], W["gcols"][l], key=gc, w=[gc])
        for nm, sc in [("qn", 192 ** -0.5), ("qr", 192 ** -0.5), ("dq", 64 ** -0.5), ("nq", 128 ** -0.5)]:
            j = GC[nm]
            k.op("dve", "tensor_scalar", out=gcs[:, j:j + 1], in0=gc[:, j:j + 1], scalar1=float(sc), scalar2=None,
                 op0=ALU.mult, r=[gc], w=[gcs])
        wuq = k.sb(es, "wuq", 4 * 768, BF16)
        wukv = k.sb(es, "wukv", 2 * 1024, BF16)
        k.dma("pool", wuq[:].rearrange("p (c n) -> p c n", c=4),
              W["mla_w_uq"][l].rearrange("(c p) n -> p c n", p=128), key=wuq, w=[wuq])
        k.dma("pool", wukv[:].rearrange("p (c n) -> p c n", c=2),
              W["mla_w_ukv"][l].rearrange("(c p) n -> p c n", p=128), key=wukv, w=[wukv])
        cqf = k.sb(es, "cqf", 4 * T, BF16)
        cqn = k.sb(es, "cqn", 4 * T, BF16)
        ckvn = k.sb(es, "ckvn", 2 * T, BF16)
        sqb = [k.sb(es, f"sqb{i}", T, BF16) for i in range(2)]
        rstd = [k.sb(es, f"rstd{i}", T, F32) for i in range(2)]
        yf = k.sb(es, "yf", T, F32)
        ybf = [k.sb(es, f"ybf{i}", T, BF16) for i in range(3)]
        t1 = k.sb(es, "t1", T, F32)
        t2 = k.sb(es, "t2", T, F32)
        cosb = k.sb(es, "cosb", T, F32)
        sinb = k.sb(es, "sinb", T, F32)
        vout = [k.sb(es, f"vout{i}", 4 * 512, BF16) for i in range(2)]
        gout = k.sb(es, "gout", 4 * 12, F32)
        cnt = {"y": 0, "sq": 0, "v": 0}

    def rms_xnT(gname):
        gain_b = gbc
        k.dma("sp", gbc[:], W[gname][l].partition_broadcast(128), key=gbc, w=[gbc])
        for tb in range(4):
            sqj = xnb[(tb + 1) % 2]
            hb = htile[:, tb * D:(tb + 1) * D]
            k.op("dve", "memset", ss[:, tb:tb + 1], 0.0, w=[ss])
            k.op("act", "activation", out=sqj[:], in_=hb, func=AF.Square, accum_out=ss[:, tb:tb + 1],
                 r=[htile], w=[sqj, ss])
            k.op("act", "activation", out=ss[:, 4 + tb:5 + tb], in_=ss[:, tb:tb + 1], func=AF.Sqrt, scale=1.0 / D, bias=EPS,
                 r=[ss], w=[ss])
            k.op("dve", "reciprocal", out=ss[:, 4 + tb:5 + tb], in_=ss[:, 4 + tb:5 + tb], r=[ss], w=[ss])
            xb = xnb[tb % 2]
            k.op("dve", "scalar_tensor_tensor", out=xb[:], in0=hb, scalar=ss[:, 4 + tb:5 + tb], in1=gain_b[:],
                 op0=ALU.mult, op1=ALU.mult, r=[htile, ss, gain_b], w=[xb])
            for half in range(2):
                for c in range(8):
                    dc = half * 8 + c
                    k.op("pe", "transpose", out=PT[:, c * 128:(c + 1) * 128], in_=xb[:, dc * 128:(dc + 1) * 128],
                         identity=cb("ident"), r=[xb, cbf], w=[PT])
                dstv = xnT[:].rearrange("p (c t) -> p c t", c=16)[:, half * 8:(half + 1) * 8, tb * 128:(tb + 1) * 128]
                k.op("act", "copy", out=dstv, in_=PT[:].rearrange("p (c t) -> p c t", c=8), r=[PT], w=[xnT])

    def ffn(wg, wu, wd):
        xv = xnT[:].rearrange("p (c t) -> p c t", c=16)
        av = actT[:].rearrange("p (c t) -> p c t", c=NFC)
        pi = 0
        for fg in range(NFC // 2):
            wb = nextwb()
            wv = wb[:, 0:8192].rearrange("p (g c n) -> p g c n", g=2, c=16)
            k.dma("pool", wv[:, 0], wg[:, fg * 256:(fg + 1) * 256].rearrange("(c p) n -> p c n", p=128), key=wb, w=[wb])
            k.dma("pool", wv[:, 1], wu[:, fg * 256:(fg + 1) * 256].rearrange("(c p) n -> p c n", p=128), key=wb, w=[wb])
            for fc in range(2):
                pg, pu, sgb = PG[pi % 2], PU[pi % 2], sg[pi % 2]
                pi += 1
                for dc in range(16):
                    k.op("pe", "matmul", pg[:], lhsT=wv[:, 0, dc, fc * 128:(fc + 1) * 128], rhs=xv[:, dc, :],
                         start=(dc == 0), stop=(dc == 15), r=[wb, xnT], w=[pg])
                for dc in range(16):
                    k.op("pe", "matmul", pu[:], lhsT=wv[:, 1, dc, fc * 128:(fc + 1) * 128], rhs=xv[:, dc, :],
                         start=(dc == 0), stop=(dc == 15), r=[wb, xnT], w=[pu])
                k.op("act", "activation", out=sgb[:], in_=pg[:], func=AF.Silu, r=[pg], w=[sgb])
                k.op("dve", "tensor_tensor", out=av[:, fg * 2 + fc, :], in0=sgb[:], in1=pu[:], op=ALU.mult,
                     r=[sgb, pu], w=[actT])
        for cg in range(16):
            wb = nextwb()
            wv = wb[:, 0:NFC * 128].rearrange("p (c n) -> p c n", c=NFC)
            for hh in range(2):
                k.dma("pool", wv[:, hh * 22:(hh + 1) * 22, :],
                      wd[hh * 2816:(hh + 1) * 2816, cg * 128:(cg + 1) * 128].rearrange("(c p) n -> p c n", p=128),
                      key=wb, w=[wb])
            for tb in range(4):
                pd = PD[(cg * 4 + tb) % 2]
                for fc in range(NFC):
                    k.op("pe", "matmul", pd[:, 0:128], lhsT=av[:, fc, tb * 128:(tb + 1) * 128], rhs=wv[:, fc, :],
                         start=(fc == 0), stop=(fc == NFC - 1), r=[wb, actT], w=[pd])
                hv = htile[:, tb * D + cg * 128: tb * D + (cg + 1) * 128]
                k.op("dve", "scalar_tensor_tensor", out=hv, in0=pd[:, 0:128], scalar=0.5, in1=hv,
                     op0=ALU.mult, op1=ALU.add, r=[pd], w=[htile])

    def load_h(tt, srcap):
        k.dma("sp", htile[:].rearrange("p (b d) -> p b d", b=4),
              srcap[tt * T:(tt + 1) * T, :].rearrange("(b p) d -> p b d", p=128), key=htile, w=[htile])

    def store_h(tt, dstap):
        k.dma("sp", dstap[tt * T:(tt + 1) * T, :].rearrange("(b p) d -> p b d", p=128),
              htile[:].rearrange("p (b d) -> p b d", b=4), key=htile, r=[htile])

    def fm_finish(ps, n, tt, norm, gcol, scale, rope, dest, keep=None):
        yb = ybf[cnt["y"] % 3]
        cnt["y"] += 1
        if norm is None:
            k.op("act", "activation", out=yb[0:n, :], in_=ps[0:n, 0:T], func=AF.Copy, scale=float(scale), r=[ps], w=[yb])
        else:
            sq = sqb[cnt["sq"] % 2]
            rs = rstd[cnt["sq"] % 2]
            cnt["sq"] += 1
            k.op("act", "activation", out=sq[0:n, :], in_=ps[0:n, 0:T], func=AF.Square, r=[ps], w=[sq])
            onesm = cb("ones", n, n) if norm == 128 else cb("blk64", n, n)
            k.op("pe", "matmul", PM[0:n, 0:T], lhsT=onesm, rhs=sq[0:n, :], start=True, stop=True, r=[sq, cbf], w=[PM])
            k.op("act", "activation", out=rs[0:n, :], in_=PM[0:n, 0:T], func=AF.Sqrt, scale=1.0 / norm, bias=EPS,
                 r=[PM], w=[rs])
            k.op("dve", "reciprocal", out=rs[0:n, :], in_=rs[0:n, :], r=[rs], w=[rs])
            if not rope:
                k.op("dve", "scalar_tensor_tensor", out=yb[0:n, :], in0=ps[0:n, 0:T], scalar=gcol[0:n, :], in1=rs[0:n, :],
                     op0=ALU.mult, op1=ALU.mult, r=[ps, rs, gc, gcs], w=[yb])
            else:
                k.op("dve", "scalar_tensor_tensor", out=yf[0:n, :], in0=ps[0:n, 0:T], scalar=gcol[0:n, :], in1=rs[0:n, :],
                     op0=ALU.mult, op1=ALU.mult, r=[ps, rs, gc, gcs], w=[yf])
                yb2 = ybf[cnt["y"] % 3]
                cnt["y"] += 1
                k.op("act", "copy", out=yb2[0:n, :], in_=yf[0:n, :], r=[yf], w=[yb2])
                k.op("pe", "matmul", PM[0:n, 0:T], lhsT=cb("prot", n, n), rhs=yb2[0:n, :], start=True, stop=True,
                     r=[yb2, cbf], w=[PM])
                k.op("dve", "tensor_tensor", out=t1[0:n, :], in0=yf[0:n, :], in1=cosb[0:n, :], op=ALU.mult,
                     r=[yf, cosb], w=[t1])
                k.op("dve", "tensor_tensor", out=t2[0:n, :], in0=PM[0:n, 0:T], in1=sinb[0:n, :], op=ALU.mult,
                     r=[PM, sinb], w=[t2])
                k.op("dve", "tensor_tensor", out=yb[0:n, :], in0=t1[0:n, :], in1=t2[0:n, :], op=ALU.add,
                     r=[t1, t2], w=[yb])
        k.dma("sp", dest, yb[0:n, :], key=yb, r=[yb])

    def proj_mm_fm(ps, wv, nin, c0, n, inv):
        for c in range(nin):
            k.op("pe", "matmul", ps[0:n, 0:T], lhsT=wv[:, c, c0:c0 + n], rhs=inv[:, c, :], start=(c == 0),
                 stop=(c == nin - 1), r=list(proj_r), w=[ps])

    proj_r = []

    def load_win(c0, n):
        wb = nextwb()
        wv = wb[:, 0:16 * n].rearrange("p (c n) -> p c n", c=16)
        k.dma("pool", wv, W["w_in"][l][:, c0:c0 + n].rearrange("(c p) n -> p c n", p=128), key=wb, w=[wb])
        return wb, wv

    def tm_group(wb, wv, nin, c0, n, inv, tt, dest_list, sigmoid=False):
        vo = vout[cnt["v"] % 2]
        cnt["v"] += 1
        vov = vo[:].rearrange("p (b n) -> p b n", b=4)
        for tb in range(4):
            pd = PD[tb % 2]
            for c in range(nin):
                k.op("pe", "matmul", pd[:, 0:n], lhsT=inv[:, c, tb * 128:(tb + 1) * 128], rhs=wv[:, c, c0:c0 + n],
                     start=(c == 0), stop=(c == nin - 1), r=list(proj_r), w=[pd])
            if sigmoid:
                k.op("act", "activation", out=gout[:, tb * 12:(tb + 1) * 12], in_=pd[:, 0:n], func=AF.Sigmoid,
                     r=[pd], w=[gout])
            else:
                k.op("act", "copy", out=vov[:, tb, 0:n], in_=pd[:, 0:n], r=[pd], w=[vo])
        if sigmoid:
            k.dma("sp", GT[tt * T:(tt + 1) * T, :].rearrange("(b p) n -> p b n", p=128),
                  gout[:].rearrange("p (b n) -> p b n", b=4), key=gout, r=[gout])
        else:
            for (co, wd_, dap) in dest_list:
                for tb in range(4):
                    k.dma("sp", dap[tt * T + tb * 128: tt * T + (tb + 1) * 128, :], vov[:, tb, co:co + wd_], key=vo, r=[vo])

    def projections(tt):
        xv = xnT[:].rearrange("p (c t) -> p c t", c=16)
        tsl = slice(tt * T, (tt + 1) * T)
        k.dma("sp", cosb[0:64, :], cos_d[:, tsl], key=cosb, w=[cosb])
        k.dma("sp", sinb[0:64, :], sin_d[:, tsl], key=sinb, w=[sinb])
        gcol = lambda nm: gc[:, GC[nm]:GC[nm] + 1]
        gscol = lambda nm: gcs[:, GC[nm]:GC[nm] + 1]
        wb, wv = load_win(0, 512)
        proj_r[:] = [wb, xnT]
        cqv = cqf[:].rearrange("p (c t) -> p c t", c=4)
        for c in range(4):
            ps = PG[c % 2]
            proj_mm_fm(ps, wv, 16, c * 128, 128, xv)
            k.op("act", "copy", out=cqv[:, c, :], in_=ps[:, 0:T], r=[ps], w=[cqf])
            sq = sqb[c % 2]
            k.op("act", "activation", out=sq[:], in_=ps[:, 0:T], func=AF.Square, r=[ps], w=[sq])
            k.op("pe", "matmul", PM[:, 0:T], lhsT=cb("ones"), rhs=sq[:], start=(c == 0), stop=(c == 3),
                 r=[sq, cbf], w=[PM])
        rs = rstd[0]
        k.op("act", "activation", out=rs[:], in_=PM[:, 0:T], func=AF.Sqrt, scale=1.0 / 512, bias=EPS, r=[PM], w=[rs])
        k.op("dve", "reciprocal", out=rs[:], in_=rs[:], r=[rs], w=[rs])
        cqnv = cqn[:].rearrange("p (c t) -> p c t", c=4)
        for c in range(4):
            k.op("dve", "scalar_tensor_tensor", out=cqnv[:, c, :], in0=cqv[:, c, :], scalar=gcol(f"cq{c}"), in1=rs[:],
                 op0=ALU.mult, op1=ALU.mult, r=[cqf, rs, gc], w=[cqn])
        wb, wv = load_win(512, 256 + 64)
        proj_r[:] = [wb, xnT]
        for c in range(2):
            ps = PG[c % 2]
            proj_mm_fm(ps, wv, 16, c * 128, 128, xv)
            k.op("act", "copy", out=cqv[:, c, :], in_=ps[:, 0:T], r=[ps], w=[cqf])
            sq = sqb[c % 2]
            k.op("act", "activation", out=sq[:], in_=ps[:, 0:T], func=AF.Square, r=[ps], w=[sq])
            k.op("pe", "matmul", PM[:, 0:T], lhsT=cb("ones"), rhs=sq[:], start=(c == 0), stop=(c == 1),
                 r=[sq, cbf], w=[PM])
        rs = rstd[1]
        k.op("act", "activation", out=rs[:], in_=PM[:, 0:T], func=AF.Sqrt, scale=1.0 / 256, bias=EPS, r=[PM], w=[rs])
        k.op("dve", "reciprocal", out=rs[:], in_=rs[:], r=[rs], w=[rs])
        ckvv = ckvn[:].rearrange("p (c t) -> p c t", c=2)
        for c in range(2):
            k.op("dve", "scalar_tensor_tensor", out=ckvv[:, c, :], in0=cqv[:, c, :], scalar=gcol(f"ckv{c}"), in1=rs[:],
                 op0=ALU.mult, op1=ALU.mult, r=[cqf, rs, gc], w=[ckvn])
        ps = PU[0]
        proj_mm_fm(ps, wv, 16, 256, 64, xv)
        fm_finish(ps, 64, tt, 64, gcol("kr"), 1.0, True, KT[4][0:64, tsl])
        wuqv = wuq[:].rearrange("p (c n) -> p c n", c=4)
        wukvv = wukv[:].rearrange("p (c n) -> p c n", c=2)
        for h in range(4):
            proj_r[:] = [wuq, cqn]
            ps = PG[h % 2]
            proj_mm_fm(ps, wuqv, 4, h * 192, 128, cqnv)
            fm_finish(ps, 128, tt, 128, gscol("qn"), 1.0, False, QT[h][:, tsl])
            ps = PU[h % 2]
            proj_mm_fm(ps, wuqv, 4, h * 192 + 128, 64, cqnv)
            fm_finish(ps, 64, tt, 64, gscol("qr"), 1.0, True, QT[4 + h][0:64, tsl])
            proj_r[:] = [wukv, ckvn]
            ps = PG[(h + 1) % 2]
            proj_mm_fm(ps, wukvv, 2, h * 256, 128, ckvv)
            fm_finish(ps, 128, tt, 128, gcol("kn"), 1.0, False, KT[h][:, tsl])
        proj_r[:] = [wukv, ckvn]
        for h in range(4):
            tm_group(wukv, wukvv, 2, h * 256 + 128, 128, ckvv, tt, [(0, 128, VV[h])])
        wb, wv = load_win(832, 512)
        proj_r[:] = [wb, xnT]
        for h in range(4):
            ps = PG[h % 2]
            proj_mm_fm(ps, wv, 16, h * 128, 128, xv)
            fm_finish(ps, 128, tt, 64, gscol("dq"), 1.0, False, QT[8 + h][:, tsl])
        wb, wv = load_win(1344, 512)
        proj_r[:] = [wb, xnT]
        for h in range(4):
            ps = PU[h % 2]
            proj_mm_fm(ps, wv, 16, h * 128, 128, xv)
            fm_finish(ps, 128, tt, 64, gcol("dk"), 1.0, False, KT[5 + h][:, tsl])
        wb, wv = load_win(1856, 512)
        proj_r[:] = [wb, xnT]
        tm_group(wb, wv, 16, 0, 512, xv, tt, [(h * 128, 128, VV[4 + h]) for h in range(4)])
        wb, wv = load_win(2368, 512)
        proj_r[:] = [wb, xnT]
        for h in range(4):
            ps = PG[h % 2]
            proj_mm_fm(ps, wv, 16, h * 128, 128, xv)
            fm_finish(ps, 128, tt, 128, gscol("nq"), 1.0, False, QT[12 + h][:, tsl])
        wb, wv = load_win(2880, 512)
        proj_r[:] = [wb, xnT]
        ps = PU[0]
        proj_mm_fm(ps, wv, 16, 0, 128, xv)
        fm_finish(ps, 128, tt, None, None, 1.0, False, KT[9][:, tsl])
        ps = PU[1]
        proj_mm_fm(ps, wv, 16, 128, 128, xv)
        fm_finish(ps, 128, tt, None, None, 1.0, False, KT[10][:, tsl])
        ps = PU[0]
        proj_mm_fm(ps, wv, 16, 256, 128, xv)
        fm_finish(ps, 128, tt, 128, gcol("nks"), 1.0, False, KT[11][:, tsl])
        tm_group(wb, wv, 16, 384, 128, xv, tt, [(0, 128, VV[8])])
        wb, wv = load_win(3392, 268)
        proj_r[:] = [wb, xnT]
        ps = PU[1]
        proj_mm_fm(ps, wv, 16, 0, 128, xv)
        fm_finish(ps, 128, tt, 128, gcol("nkw"), 1.0, False, KT[12][:, tsl])
        tm_group(wb, wv, 16, 128, 128, xv, tt, [(0, 128, VV[9])])
        tm_group(wb, wv, 16, 256, 12, xv, tt, None, sigmoid=True)
        wb, wv = load_win(3660, 512)
        proj_r[:] = [wb, xnT]
        for h in range(4):
            ps = PG[h % 2]
            proj_mm_fm(ps, wv, 16, h * 128, 128, xv)
            fm_finish(ps, 128, tt, None, None, 128 ** -0.5, False, QT[16 + h][:, tsl])
        wb, wv = load_win(4172, 512)
        proj_r[:] = [wb, xnT]
        for h in range(4):
            ps = PU[h % 2]
            proj_mm_fm(ps, wv, 16, h * 128, 128, xv)
            fm_finish(ps, 128, tt, None, None, 1.0, False, KT[13 + h][:, tsl])
        wb, wv = load_win(4684, 512)
        proj_r[:] = [wb, xnT]
        tm_group(wb, wv, 16, 0, 512, xv, tt, [(h * 128, 128, VV[10 + h]) for h in range(4)])

    def wout_add(tt):
        ot = xnT
        ov = ot[:].rearrange("p (c t) -> p c t", c=16)
        k.dma("sp", ov, OT[:, :, tt * T:(tt + 1) * T].rearrange("c p t -> p c t"), key=ot, w=[ot])
        for cg in range(4):
            wb = nextwb()
            wv = wb[:, 0:8192].rearrange("p (c n) -> p c n", c=16)
            k.dma("pool", wv, W["w_out"][l][:, cg * 512:(cg + 1) * 512].rearrange("(c p) n -> p c n", p=128), key=wb, w=[wb])
            for tb in range(4):
                pd = PD[tb % 2]
                for c in range(16):
                    k.op("pe", "matmul", pd[:], lhsT=ov[:, c, tb * 128:(tb + 1) * 128], rhs=wv[:, c, :],
                         start=(c == 0), stop=(c == 15), r=[wb, ot], w=[pd])
                hv = htile[:, tb * D + cg * 512: tb * D + (cg + 1) * 512]
                k.op("dve", "tensor_tensor", out=hv, in0=pd[:], in1=hv, op=ALU.add, r=[pd], w=[htile])

    for tt in range(NTT):
        load_h(tt, src)
        if A:
            rms_xnT("ffn1_norm")
            ffn(W["ffn1_w_gate"][l], W["ffn1_w_up"][l], W["ffn1_w_down"][l])
            store_h(tt, dst)
            rms_xnT("mix_norm")
            projections(tt)
        else:
            wout_add(tt)
            rms_xnT("ffn2_norm")
            ffn(W["ffn2_w_gate"][l], W["ffn2_w_up"][l], W["ffn2_w_down"][l])
            store_h(tt, dst)


def phase_attn(k, es, nc, C, W, l, cb, cbf, cf, QT, KT, VV, GT, OT, lam_init, only=None):
    NG = SL // 512
    kt_read, vv_read = KT

    def common(ms):
        d = {}
        d["S"] = [k.ps(ms, f"S{i}", 512) for i in range(3)]
        d["O"] = [k.ps(ms, f"O{i}", 512) for i in range(4)]
        d["PT"] = k.ps(ms, "PTt", 1024, BF16)
        d["pt"] = [k.sb(ms, f"pt{i}", 512, BF16) for i in range(3)]
        d["on"] = [k.sb(ms, f"on{i}", 512, F32) for i in range(3)]
        d["sm"] = k.sb(ms, "sm", 16, F32)
        d["ob"] = [k.sb(ms, f"ob{i}", 128, BF16) for i in range(2)]
        d["ot"] = [k.sb(ms, f"ot{i}", 512, BF16) for i in range(2)]
        d["gbc"] = k.sb(ms, "gbco", 128, F32)
        d["junk"] = k.sb(ms, "junk", 128, F32)
        d["cnt"] = {"s": 0, "p": 0, "ob": 0, "ot": 0}
        return d

    def causal_mode(kt, j):
        e = kt - 2 * j
        if e > 1:
            return "skip"
        return "m1" if e == 1 else ("m0" if e == 0 else "full")

    def std_mask(m, kt, qb):
        return cb(m), [cbf]

    def soft_sub(cm, g, pairs, vfn, nv, ktlist, modefn, biasfn, maskfn, fin, preads, nchunk=4):
        ktlist = list(ktlist)
        valid = {qb: [kt for kt in ktlist if modefn(kt, 4 * g + qb) != "skip"] for qb in range(4)}
        steps = []
        for kt in ktlist:
            qbs = [qb for qb in range(4) if modefn(kt, 4 * g + qb) != "skip"]
            if qbs:
                steps.append((kt, qbs))

        def emit_qk(st):
            kt, qbs = st["kt"], st["qbs"]
            lo, hi = qbs[0], qbs[-1] + 1
            Sb = cm["S"][cm["cnt"]["s"] % 3]
            cm["cnt"]["s"] += 1
            P = cm["pt"][cm["cnt"]["p"] % 3]
            cm["cnt"]["p"] += 1
            st["S"], st["P"] = Sb, P
            for i, (lf, rf) in enumerate(pairs):
                k.op("pe", "matmul", Sb[:, lo * 128:hi * 128], lhsT=lf(kt), rhs=rf(g * 512 + lo * 128, g * 512 + hi * 128),
                     start=(i == 0), stop=(i == len(pairs) - 1), r=preads, w=[Sb])

        def emit_act(st):
            kt, qbs, Sb, P = st["kt"], st["qbs"], st["S"], st["P"]
            lo, hi = qbs[0], qbs[-1] + 1
            if biasfn is None:
                k.op("act", "activation", out=P[:, lo * 128:hi * 128], in_=Sb[:, lo * 128:hi * 128], func=AF.Exp,
                     r=[Sb], w=[P])
            else:
                cs = 4 // nchunk
                for c in range(nchunk):
                    a, b = max(lo, c * cs), min(hi, (c + 1) * cs)
                    if a >= b:
                        continue
                    k.op("act", "activation", out=P[:, a * 128:b * 128], in_=Sb[:, a * 128:b * 128],
                         func=AF.Exp, bias=biasfn(kt, 4 * g + (c + 1) * cs - 1), r=[Sb, cf], w=[P])
            for qb in qbs:
                m = modefn(kt, 4 * g + qb)
                if m != "full":
                    map_, mr = maskfn(m, kt, qb)
                    k.op("dve", "tensor_tensor", out=P[:, qb * 128:(qb + 1) * 128], in0=P[:, qb * 128:(qb + 1) * 128],
                         in1=map_, op=ALU.mult, r=mr, w=[P])

        def emit_pv(st):
            kt, qbs, P = st["kt"], st["qbs"], st["P"]
            for qb in qbs:
                k.op("pe", "matmul", cm["O"][qb][:, 0:nv], lhsT=P[:, qb * 128:(qb + 1) * 128], rhs=vfn(kt),
                     start=(kt == valid[qb][0]), stop=(kt == valid[qb][-1]), r=[P] + preads, w=[cm["O"][qb]])

        prev = None
        for (kt, qbs) in steps:
            st = {"kt": kt, "qbs": qbs}
            emit_qk(st)
            if prev is not None:
                emit_pv(prev)
            emit_act(st)
            prev = st
        if prev is not None:
            emit_pv(prev)
        for qb in range(4):
            fin(qb, cm["O"][qb])

    def fin_norm(cm, onb):
        def fin(qb, Ob):
            sm = cm["sm"]
            k.op("dve", "tensor_scalar", out=sm[:, qb:qb + 1], in0=Ob[:, 128:129], scalar1=1e-37, scalar2=None,
                 op0=ALU.max, r=[Ob], w=[sm])
            k.op("dve", "reciprocal", out=sm[:, qb:qb + 1], in_=sm[:, qb:qb + 1], r=[sm], w=[sm])
            k.op("dve", "tensor_scalar", out=onb[:, qb * 128:(qb + 1) * 128], in0=Ob[:, 0:128], scalar1=sm[:, qb:qb + 1],
                 scalar2=None, op0=ALU.mult, r=[Ob, sm], w=[onb])
        return fin

    def finish_head(cm, slot, g, onb, extra):
        sm = cm["sm"]
        otb = cm["ot"][cm["cnt"]["ot"] % 2]
        cm["cnt"]["ot"] += 1
        for qb in range(4):
            ov = onb[:, qb * 128:(qb + 1) * 128]
            k.op("dve", "memset", sm[:, 4 + qb:5 + qb], 0.0, w=[sm])
            k.op("act", "activation", out=cm["junk"][:], in_=ov, func=AF.Square, accum_out=sm[:, 4 + qb:5 + qb],
                 r=[onb], w=[cm["junk"], sm])
            k.op("act", "activation", out=sm[:, 8 + qb:9 + qb], in_=sm[:, 4 + qb:5 + qb], func=AF.Sqrt, scale=1.0 / 128,
                 bias=EPS, r=[sm], w=[sm])
            k.op("dve", "reciprocal", out=sm[:, 8 + qb:9 + qb], in_=sm[:, 8 + qb:9 + qb], r=[sm], w=[sm])
            if extra != 1.0:
                k.op("dve", "tensor_scalar", out=sm[:, 8 + qb:9 + qb], in0=sm[:, 8 + qb:9 + qb], scalar1=float(extra),
                     scalar2=None, op0=ALU.mult, r=[sm], w=[sm])
            ob = cm["ob"][cm["cnt"]["ob"] % 2]
            cm["cnt"]["ob"] += 1
            k.op("dve", "scalar_tensor_tensor", out=ob[:], in0=ov, scalar=sm[:, 8 + qb:9 + qb], in1=cm["gbc"][:],
                 op0=ALU.mult, op1=ALU.mult, r=[onb, sm, cm["gbc"]], w=[ob])
            k.op("pe", "transpose", out=cm["PT"][:, qb * 128:(qb + 1) * 128], in_=ob[:], identity=cb("ident"),
                 r=[ob, cbf], w=[cm["PT"]])
        k.op("act", "copy", out=otb[:], in_=cm["PT"][:, 0:512], r=[cm["PT"]], w=[otb])
        k.dma("sp", OT[slot][:, g * 512:(g + 1) * 512], otb[:], key=otb, r=[otb])

    def load_v(v, slot):
        vv = v[:].rearrange("p (t c) -> p t c", c=132)
        for rr in range(2):
            k.dma("sp", vv.rearrange("p (b two) c -> p b two c", two=2)[:, :, rr, 0:128],
                  vv_read(rr, slot).rearrange("(b p) d -> p b d", p=128), key=v, w=[v])
        return vv

    def load_k(buf, slot, rows=128):
        for rr in range(2):
            k.dma("sp", buf[0:rows, :].rearrange("p (b two t) -> p b two t", two=2, t=128)[:, :, rr, :],
                  kt_read(rr, slot)[0:rows, :].rearrange("p (b t) -> p b t", t=128), key=buf, w=[buf])

    def ab_bias(si):
        return lambda kt, j: cf[:, si * 64 + (kt - 2 * j + 62): si * 64 + (kt - 2 * j + 62) + 1]

    def ab_chunks(si):
        sl = SLOPES[si]
        return 4 if sl * 511 > 70 else (2 if sl * 1023 > 70 else 1)

    def far_skip(si, modefn):
        sl = SLOPES[si]

        def f(kt, j):
            rel = kt - 2 * j
            if rel < 0 and sl * 128.0 * (-rel - 1) >= 110.0:
                return "skip"
            return modefn(kt, j)
        return f

    mixers = only if only is not None else ("mla", "diff", "nsa", "sb")

    if "mla" in mixers:
        with ExitStack() as ms:
            cm = common(ms)
            aq = [k.sb(ms, f"aq{i}", SL, BF16) for i in range(2)]
            aqr = [k.sb(ms, f"aqr{i}", SL, BF16) for i in range(2)]
            ak = [k.sb(ms, f"ak{i}", S, BF16) for i in range(2)]
            akr = k.sb(ms, "akr", S, BF16)
            av = [k.sb(ms, f"av{i}", 32 * 132, BF16) for i in range(2)]
            k.dma("sp", cm["gbc"][:], W["mla_o_norm"][l].partition_broadcast(128), key=cm["gbc"], w=[cm["gbc"]])
            load_k(akr, 4, 64)
            for i in range(2):
                k.op("dve", "memset", av[i][:], 1.0, w=[av[i]])
            for h in range(4):
                q, qr, kk, v = aq[h % 2], aqr[h % 2], ak[h % 2], av[h % 2]
                k.dma("sp", q[:], QT[h], key=q, w=[q])
                k.dma("sp", qr[0:64, :], QT[4 + h][0:64, :], key=qr, w=[qr])
                load_k(kk, h)
                vv = load_v(v, h)
                pairs = [(lambda kt, kk=kk: kk[:, kt * 128:(kt + 1) * 128], lambda a, b, q=q: q[:, a:b]),
                         (lambda kt: akr[0:64, kt * 128:(kt + 1) * 128], lambda a, b, qr=qr: qr[0:64, a:b])]
                for g in range(NG):
                    onb = cm["on"][0]
                    soft_sub(cm, g, pairs, lambda kt, vv=vv: vv[:, kt, 0:129], 129, range(0, 8 * g + 8), causal_mode,
                             None, std_mask, fin_norm(cm, onb), [q, qr, kk, akr, v])
                    finish_head(cm, h, g, onb, 1.0)
            k.flush()

    if "diff" in mixers:
        with ExitStack() as ms:
            cm = common(ms)
            aq = [k.sb(ms, f"dq{i}", SL, BF16) for i in range(2)]
            ak = [k.sb(ms, f"dk{i}", S, BF16) for i in range(2)]
            av = [k.sb(ms, f"dv{i}", 32 * 132, BF16) for i in range(2)]
            lt = k.sb(ms, "lt", 256, F32)
            ltmp = k.sb(ms, "ltmp", 64, F32)
            ls = k.sb(ms, "ls", 8, F32)
            k.dma("sp", cm["gbc"][:], W["diff_subln"][l].partition_broadcast(128), key=cm["gbc"], w=[cm["gbc"]])
            for i, nm in enumerate(["diff_lq1", "diff_lk1", "diff_lq2", "diff_lk2"]):
                k.dma("sp", lt[:, i * 64:(i + 1) * 64], W[nm][l].partition_broadcast(128), key=lt, w=[lt])
            for j in range(2):
                k.op("dve", "tensor_tensor", out=ltmp[:], in0=lt[:, j * 128:j * 128 + 64], in1=lt[:, j * 128 + 64:j * 128 + 128],
                     op=ALU.mult, r=[lt], w=[ltmp])
                k.op("dve", "reduce_sum", out=ls[:, j:j + 1], in_=ltmp[:], axis=AX.X, r=[ltmp], w=[ls])
                k.op("act", "activation", out=ls[:, 2 + j:3 + j], in_=ls[:, j:j + 1], func=AF.Exp, r=[ls], w=[ls])
            k.op("dve", "tensor_tensor", out=ls[:, 4:5], in0=ls[:, 3:4], in1=ls[:, 2:3], op=ALU.subtract, r=[ls], w=[ls])
            k.op("dve", "tensor_scalar", out=ls[:, 5:6], in0=ls[:, 4:5], scalar1=float(-lam_init), scalar2=None, op0=ALU.add,
                 r=[ls], w=[ls])
            for i in range(2):
                k.op("dve", "memset", av[i][:], 1.0, w=[av[i]])
            for h in range(4):
                q, kk, v = aq[h % 2], ak[h % 2], av[h % 2]
                k.dma("sp", q[:], QT[8 + h], key=q, w=[q])
                load_k(kk, 5 + h)
                vv = load_v(v, 4 + h)
                for g in range(NG):
                    for sub in range(2):
                        r0, r1 = sub * 64, (sub + 1) * 64
                        pairs = [(lambda kt, kk=kk, r0=r0, r1=r1: kk[r0:r1, kt * 128:(kt + 1) * 128],
                                  lambda a, b, q=q, r0=r0, r1=r1: q[r0:r1, a:b])]
                        soft_sub(cm, g, pairs, lambda kt, vv=vv: vv[:, kt, 0:129], 129, range(0, 8 * g + 8),
                                 far_skip(2 * h, causal_mode), ab_bias(2 * h), std_mask, fin_norm(cm, cm["on"][sub]),
                                 [q, kk, v], nchunk=ab_chunks(2 * h))
                    on0, on1 = cm["on"][0], cm["on"][1]
                    k.op("dve", "scalar_tensor_tensor", out=on0[:], in0=on1[:], scalar=ls[:, 5:6], in1=on0[:],
                         op0=ALU.mult, op1=ALU.add, r=[on1, ls], w=[on0])
                    finish_head(cm, 4 + h, g, on0, 1.0 - lam_init)
            k.flush()

    if "nsa" in mixers:
        with ExitStack() as ms:
            cm = common(ms)
            nq = [k.sb(ms, f"nq{i}", SL, BF16) for i in range(4)]
            nks = k.sb(ms, "nks", S, BF16)
            nkw = k.sb(ms, "nkw", S, BF16)
            nvs = k.sb(ms, "nvs", 32 * 132, BF16)
            nvw = k.sb(ms, "nvw", 32 * 132, BF16)
            raw = [k.sb(ms, f"raw{i}", S, BF16) for i in range(2)]
            wc = [k.sb(ms, f"wc{i}", 32 * 128, BF16) for i in range(2)]
            kcT = k.sb(ms, "kcT", 256, BF16)
            vcx = k.sb(ms, "vcx", 2 * 196, BF16)
            gts = k.sb(ms, "gts", NQB * 12, F32)
            bonus = k.sb(ms, "bonus", NQB * 64, F32)
            maskT = k.sb(ms, "maskT", 4 * 32 * 128, BF16)
            selexp = k.sb(ms, "selexp", 4096, BF16)
            selb = k.sb(ms, "selb", 64, BF16)
            scb = k.sb(ms, "scb", 64, F32)
            sc2 = k.sb(ms, "sc2", 64, F32)
            imp = [k.sb(ms, f"imp{i}", 64, F32) for i in range(4)]
            m8 = k.sb(ms, "m8", 16, F32)
            oc = k.sb(ms, "oc", 4 * 4 * 128, F32)
            pe_f = k.sb(ms, "pe_f", 128, F32)
            pe_b = k.sb(ms, "pe_b", 128, BF16)
            peT = k.sb(ms, "peT", 32, BF16)
            crow = k.sb(ms, "crow", 128, BF16)
            kcn = k.sb(ms, "kcn", 128, BF16)
            gcn = k.sb(ms, "gcn", NGC, F32)
            sm = cm["sm"]
            k.dma("sp", cm["gbc"][:], W["nsa_o_norm"][l].partition_broadcast(128), key=cm["gbc"], w=[cm["gbc"]])
            k.dma("sp", gcn[:], W["gcols"][l], key=gcn, w=[gcn])
            k.dma("sp", bonus[:], W["bonus"], key=bonus, w=[bonus])
            k.dma("sp", gts[:].rearrange("p (b n) -> p b n", n=12), GT.rearrange("(b p) n -> p b n", p=128), key=gts, w=[gts])
            for h in range(4):
                k.dma("sp", nq[h][:], QT[12 + h], key=nq[h], w=[nq[h]])
            load_k(nks, 11)
            load_k(nkw, 12)
            k.op("dve", "memset", nvs[:], 1.0, w=[nvs])
            k.op("dve", "memset", nvw[:], 1.0, w=[nvw])
            vsv = load_v(nvs, 8)
            vwv = load_v(nvw, 9)
            load_k(raw[0], 9)
            load_k(raw[1], 10)
            k.dma("pool", wc[0][:].rearrange("p (l n) -> p l n", n=128), W["nsa_w_ck"][l].rearrange("(l p) n -> p l n", p=128),
                  key=wc[0], w=[wc[0]])
            k.dma("pool", wc[1][:].rearrange("p (l n) -> p l n", n=128), W["nsa_w_cv"][l].rearrange("(l p) n -> p l n", p=128),
                  key=wc[1], w=[wc[1]])
            k.op("dve", "memset", kcT[:], 0.0, w=[kcT])
            k.op("dve", "memset", vcx[:], 0.0, w=[vcx])
            vcv = vcx[:].rearrange("p (t c) -> p t c", c=196)
            k.op("dve", "memset", vcv[:, :, 128:129], 1.0, w=[vcx])
            k.op("dve", "tensor_copy", out=vcv[:, :, 129:193], in_=cb("ov").rearrange("p (t c) -> p t c", c=64), r=[cbf], w=[vcx])
            for which in range(2):
                pen = "nsa_pe_k" if which == 0 else "nsa_pe_v"
                wv = wc[which][:].rearrange("p (l n) -> p l n", n=128)
                k.dma("sp", pe_f[0:32, :], W[pen][l], key=pe_f, w=[pe_f])
                k.op("act", "copy", out=pe_b[0:32, :], in_=pe_f[0:32, :], r=[pe_f], w=[pe_b])
                k.op("pe", "transpose", out=cm["PT"][:, 0:32], in_=pe_b[0:32, :], identity=cb("ident", 32, 32),
                     r=[pe_b, cbf], w=[cm["PT"]])
                k.op("act", "copy", out=peT[:], in_=cm["PT"][:, 0:32], r=[cm["PT"]], w=[peT])
                Oc = cm["O"][0]
                for li in range(32):
                    k.op("pe", "matmul", Oc[0:1, 0:128], lhsT=peT[:, li:li + 1], rhs=wv[:, li, :], start=(li == 0),
                         stop=(li == 31), r=[peT, wc[which]], w=[Oc])
                k.op("act", "copy", out=crow[0:1, :], in_=Oc[0:1, 0:128], r=[Oc], w=[crow])
                for nt in range(2):
                    nr = 128 if nt == 0 else 127
                    Ok = cm["O"][1 + nt]
                    for li in range(32):
                        a0 = nt * 2048 + li
                        k.op("pe", "matmul", Ok[0:nr, 0:128], lhsT=raw[which][:, a0:a0 + 16 * (nr - 1) + 1:16], rhs=wv[:, li, :],
                             start=(li == 0), stop=False, r=[raw[which], wc[which]], w=[Ok])
                    k.op("pe", "matmul", Ok[0:nr, 0:128], lhsT=cb("ones", 1, nr), rhs=crow[0:1, :], start=False, stop=True,
                         r=[crow, cbf], w=[Ok])
                    if which == 0:
                        k.op("dve", "memset", sm[:, 0:1], 0.0, w=[sm])
                        k.op("act", "activation", out=cm["junk"][0:nr, :], in_=Ok[0:nr, 0:128], func=AF.Square,
                             accum_out=sm[0:nr, 0:1], r=[Ok], w=[cm["junk"], sm])
                        k.op("act", "activation", out=sm[0:nr, 1:2], in_=sm[0:nr, 0:1], func=AF.Sqrt, scale=1.0 / 128, bias=EPS,
                             r=[sm], w=[sm])
                        k.op("dve", "reciprocal", out=sm[0:nr, 1:2], in_=sm[0:nr, 1:2], r=[sm], w=[sm])
                        k.op("dve", "memset", kcn[:], 0.0, w=[kcn])
                        k.op("dve", "tensor_scalar", out=kcn[0:nr, :], in0=Ok[0:nr, 0:128], scalar1=sm[0:nr, 1:2], scalar2=None,
                             op0=ALU.mult, r=[Ok, sm], w=[kcn])
                        k.op("pe", "transpose", out=cm["PT"][:, 128:256], in_=kcn[:], identity=cb("ident"), r=[kcn, cbf],
                             w=[cm["PT"]])
                        k.op("dve", "tensor_scalar", out=kcT[:, nt * 128:(nt + 1) * 128], in0=cm["PT"][:, 128:256],
                             scalar1=gcn[:, GC["nkc"]:GC["nkc"] + 1], scalar2=None, op0=ALU.mult, r=[cm["PT"], gcn], w=[kcT])
                    else:
                        k.op("act", "copy", out=vcv[0:nr, nt, 0:128], in_=Ok[0:nr, 0:128], r=[Ok], w=[vcx])
            ocv = oc[:].rearrange("p (q h d) -> p q h d", q=4, h=4)
            mkv = maskT[:].rearrange("p (q t d) -> p q t d", q=4, t=32)

            def cmp_mode(nt, j):
                if nt == 0:
                    return "full" if j >= 9 else ("cm", j)
                return "skip" if j < 8 else ("cm", 9 + j - 8)

            def cmp_mask(m, kt, qb):
                return cb("cm", 128, 128, m[1] * 128), [cbf]

            def sel_mode(kt, j):
                return "skip" if kt - 2 * j > 1 else "m"

            def win_mode(kt, j):
                e = kt - 2 * j
                if e > 1 or e < -4:
                    return "skip"
                if e in (-2, -1):
                    return "full"
                return f"w{e}"

            for g in range(NG):
                for h in range(4):
                    def fin_cmp(qb, Ob, h=h):
                        k.op("dve", "tensor_scalar", out=sm[:, qb:qb + 1], in0=Ob[:, 128:129], scalar1=1e-37, scalar2=None,
                             op0=ALU.max, r=[Ob], w=[sm])
                        k.op("dve", "reciprocal", out=sm[:, qb:qb + 1], in_=sm[:, qb:qb + 1], r=[sm], w=[sm])
                        k.op("dve", "tensor_scalar", out=ocv[:, qb, h, :], in0=Ob[:, 0:128], scalar1=sm[:, qb:qb + 1],
                             scalar2=None, op0=ALU.mult, r=[Ob, sm], w=[oc])
                        if h == 0:
                            k.op("dve", "tensor_scalar", out=imp[qb][:], in0=Ob[:, 129:193], scalar1=sm[:, qb:qb + 1],
                                 scalar2=None, op0=ALU.mult, r=[Ob, sm], w=[imp[qb]])
                        else:
                            k.op("dve", "scalar_tensor_tensor", out=imp[qb][:], in0=Ob[:, 129:193], scalar=sm[:, qb:qb + 1],
                                 in1=imp[qb][:], op0=ALU.mult, op1=ALU.add, r=[Ob, sm], w=[imp[qb]])
                    pairs = [(lambda nt: kcT[:, nt * 128:(nt + 1) * 128], lambda a, b, h=h: nq[h][:, a:b])]
                    cbias = lambda nt, j, h=h: cf[:, 512 + h * 32 + nt * 16 + j: 512 + h * 32 + nt * 16 + j + 1]
                    soft_sub(cm, g, pairs, lambda nt: vcv[:, nt, 0:193], 193, range(2), cmp_mode, cbias, cmp_mask, fin_cmp,
                             [kcT, nq[h], vcx])
                for qb in range(4):
                    qbg = 4 * g + qb
                    k.op("dve", "tensor_tensor", out=scb[:], in0=imp[qb][:], in1=bonus[:, qbg * 64:(qbg + 1) * 64], op=ALU.add,
                         r=[imp[qb], bonus], w=[scb])
                    k.op("dve", "max", out=m8[:, 0:8], in_=scb[:], r=[scb], w=[m8])
                    k.op("dve", "match_replace", out=sc2[:], in_to_replace=m8[:, 0:8], in_values=scb[:], imm_value=-3.0e38,
                         r=[m8, scb], w=[sc2])
                    k.op("dve", "max", out=m8[:, 8:16], in_=sc2[:], r=[sc2], w=[m8])
                    k.op("dve", "tensor_scalar", out=sm[:, 12:13], in0=m8[:, 15:16], scalar1=-1e29, scalar2=None, op0=ALU.max,
                         r=[m8], w=[sm])
                    k.op("dve", "tensor_scalar", out=selb[:], in0=scb[:], scalar1=sm[:, 12:13], scalar2=None, op0=ALU.is_ge,
                         r=[scb, sm], w=[selb])
                    k.op("dve", "tensor_copy", out=selexp[:].rearrange("p (j s) -> p j s", s=64),
                         in_=selb[:, 0:64].unsqueeze(2).to_broadcast([128, 64, 64]), r=[selb], w=[selexp])
                    for kt0 in range(0, 2 * qbg + 2, 4):
                        n = min(4, 2 * qbg + 2 - kt0)
                        Sb = cm["S"][cm["cnt"]["s"] % 3]
                        cm["cnt"]["s"] += 1
                        for j in range(n):
                            k.op("pe", "matmul", Sb[:, j * 128:(j + 1) * 128], lhsT=selexp[:, (kt0 + j) * 128:(kt0 + j + 1) * 128],
                                 rhs=cb("ident"), start=True, stop=True, r=[selexp, cbf], w=[Sb])
                        k.op("act", "copy", out=mkv[:, qb, kt0:kt0 + n, :], in_=Sb[:, 0:n * 128].rearrange("p (t d) -> p t d", d=128),
                             r=[Sb], w=[maskT])
                    k.op("dve", "tensor_tensor", out=mkv[:, qb, 2 * qbg, :], in0=mkv[:, qb, 2 * qbg, :], in1=cb("m0"), op=ALU.mult,
                         r=[cbf], w=[maskT])
                    k.op("dve", "tensor_tensor", out=mkv[:, qb, 2 * qbg + 1, :], in0=mkv[:, qb, 2 * qbg + 1, :], in1=cb("m1"),
                         op=ALU.mult, r=[cbf], w=[maskT])
                for h in range(4):
                    si = 2 * h + 1
                    pairs = [(lambda kt: nks[:, kt * 128:(kt + 1) * 128], lambda a, b, h=h: nq[h][:, a:b])]
                    soft_sub(cm, g, pairs, lambda kt: vsv[:, kt, 0:129], 129, range(0, 8 * g + 8), far_skip(si, sel_mode),
                             ab_bias(si), lambda m, kt, qb: (mkv[:, qb, kt, :], [maskT]), fin_norm(cm, cm["on"][0]),
                             [nks, nq[h], nvs], nchunk=ab_chunks(si))
                    pairs = [(lambda kt: nkw[:, kt * 128:(kt + 1) * 128], lambda a, b, h=h: nq[h][:, a:b])]
                    soft_sub(cm, g, pairs, lambda kt: vwv[:, kt, 0:129], 129, range(max(0, 8 * g - 4), 8 * g + 8),
                             far_skip(si, win_mode), ab_bias(si), std_mask, fin_norm(cm, cm["on"][1]), [nkw, nq[h], nvw],
                             nchunk=ab_chunks(si))
                    onc = cm["on"][2]
                    for qb in range(4):
                        qbg = 4 * g + qb
                        gcol = lambda i: gts[:, qbg * 12 + 3 * h + i: qbg * 12 + 3 * h + i + 1]
                        dst = onc[:, qb * 128:(qb + 1) * 128]
                        k.op("dve", "tensor_scalar", out=dst, in0=ocv[:, qb, h, :], scalar1=gcol(0), scalar2=None, op0=ALU.mult,
                             r=[oc, gts], w=[onc])
                        k.op("dve", "scalar_tensor_tensor", out=dst, in0=cm["on"][0][:, qb * 128:(qb + 1) * 128], scalar=gcol(1),
                             in1=dst, op0=ALU.mult, op1=ALU.add, r=[cm["on"][0], gts], w=[onc])
                        k.op("dve", "scalar_tensor_tensor", out=dst, in0=cm["on"][1][:, qb * 128:(qb + 1) * 128], scalar=gcol(2),
                             in1=dst, op0=ALU.mult, op1=ALU.add, r=[cm["on"][1], gts], w=[onc])
                    finish_head(cm, 8 + h, g, onc, 1.0)
            k.flush()

    if "sb" in mixers:
        with ExitStack() as ms:
            cm = common(ms)
            aq = [k.sb(ms, f"sq{i}", SL, BF16) for i in range(2)]
            ak = [k.sb(ms, f"sk{i}", S, BF16) for i in range(2)]
            av = [k.sb(ms, f"sv{i}", 32 * 132, BF16) for i in range(2)]
            Ef = [k.sb(ms, f"Ef{i}", 512, F32) for i in range(2)]
            Lb = [k.sb(ms, f"Lb{i}", 512, BF16) for i in range(3)]
            Lf = k.sb(ms, "Lf", 512, F32)
            Lsb = [k.sb(ms, f"Lsb{i}", 512, BF16) for i in range(3)]
            k.dma("sp", cm["gbc"][:], W["sb_o_norm"][l].partition_broadcast(128), key=cm["gbc"], w=[cm["gbc"]])
            step = [0]
            for h in range(4):
                q, kk, v = aq[h % 2], ak[h % 2], av[h % 2]
                k.dma("sp", q[:], QT[16 + h], key=q, w=[q])
                load_k(kk, 13 + h)
                vv = load_v(v, 10 + h)
                for g in range(NG):
                    k.op("dve", "memset", Lf[:], 0.0, w=[Lf])
                    ktl = list(range(8 * g + 7, -1, -1))

                    def stage_a(idx, kt, g=g, q=q, kk=kk):
                        st = {}
                        lo = max(0, (kt - 8 * g) // 2)
                        c0, c1 = lo * 128, 512
                        sidx = step[0]
                        step[0] += 1
                        Sb = cm["S"][sidx % 2]
                        E, L = Ef[sidx % 2], Lb[sidx % 3]
                        st.update(kt=kt, idx=idx, lo=lo, c0=c0, c1=c1, L=L, P=cm["pt"][sidx % 3],
                                  Ls_prev=Lsb[(sidx + 2) % 3], Ls_new=Lsb[sidx % 3])
                        lhs = kk[:, kt * 128:(kt + 1) * 128]
                        rhs = q[:, g * 512 + c0: g * 512 + c1]
                        st["lhs"], st["rhs"] = lhs, rhs
                        k.op("pe", "matmul", Sb[:, c0:c1], lhsT=lhs, rhs=rhs, start=True, stop=True, r=[kk, q], w=[Sb])
                        k.op("act", "activation", out=E[:, c0:c1], in_=Sb[:, c0:c1], func=AF.Exp, r=[Sb], w=[E])
                        k.op("act", "activation", out=L[:, c0:c1], in_=E[:, c0:c1], func=AF.Ln, bias=1.0, r=[E], w=[L])
                        if kt >= 8 * g:
                            k.op("dve", "tensor_tensor", out=L[:, c0:c0 + 128], in0=L[:, c0:c0 + 128],
                                 in1=cb("ms0" if kt % 2 == 0 else "ms1"), op=ALU.mult, r=[cbf], w=[L])
                        if kt > 0:
                            k.op("dve", "tensor_tensor", out=Lf[:, c0:c1], in0=Lf[:, c0:c1], in1=L[:, c0:c1], op=ALU.add,
                                 r=[L], w=[Lf])
                            k.op("dve", "tensor_copy", out=st["Ls_new"][:], in_=Lf[:], r=[Lf], w=[st["Ls_new"]])
                        return st

                    def stage_b(st, g=g, q=q, kk=kk, v=v, vv=vv):
                        kt, idx, lo, c0, c1, L, P = st["kt"], st["idx"], st["lo"], st["c0"], st["c1"], st["L"], st["P"]
                        Cb = cm["S"][2]
                        k.op("pe", "matmul", Cb[:, c0:c1], lhsT=cb("ntri"), rhs=L[:, c0:c1], start=True, stop=False,
                             r=[L, cbf], w=[Cb])
                        if idx > 0:
                            k.op("pe", "matmul", Cb[:, c0:c1], lhsT=cb("nones"), rhs=st["Ls_prev"][:, c0:c1], start=False, stop=False,
                                 r=[st["Ls_prev"], cbf], w=[Cb])
                        k.op("pe", "matmul", Cb[:, c0:c1], lhsT=st["lhs"], rhs=st["rhs"], start=False, stop=True, r=[kk, q], w=[Cb])
                        k.op("act", "activation", out=P[:, c0:c1], in_=Cb[:, c0:c1], func=AF.Exp, r=[Cb], w=[P])
                        if kt >= 8 * g:
                            k.op("dve", "tensor_tensor", out=P[:, c0:c0 + 128], in0=P[:, c0:c0 + 128],
                                 in1=cb("ms0" if kt % 2 == 0 else "ms1"), op=ALU.mult, r=[cbf], w=[P])
                        for qb in range(lo, 4):
                            k.op("pe", "matmul", cm["O"][qb][:, 0:128], lhsT=P[:, qb * 128:(qb + 1) * 128], rhs=vv[:, kt, 0:128],
                                 start=(kt == 2 * (4 * g + qb) + 1), stop=(kt == 0), r=[P, v], w=[cm["O"][qb]])

                    prev = None
                    for idx, kt in enumerate(ktl):
                        st = stage_a(idx, kt)
                        if prev is not None:
                            stage_b(prev)
                        prev = st
                    stage_b(prev)
                    onb = cm["on"][0]
                    for qb in range(4):
                        k.op("act", "copy", out=onb[:, qb * 128:(qb + 1) * 128], in_=cm["O"][qb][:, 0:128], r=[cm["O"][qb]], w=[onb])
                    finish_head(cm, 12 + h, g, onb, 1.0)
            k.flush()


_CACHE = {}


def _host_inputs(inputs, C):
    common = {}
    for nm in ["ffn1_w_gate", "ffn1_w_up", "ffn1_w_down", "ffn2_w_gate", "ffn2_w_up", "ffn2_w_down", "w_in", "w_out",
               "mla_w_uq", "mla_w_ukv", "nsa_w_ck", "nsa_w_cv", "nsa_pe_k", "nsa_pe_v", "ffn1_norm", "mix_norm",
               "ffn2_norm", "mla_o_norm", "diff_subln", "nsa_o_norm", "sb_o_norm", "diff_lq1", "diff_lk1", "diff_lq2",
               "diff_lk2"]:
        common[nm] = np.ascontiguousarray(np.asarray(inputs[nm], np.float32))
    common["gcols"] = np.stack([_gain_cols(inputs, l) for l in range(DEPTH)], axis=0).astype(np.float32)
    common["cbf"] = C["cbf"]
    common["cf32"] = C["cf32"]
    common["cosT"] = C["cosT"]
    common["sinT"] = C["sinT"]
    common["bonus"] = C["bonus"]
    return common


def _rows(r):
    j = np.arange(NQB)
    return ((2 * j[:, None] + r) * 128 + np.arange(128)[None, :]).reshape(-1)


def kernel(**inputs):
    Cs = [_consts(0), _consts(1)]
    inputs = {kk: np.asarray(v) for kk, v in inputs.items()}
    _gain_cols(inputs, 0)
    nc = build_program(Cs[0])
    commons = [_host_inputs(inputs, Cs[r]) for r in range(2)]
    x = np.asarray(inputs["x"], np.float32)
    in_maps = []
    for c in range(8):
        b, r = c // 2, c % 2
        m = dict(commons[r])
        m["x"] = np.ascontiguousarray(x[b][_rows(r)])
        in_maps.append(m)
    res = run_bass_kernel_spmd(nc, in_maps, core_ids=list(range(8)))
    out = np.empty((4, S, D), np.float32)
    for c in range(8):
        b, r = c // 2, c % 2
        out[b, _rows(r)] = res.results[c]["y"]
    return out
```

```python
import math
from contextlib import ExitStack

import numpy as np
import ml_dtypes

import concourse.bass as bass
import concourse.mybir as mybir
from concourse.bass_utils import run_bass_kernel_spmd

F32 = mybir.dt.float32
BF16 = mybir.dt.bfloat16
AF = mybir.ActivationFunctionType
ALU = mybir.AluOpType
AX = mybir.AxisListType

D = 2048
S = 4096
DEPTH = 2
DFF = 5632
NFC = DFF // 128
DIN = 5196
SL = 2048
NQB = SL // 128
T = 512
NTT = SL // T
EPS = 1e-6
NSLOPE = 8
SLOPES = [2.0 ** (-(i + 1)) for i in range(8)]
DIFF_SLOPES = SLOPES[0::2]
NSA_SLOPES = SLOPES[1::2]

DEBUG = None
ONLY = None
PRECONV = True


class Buf:
    __slots__ = ("t", "name", "lw", "rd", "sem", "ndma")

    def __init__(self, t, name):
        self.t = t
        self.name = name
        self.lw = None
        self.rd = []
        self.sem = None
        self.ndma = 0

    def __getitem__(self, key):
        return self.t[key]


class Op:
    __slots__ = ("eng", "dma", "calls", "deps", "inc", "count", "sem", "key")

    def __init__(self, eng, dma, calls, key=None):
        self.eng = eng
        self.dma = dma
        self.calls = calls
        self.deps = set()
        self.inc = False
        self.count = 0
        self.sem = None
        self.key = key


class KB:
    CENG = ("pe", "act", "dve", "pool")

    def __init__(self, nc, es):
        self.nc = nc
        self.es = es
        self.engs = {"pe": nc.tensor, "act": nc.scalar, "dve": nc.vector, "pool": nc.gpsimd, "sp": nc.sync}
        self.ops = []
        self.base = 0
        self.csem = {e: es.enter_context(nc.semaphore("cs_" + e)) for e in self.CENG}
        self.ccount = {e: 0 for e in self.CENG}
        self.seen = {e: {} for e in self.engs}
        self.keys = []
        self.bufs = []
        self.n_ins = 0

    def sb(self, es, name, cols, dtype):
        self.uid = getattr(self, "uid", 0) + 1
        name = f"{name}_{self.uid}"
        t = es.enter_context(self.nc.sbuf_tensor(name, [128, cols], dtype))
        b = Buf(t, name)
        self.bufs.append(b)
        return b

    def ps(self, es, name, cols, dtype=F32):
        self.uid = getattr(self, "uid", 0) + 1
        name = f"{name}_{self.uid}"
        t = es.enter_context(self.nc.psum_tensor(name, [128, cols], dtype))
        b = Buf(t, name)
        self.bufs.append(b)
        return b

    def _track(self, op, idx, r, w):
        for b in w:
            if b.lw is not None:
                op.deps.add(b.lw)
            op.deps.update(b.rd)
        for b in r:
            if b.lw is not None:
                op.deps.add(b.lw)
        for b in w:
            b.lw = idx
            b.rd = []
        for b in r:
            if b not in w:
                b.rd.append(idx)
                if len(b.rd) > 64:
                    keep = {}
                    rest = []
                    for i in b.rd:
                        o = self.ops[i]
                        if o.dma:
                            rest.append(i)
                        else:
                            keep[o.eng] = i
                    b.rd = rest + list(keep.values())

    def op(self, eng, method, *args, r=(), w=(), **kw):
        o = Op(eng, False, [(method, args, kw)])
        idx = len(self.ops)
        self.ops.append(o)
        self._track(o, idx, r, w)
        return idx

    def dma(self, eng, out, in_, key, r=(), w=()):
        o = Op(eng, True, [("dma_start", (), {"out": out, "in_": in_})], key=key)
        idx = len(self.ops)
        self.ops.append(o)
        self._track(o, idx, r, w)
        return idx

    def flush(self, final=False):
        nc = self.nc
        ops = self.ops
        n = len(ops)
        for i in range(self.base, n):
            o = ops[i]
            for d in o.deps:
                if d >= self.base:
                    od = ops[d]
                    if not (od.eng == "pe" and o.eng == "pe" and not od.dma and not o.dma):
                        od.inc = True
        last = {}
        for i in range(self.base, n):
            o = ops[i]
            if not o.dma:
                last[o.eng] = i
        for e, i in last.items():
            ops[i].inc = True
        for i in range(self.base, n):
            o = ops[i]
            if o.dma:
                kb = o.key
                if kb.sem is None:
                    pool = self.__dict__.setdefault("sempool", [])
                    if pool:
                        kb.sem, kb.ndma = pool.pop()
                    else:
                        kb.sem = self.es.enter_context(nc.semaphore("ds_" + kb.name))
                        kb.ndma = 0
                    self.keys.append(kb)
                kb.ndma += 1
                o.sem = kb.sem
                o.count = 16 * kb.ndma
            else:
                if o.inc:
                    self.ccount[o.eng] += 1
                o.sem = self.csem[o.eng]
                o.count = self.ccount[o.eng]
        for i in range(self.base, n):
            o = ops[i]
            E = self.engs[o.eng]
            seen = self.seen[o.eng]
            need = {}
            for d in o.deps:
                if d < self.base:
                    continue
                od = ops[d]
                if od.eng == "pe" and o.eng == "pe" and not od.dma and not o.dma:
                    continue
                if not od.dma and not od.inc:
                    raise RuntimeError("dep without inc")
                key = od.sem
                if need.get(key, (None, 0))[1] < od.count:
                    need[key] = (od.sem, od.count)
            for key, (sem, cnt) in need.items():
                if seen.get(key, 0) < cnt:
                    E.wait_ge(sem, cnt)
                    seen[key] = cnt
                    self.n_ins += 1
            ins = None
            for (m, a, kw) in o.calls:
                ins = getattr(E, m)(*a, **kw)
                self.n_ins += 1
            if o.dma:
                ins.then_inc(o.sem, 16)
            elif o.inc:
                ins.then_inc(o.sem, 1)
        targets = [(self.csem[e], self.ccount[e]) for e in self.CENG if self.ccount[e] > 0]
        targets += [(kb.sem, 16 * kb.ndma) for kb in self.keys]
        wait_engs = ["sp"] if final else list(self.engs.keys())
        for e in wait_engs:
            E = self.engs[e]
            seen = self.seen[e]
            for (sem, cnt) in targets:
                if seen.get(sem, 0) < cnt:
                    E.wait_ge(sem, cnt)
                    seen[sem] = cnt
                    self.n_ins += 1
        self.base = n
        pool = self.__dict__.setdefault("sempool", [])
        for kb in self.keys:
            pool.append((kb.sem, kb.ndma))
            kb.sem = None
        self.keys = []
        for b in self.bufs:
            b.lw = None
            b.rd = []


def _consts(r=0):
    c = {}
    p = np.arange(128)
    ident = np.eye(128, dtype=np.float32)
    ones = np.ones((128, 128), np.float32)
    zeros = np.zeros((128, 128), np.float32)
    blk64 = np.kron(np.eye(2), np.ones((64, 64))).astype(np.float32)
    tri = (p[:, None] <= p[None, :]).astype(np.float32)
    tris = (p[:, None] < p[None, :]).astype(np.float32)
    atri = (p[:, None] > p[None, :]).astype(np.float32)
    ntri_incl = -(p[:, None] >= p[None, :]).astype(np.float32)
    nones = -ones
    prot = np.zeros((128, 128), np.float32)
    for m in range(32):
        prot[m + 32, m] = -1.0
    for m in range(32, 64):
        prot[m - 32, m] = 1.0
    if r == 0:
        m0, m1, ms0, ms1 = tri, zeros, tris, zeros
        wm = {-4: atri, -3: ones, 0: tri, 1: zeros}
    else:
        m0, m1, ms0, ms1 = ones, tri, ones, tris
        wm = {-4: zeros, -3: atri, 0: ones, 1: tri}
    cm = []
    for (nt, j) in [(0, j) for j in range(9)] + [(1, j) for j in range(8, 16)]:
        G = 2 * j + r
        cm.append(((16 * p[:, None] + 31 - p[None, :]) <= 128 * G - 2048 * nt).astype(np.float32))
    cm = np.concatenate(cm, axis=1)
    n = np.arange(256)
    cst = 16 * n
    sel_start = 64 * np.arange(64)
    ov = ((cst[:, None] < sel_start[None, :] + 64) & (cst[:, None] + 32 > sel_start[None, :])).astype(np.float32)
    ov[255] = 0
    ov2 = ov.reshape(2, 128, 64).transpose(1, 0, 2)
    parts = [("ident", ident), ("ones", ones), ("blk64", blk64), ("tri", tri), ("tris", tris), ("atri", atri),
             ("ntri", ntri_incl), ("nones", nones), ("prot", prot), ("m0", m0), ("m1", m1), ("ms0", ms0), ("ms1", ms1),
             ("w-4", wm[-4]), ("w-3", wm[-3]), ("w0", wm[0]), ("w1", wm[1]), ("cm", cm), ("ov", ov2.reshape(128, -1))]
    c["cbf"] = np.concatenate([a for _, a in parts], axis=1).astype(ml_dtypes.bfloat16)
    off = {}
    o = 0
    for name, a in parts:
        off[name] = o
        o += a.shape[1]
    c["cbf_off"] = off
    c["cbf_w"] = o
    jl = np.arange(NQB)
    posl = ((2 * jl[:, None] + r) * 128 + p[None, :]).reshape(-1).astype(np.float32)
    inv_freq = (10000.0 ** (-np.arange(0, 64, 2, dtype=np.float32) / 64)).astype(np.float32)
    ang = posl[:, None] * inv_freq[None, :]
    c["cosT"] = np.ascontiguousarray(np.concatenate([np.cos(ang), np.cos(ang)], axis=1).T).astype(np.float32)
    c["sinT"] = np.ascontiguousarray(np.concatenate([np.sin(ang), np.sin(ang)], axis=1).T).astype(np.float32)
    e = np.arange(64) - 62
    base = 128.0 * (e[None, :] - r) + p[:, None] - 127.0
    ab = np.stack([sl * base for sl in SLOPES], axis=1)
    nt = np.arange(2)
    G = 2 * jl + r
    cbase = 16.0 * (128 * nt[None, :, None] + p[:, None, None]) + 31 - 128.0 * G[None, None, :] - 127.0
    cbias = np.stack([sl * cbase for sl in NSA_SLOPES], axis=1)
    cbias = np.minimum(cbias, 60.0)
    t = (128 * G[None, :] + p[:, None])
    jb = np.arange(64)
    cur = t // 64
    valid = sel_start[None, None, :] <= t[:, :, None]
    forced = (jb[None, None, :] == 0) | (jb[None, None, :] == cur[:, :, None]) | (jb[None, None, :] == cur[:, :, None] - 1)
    bonus = np.where(valid, np.where(forced, 1e4, 0.0), -1e30).astype(np.float32)
    cf = np.concatenate([ab.reshape(128, -1), cbias.reshape(128, -1)], axis=1)
    c["cf32"] = cf.astype(np.float32)
    c["bonus"] = bonus.reshape(128, -1).astype(np.float32)
    c["cf_off"] = {"ab": 0, "cbias": 512}
    c["cf_w"] = cf.shape[1]
    return c


GC = {}


def _gain_cols(inputs, l):
    cols = []

    def add(name, v):
        GC[name] = len(cols)
        cols.append(np.asarray(v, np.float32).reshape(128))

    for c in range(4):
        add(f"cq{c}", inputs["mla_cq_norm"][l][c * 128:(c + 1) * 128])
    for c in range(2):
        add(f"ckv{c}", inputs["mla_ckv_norm"][l][c * 128:(c + 1) * 128])
    add("qn", inputs["mla_qn_norm"][l])
    add("qr", np.tile(inputs["mla_qr_norm"][l], 2))
    add("kn", inputs["mla_kn_norm"][l])
    add("kr", np.tile(inputs["mla_kr_norm"][l], 2))
    add("dq", np.tile(inputs["diff_q_norm"][l], 2))
    add("dk", np.tile(inputs["diff_k_norm"][l], 2))
    add("nq", inputs["nsa_q_norm"][l])
    add("nkc", inputs["nsa_kc_norm"][l])
    add("nks", inputs["nsa_ks_norm"][l])
    add("nkw", inputs["nsa_kw_norm"][l])
    return np.stack(cols, axis=1)


NGC = 16


def build_program(C, stop_after=None, debug_out=False):
    nc = bass.Bass("TRN2", target_bir_lowering=False, num_devices=8)

    def din(name, shape, dt=F32):
        return nc.dram_tensor(name, list(shape), dt, kind="ExternalInput").ap()

    skind = "ExternalOutput" if debug_out else "Internal"

    def dscr(name, shape, dt):
        return nc.dram_tensor(name, list(shape), dt, kind=skind).ap()

    x = din("x", [SL, D])
    W = {}
    for nm, shp in [("ffn1_w_gate", [DEPTH, D, DFF]), ("ffn1_w_up", [DEPTH, D, DFF]), ("ffn1_w_down", [DEPTH, DFF, D]),
                    ("ffn2_w_gate", [DEPTH, D, DFF]), ("ffn2_w_up", [DEPTH, D, DFF]), ("ffn2_w_down", [DEPTH, DFF, D]),
                    ("w_in", [DEPTH, D, DIN]), ("w_out", [DEPTH, D, D]),
                    ("mla_w_uq", [DEPTH, 512, 768]), ("mla_w_ukv", [DEPTH, 256, 1024]),
                    ("nsa_w_ck", [DEPTH, 4096, 128]), ("nsa_w_cv", [DEPTH, 4096, 128]),
                    ("nsa_pe_k", [DEPTH, 32, 128]), ("nsa_pe_v", [DEPTH, 32, 128]),
                    ("ffn1_norm", [DEPTH, D]), ("mix_norm", [DEPTH, D]), ("ffn2_norm", [DEPTH, D]),
                    ("mla_o_norm", [DEPTH, 128]), ("diff_subln", [DEPTH, 128]), ("nsa_o_norm", [DEPTH, 128]),
                    ("sb_o_norm", [DEPTH, 128]),
                    ("diff_lq1", [DEPTH, 64]), ("diff_lk1", [DEPTH, 64]), ("diff_lq2", [DEPTH, 64]), ("diff_lk2", [DEPTH, 64]),
                    ("gcols", [DEPTH, 128, NGC])]:
        W[nm] = din(nm, shp)
    cbf_d = din("cbf", [128, C["cbf_w"]], BF16)
    cf_d = din("cf32", [128, C["cf_w"]])
    cos_d = din("cosT", [64, SL])
    sin_d = din("sinT", [64, SL])
    bonus_d = din("bonus", [128, NQB * 64])
    W["bonus"] = bonus_d
    y = nc.dram_tensor("y", [SL, D], F32, kind="ExternalOutput").ap()

    hS1 = dscr("hS1", [SL, D], F32)
    hS3 = dscr("hS3", [SL, D], F32)
    QT = dscr("QT", [20, 128, SL], BF16)
    KVs = nc.dram_tensor("KVs", [2 * 4096, SL], BF16, kind="Internal", addr_space="Shared").ap()
    pid = nc.partition_id()
    rk = pid % 2
    kv_mine = KVs[bass.ts(rk, 4096), :]

    KVloc = dscr("KVloc", [4096, SL], BF16)
    kv_dyn = [kv_mine[i * 512:(i + 1) * 512, :] for i in range(8)]

    class _KTW:
        def __getitem__(self, slot):
            return KVloc[slot * 128:(slot + 1) * 128, :]

    class _VVW:
        def __getitem__(self, slot):
            return KVloc[2304 + slot * 128:2304 + (slot + 1) * 128, :].rearrange("r (a d) -> (r a) d", d=128)

    KT = _KTW()
    VV = _VVW()

    def kt_read(rr, slot):
        return KVs[rr * 4096 + slot * 128: rr * 4096 + (slot + 1) * 128, :]

    def vv_read(rr, slot):
        return KVs[rr * 4096 + 2304 + slot * 128: rr * 4096 + 2304 + (slot + 1) * 128, :].rearrange("r (a d) -> (r a) d", d=128)

    GT = dscr("GT", [SL, 12], F32)
    OT = dscr("OT", [16, 128, SL], BF16)
    WBF = {}
    for nm, shp in [("ffn2_w_gate", [D, DFF]), ("ffn2_w_up", [D, DFF]), ("ffn2_w_down", [DFF, D]), ("w_out", [D, D]),
                    ("ffn1_w_gate", [D, DFF]), ("ffn1_w_up", [D, DFF]), ("ffn1_w_down", [DFF, D]), ("w_in", [D, DIN])]:
        WBF[nm] = nc.dram_tensor("wbf_" + nm, shp, BF16, kind="Internal").ap()
    W["_bf"] = WBF
    W["_have"] = set()

    with ExitStack() as es:
        k = KB(nc, es)
        xsem = es.enter_context(nc.semaphore("xsem"))
        xcnt = [0]
        CO = C["cbf_off"]
        cbf = k.sb(es, "cbf", C["cbf_w"], BF16)
        cf = k.sb(es, "cf", C["cf_w"], F32)
        k.dma("sp", cbf[:], cbf_d[:, :], key=cbf, w=[cbf])
        k.dma("sp", cf[:], cf_d[:, :], key=cf, w=[cf])
        if DEBUG == "recompile":
            k.op("dve", "memset", cf[:, 0:1], 0.0, w=[cf])

        def cb(name, rows=128, cols=128, c0=0):
            o = CO[name] + c0
            return cbf[0:rows, o:o + cols]

        for l in range(DEPTH):
            lam_init = 0.8 - 0.6 * math.exp(-0.3 * l)
            src = x if l == 0 else hS3
            with ExitStack() as pa:
                phase_tokens(k, pa, nc, C, W, l, "A", src, hS1, cb, cbf, cf, cos_d, sin_d,
                             QT, KT, VV, GT, OT, lam_init)
                k.flush()
            for i in range(8):
                nc.sync.dma_start(out=kv_dyn[i], in_=KVloc[i * 512:(i + 1) * 512, :]).then_inc(xsem, 16)
            xcnt[0] += 8 * 16
            nc.sync.wait_ge(xsem, xcnt[0])
            nc.all_core_barrier()
            if stop_after == f"A{l}":
                break
            with ExitStack() as pb:
                conv = [("ffn2_w_gate", l), ("ffn2_w_up", l), ("ffn2_w_down", l), ("w_out", l)]
                if l + 1 < DEPTH:
                    conv += [("ffn1_w_gate", l + 1), ("ffn1_w_up", l + 1), ("ffn1_w_down", l + 1), ("w_in", l + 1)]
                if not PRECONV:
                    conv = []
                for (nm, ll) in conv:
                    kb_ = Buf(None, f"cv_{nm}_{ll}")
                    R = W[nm].shape[1]
                    step_r = R // 32
                    for i in range(32):
                        k.dma("pool", WBF[nm][i * step_r:(i + 1) * step_r, :], W[nm][ll][i * step_r:(i + 1) * step_r, :], key=kb_)
                    W["_have"].add((nm, ll))
                phase_attn(k, pb, nc, C, W, l, cb, cbf, cf, QT, (kt_read, vv_read), None, GT, OT, lam_init, only=ONLY)
                k.flush()
            nc.all_core_barrier()
            if stop_after == f"B{l}":
                break
            dst = y if l == DEPTH - 1 else hS3
            with ExitStack() as pc:
                phase_tokens(k, pc, nc, C, W, l, "C", hS1, dst, cb, cbf, cf, cos_d, sin_d,
                             QT, KT, VV, GT, OT, lam_init)
                k.flush()
            if stop_after == f"C{l}":
                break
        k.flush(final=True)
        print("instructions:", k.n_ins, "ops:", len(k.ops), "dma sems:", len(k.keys))
    return nc


def phase_tokens(k, es, nc, C, W, l, which, src, dst, cb, cbf, cf, cos_d, sin_d, QT, KT, VV, GT, OT, lam_init):
    A = which == "A"
    htile = k.sb(es, "htile", 4 * D, F32)
    xnb = [k.sb(es, f"xnb{i}", D, BF16) for i in range(2)]
    xnT = k.sb(es, "xnT", 16 * T, BF16)
    actT = k.sb(es, "actT", NFC * T, BF16)
    WB = [k.sb(es, f"WB{i}", 8192, BF16) for i in range(2)]
    sg = [k.sb(es, f"sg{i}", T, F32) for i in range(2)]
    gbc = k.sb(es, "gbc", D, F32)
    ss = k.sb(es, "ss", 8, F32)
    PG = [k.ps(es, f"PG{i}", 512) for i in range(2)]
    PU = [k.ps(es, f"PU{i}", 512) for i in range(2)]
    PD = [k.ps(es, f"PD{i}", 512) for i in range(2)]
    PM = k.ps(es, "PM", 512)
    PT = k.ps(es, "PT", 1024, BF16)
    wbi = [0]

    def nextwb():
        b = WB[wbi[0] % 2]
        wbi[0] += 1
        return b

    fn = "ffn1" if A else "ffn2"

    def wsrc(nm):
        return W["_bf"][nm] if (nm, l) in W["_have"] else W[nm][l]
    if A:
        gc = k.sb(es, "gc", NGC, F32)
        gcs = k.sb(es, "gcs", NGC, F32)
        k.dma("sp", gc[:], W["gcols"][l], key=gc, w=[gc])
        for nm, sc in [("qn", 192 ** -0.5), ("qr", 192 ** -0.5), ("dq", 64 ** -0.5), ("nq", 128 ** -0.5)]:
            j = GC[nm]
            k.op("dve", "tensor_scalar", out=gcs[:, j:j + 1], in0=gc[:, j:j + 1], scalar1=float(sc), scalar2=None,
                 op0=ALU.mult, r=[gc], w=[gcs])
        wuq = k.sb(es, "wuq", 4 * 768, BF16)
        wukv = k.sb(es, "wukv", 2 * 1024, BF16)
        k.dma("pool", wuq[:].rearrange("p (c n) -> p c n", c=4),
              W["mla_w_uq"][l].rearrange("(c p) n -> p c n", p=128), key=wuq, w=[wuq])
        k.dma("pool", wukv[:].rearrange("p (c n) -> p c n", c=2),
              W["mla_w_ukv"][l].rearrange("(c p) n -> p c n", p=128), key=wukv, w=[wukv])
        cqf = k.sb(es, "cqf", 4 * T, BF16)
        cqn = k.sb(es, "cqn", 4 * T, BF16)
        ckvn = k.sb(es, "ckvn", 2 * T, BF16)
        sqb = [k.sb(es, f"sqb{i}", T, BF16) for i in range(2)]
        rstd = [k.sb(es, f"rstd{i}", T, F32) for i in range(2)]
        yf = k.sb(es, "yf", T, F32)
        ybf = [k.sb(es, f"ybf{i}", T, BF16) for i in range(3)]
        t1 = k.sb(es, "t1", T, F32)
        t2 = k.sb(es, "t2", T, F32)
        cosb = k.sb(es, "cosb", T, F32)
        sinb = k.sb(es, "sinb", T, F32)
        vout = [k.sb(es, f"vout{i}", 4 * 512, BF16) for i in range(2)]
        gout = k.sb(es, "gout", 4 * 12, F32)
        cnt = {"y": 0, "sq": 0, "v": 0}

    def rms_xnT(gname):
        gain_b = gbc
        k.dma("sp", gbc[:], W[gname][l].partition_broadcast(128), key=gbc, w=[gbc])
        for tb in range(4):
            sqj = xnb[(tb + 1) % 2]
            hb = htile[:, tb * D:(tb + 1) * D]
            k.op("dve", "memset", ss[:, tb:tb + 1], 0.0, w=[ss])
            k.op("act", "activation", out=sqj[:], in_=hb, func=AF.Square, accum_out=ss[:, tb:tb + 1],
                 r=[htile], w=[sqj, ss])
            k.op("act", "activation", out=ss[:, 4 + tb:5 + tb], in_=ss[:, tb:tb + 1], func=AF.Sqrt, scale=1.0 / D, bias=EPS,
                 r=[ss], w=[ss])
            k.op("dve", "reciprocal", out=ss[:, 4 + tb:5 + tb], in_=ss[:, 4 + tb:5 + tb], r=[ss], w=[ss])
            xb = xnb[tb % 2]
            k.op("dve", "scalar_tensor_tensor", out=xb[:], in0=hb, scalar=ss[:, 4 + tb:5 + tb], in1=gain_b[:],
                 op0=ALU.mult, op1=ALU.mult, r=[htile, ss, gain_b], w=[xb])
            for half in range(2):
                for c in range(8):
                    dc = half * 8 + c
                    k.op("pe", "transpose", out=PT[:, c * 128:(c + 1) * 128], in_=xb[:, dc * 128:(dc + 1) * 128],
                         identity=cb("ident"), r=[xb, cbf], w=[PT])
                dstv = xnT[:].rearrange("p (c t) -> p c t", c=16)[:, half * 8:(half + 1) * 8, tb * 128:(tb + 1) * 128]
                k.op("act", "copy", out=dstv, in_=PT[:].rearrange("p (c t) -> p c t", c=8), r=[PT], w=[xnT])

    def ffn(wg, wu, wd):
        xv = xnT[:].rearrange("p (c t) -> p c t", c=16)
        av = actT[:].rearrange("p (c t) -> p c t", c=NFC)
        pi = 0
        for fg in range(NFC // 2):
            wb = nextwb()
            wv = wb[:, 0:8192].rearrange("p (g c n) -> p g c n", g=2, c=16)
            k.dma("pool", wv[:, 0], wg[:, fg * 256:(fg + 1) * 256].rearrange("(c p) n -> p c n", p=128), key=wb, w=[wb])
            k.dma("pool", wv[:, 1], wu[:, fg * 256:(fg + 1) * 256].rearrange("(c p) n -> p c n", p=128), key=wb, w=[wb])
            for fc in range(2):
                pg, pu, sgb = PG[pi % 2], PU[pi % 2], sg[pi % 2]
                pi += 1
                for dc in range(16):
                    k.op("pe", "matmul", pg[:], lhsT=wv[:, 0, dc, fc * 128:(fc + 1) * 128], rhs=xv[:, dc, :],
                         start=(dc == 0), stop=(dc == 15), r=[wb, xnT], w=[pg])
                for dc in range(16):
                    k.op("pe", "matmul", pu[:], lhsT=wv[:, 1, dc, fc * 128:(fc + 1) * 128], rhs=xv[:, dc, :],
                         start=(dc == 0), stop=(dc == 15), r=[wb, xnT], w=[pu])
                k.op("act", "activation", out=sgb[:], in_=pg[:], func=AF.Silu, r=[pg], w=[sgb])
                k.op("dve", "tensor_tensor", out=av[:, fg * 2 + fc, :], in0=sgb[:], in1=pu[:], op=ALU.mult,
                     r=[sgb, pu], w=[actT])
        for cg in range(16):
            wb = nextwb()
            wv = wb[:, 0:NFC * 128].rearrange("p (c n) -> p c n", c=NFC)
            for hh in range(2):
                k.dma("pool", wv[:, hh * 22:(hh + 1) * 22, :],
                      wd[hh * 2816:(hh + 1) * 2816, cg * 128:(cg + 1) * 128].rearrange("(c p) n -> p c n", p=128),
                      key=wb, w=[wb])
            for tb in range(4):
                pd = PD[(cg * 4 + tb) % 2]
                for fc in range(NFC):
                    k.op("pe", "matmul", pd[:, 0:128], lhsT=av[:, fc, tb * 128:(tb + 1) * 128], rhs=wv[:, fc, :],
                         start=(fc == 0), stop=(fc == NFC - 1), r=[wb, actT], w=[pd])
                hv = htile[:, tb * D + cg * 128: tb * D + (cg + 1) * 128]
                k.op("dve", "scalar_tensor_tensor", out=hv, in0=pd[:, 0:128], scalar=0.5, in1=hv,
                     op0=ALU.mult, op1=ALU.add, r=[pd], w=[htile])

    def load_h(tt, srcap):
        k.dma("sp", htile[:].rearrange("p (b d) -> p b d", b=4),
              srcap[tt * T:(tt + 1) * T, :].rearrange("(b p) d -> p b d", p=128), key=htile, w=[htile])

    def store_h(tt, dstap):
        k.dma("sp", dstap[tt * T:(tt + 1) * T, :].rearrange("(b p) d -> p b d", p=128),
              htile[:].rearrange("p (b d) -> p b d", b=4), key=htile, r=[htile])

    def fm_finish(ps, n, tt, norm, gcol, scale, rope, dest, keep=None):
        yb = ybf[cnt["y"] % 3]
        cnt["y"] += 1
        if norm is None:
            k.op("act", "activation", out=yb[0:n, :], in_=ps[0:n, 0:T], func=AF.Copy, scale=float(scale), r=[ps], w=[yb])
        else:
            sq = sqb[cnt["sq"] % 2]
            rs = rstd[cnt["sq"] % 2]
            cnt["sq"] += 1
            k.op("act", "activation", out=sq[0:n, :], in_=ps[0:n, 0:T], func=AF.Square, r=[ps], w=[sq])
            onesm = cb("ones", n, n) if norm == 128 else cb("blk64", n, n)
            k.op("pe", "matmul", PM[0:n, 0:T], lhsT=onesm, rhs=sq[0:n, :], start=True, stop=True, r=[sq, cbf], w=[PM])
            k.op("act", "activation", out=rs[0:n, :], in_=PM[0:n, 0:T], func=AF.Sqrt, scale=1.0 / norm, bias=EPS,
                 r=[PM], w=[rs])
            k.op("dve", "reciprocal", out=rs[0:n, :], in_=rs[0:n, :], r=[rs], w=[rs])
            if not rope:
                k.op("dve", "scalar_tensor_tensor", out=yb[0:n, :], in0=ps[0:n, 0:T], scalar=gcol[0:n, :], in1=rs[0:n, :],
                     op0=ALU.mult, op1=ALU.mult, r=[ps, rs, gc, gcs], w=[yb])
            else:
                k.op("dve", "scalar_tensor_tensor", out=yf[0:n, :], in0=ps[0:n, 0:T], scalar=gcol[0:n, :], in1=rs[0:n, :],
                     op0=ALU.mult, op1=ALU.mult, r=[ps, rs, gc, gcs], w=[yf])
                yb2 = ybf[cnt["y"] % 3]
                cnt["y"] += 1
                k.op("act", "copy", out=yb2[0:n, :], in_=yf[0:n, :], r=[yf], w=[yb2])
                k.op("pe", "matmul", PM[0:n, 0:T], lhsT=cb("prot", n, n), rhs=yb2[0:n, :], start=True, stop=True,
                     r=[yb2, cbf], w=[PM])
                k.op("dve", "tensor_tensor", out=t1[0:n, :], in0=yf[0:n, :], in1=cosb[0:n, :], op=ALU.mult,
                     r=[yf, cosb], w=[t1])
                k.op("dve", "tensor_tensor", out=t2[0:n, :], in0=PM[0:n, 0:T], in1=sinb[0:n, :], op=ALU.mult,
                     r=[PM, sinb], w=[t2])
                k.op("dve", "tensor_tensor", out=yb[0:n, :], in0=t1[0:n, :], in1=t2[0:n, :], op=ALU.add,
                     r=[t1, t2], w=[yb])
        k.dma("sp", dest, yb[0:n, :], key=yb, r=[yb])

    def proj_mm_fm(ps, wv, nin, c0, n, inv):
        for c in range(nin):
            k.op("pe", "matmul", ps[0:n, 0:T], lhsT=wv[:, c, c0:c0 + n], rhs=inv[:, c, :], start=(c == 0),
                 stop=(c == nin - 1), r=list(proj_r), w=[ps])

    proj_r = []

    def load_win(c0, n):
        wb = nextwb()
        wv = wb[:, 0:16 * n].rearrange("p (c n) -> p c n", c=16)
        k.dma("pool", wv, wsrc("w_in")[:, c0:c0 + n].rearrange("(c p) n -> p c n", p=128), key=wb, w=[wb])
        return wb, wv

    def tm_group(wb, wv, nin, c0, n, inv, tt, dest_list, sigmoid=False):
        vo = vout[cnt["v"] % 2]
        cnt["v"] += 1
        vov = vo[:].rearrange("p (b n) -> p b n", b=4)
        for tb in range(4):
            pd = PD[tb % 2]
            for c in range(nin):
                k.op("pe", "matmul", pd[:, 0:n], lhsT=inv[:, c, tb * 128:(tb + 1) * 128], rhs=wv[:, c, c0:c0 + n],
                     start=(c == 0), stop=(c == nin - 1), r=list(proj_r), w=[pd])
            if sigmoid:
                k.op("act", "activation", out=gout[:, tb * 12:(tb + 1) * 12], in_=pd[:, 0:n], func=AF.Sigmoid,
                     r=[pd], w=[gout])
            else:
                k.op("act", "copy", out=vov[:, tb, 0:n], in_=pd[:, 0:n], r=[pd], w=[vo])
        if sigmoid:
            k.dma("sp", GT[tt * T:(tt + 1) * T, :].rearrange("(b p) n -> p b n", p=128),
                  gout[:].rearrange("p (b n) -> p b n", b=4), key=gout, r=[gout])
        else:
            for (co, wd_, dap) in dest_list:
                for tb in range(4):
                    k.dma("sp", dap[tt * T + tb * 128: tt * T + (tb + 1) * 128, :], vov[:, tb, co:co + wd_], key=vo, r=[vo])

    def projections(tt):
        xv = xnT[:].rearrange("p (c t) -> p c t", c=16)
        tsl = slice(tt * T, (tt + 1) * T)
        k.dma("sp", cosb[0:64, :], cos_d[:, tsl], key=cosb, w=[cosb])
        k.dma("sp", sinb[0:64, :], sin_d[:, tsl], key=sinb, w=[sinb])
        gcol = lambda nm: gc[:, GC[nm]:GC[nm] + 1]
        gscol = lambda nm: gcs[:, GC[nm]:GC[nm] + 1]
        wb, wv = load_win(0, 512)
        proj_r[:] = [wb, xnT]
        cqv = cqf[:].rearrange("p (c t) -> p c t", c=4)
        for c in range(4):
            ps = PG[c % 2]
            proj_mm_fm(ps, wv, 16, c * 128, 128, xv)
            k.op("act", "copy", out=cqv[:, c, :], in_=ps[:, 0:T], r=[ps], w=[cqf])
            sq = sqb[c % 2]
            k.op("act", "activation", out=sq[:], in_=ps[:, 0:T], func=AF.Square, r=[ps], w=[sq])
            k.op("pe", "matmul", PM[:, 0:T], lhsT=cb("ones"), rhs=sq[:], start=(c == 0), stop=(c == 3),
                 r=[sq, cbf], w=[PM])
        rs = rstd[0]
        k.op("act", "activation", out=rs[:], in_=PM[:, 0:T], func=AF.Sqrt, scale=1.0 / 512, bias=EPS, r=[PM], w=[rs])
        k.op("dve", "reciprocal", out=rs[:], in_=rs[:], r=[rs], w=[rs])
        cqnv = cqn[:].rearrange("p (c t) -> p c t", c=4)
        for c in range(4):
            k.op("dve", "scalar_tensor_tensor", out=cqnv[:, c, :], in0=cqv[:, c, :], scalar=gcol(f"cq{c}"), in1=rs[:],
                 op0=ALU.mult, op1=ALU.mult, r=[cqf, rs, gc], w=[cqn])
        wb, wv = load_win(512, 256 + 64)
        proj_r[:] = [wb, xnT]
        for c in range(2):
            ps = PG[c % 2]
            proj_mm_fm(ps, wv, 16, c * 128, 128, xv)
            k.op("act", "copy", out=cqv[:, c, :], in_=ps[:, 0:T], r=[ps], w=[cqf])
            sq = sqb[c % 2]
            k.op("act", "activation", out=sq[:], in_=ps[:, 0:T], func=AF.Square, r=[ps], w=[sq])
            k.op("pe", "matmul", PM[:, 0:T], lhsT=cb("ones"), rhs=sq[:], start=(c == 0), stop=(c == 1),
                 r=[sq, cbf], w=[PM])
        rs = rstd[1]
        k.op("act", "activation", out=rs[:], in_=PM[:, 0:T], func=AF.Sqrt, scale=1.0 / 256, bias=EPS, r=[PM], w=[rs])
        k.op("dve", "reciprocal", out=rs[:], in_=rs[:], r=[rs], w=[rs])
        ckvv = ckvn[:].rearrange("p (c t) -> p c t", c=2)
        for c in range(2):
            k.op("dve", "scalar_tensor_tensor", out=ckvv[:, c, :], in0=cqv[:, c, :], scalar=gcol(f"ckv{c}"), in1=rs[:],
                 op0=ALU.mult, op1=ALU.mult, r=[cqf, rs, gc], w=[ckvn])
        ps = PU[0]
        proj_mm_fm(ps, wv, 16, 256, 64, xv)
        fm_finish(ps, 64, tt, 64, gcol("kr"), 1.0, True, KT[4][0:64, tsl])
        wuqv = wuq[:].rearrange("p (c n) -> p c n", c=4)
        wukvv = wukv[:].rearrange("p (c n) -> p c n", c=2)
        for h in range(4):
            proj_r[:] = [wuq, cqn]
            ps = PG[h % 2]
            proj_mm_fm(ps, wuqv, 4, h * 192, 128, cqnv)
            fm_finish(ps, 128, tt, 128, gscol("qn"), 1.0, False, QT[h][:, tsl])
            ps = PU[h % 2]
            proj_mm_fm(ps, wuqv, 4, h * 192 + 128, 64, cqnv)
            fm_finish(ps, 64, tt, 64, gscol("qr"), 1.0, True, QT[4 + h][0:64, tsl])
            proj_r[:] = [wukv, ckvn]
            ps = PG[(h + 1) % 2]
            proj_mm_fm(ps, wukvv, 2, h * 256, 128, ckvv)
            fm_finish(ps, 128, tt, 128, gcol("kn"), 1.0, False, KT[h][:, tsl])
        proj_r[:] = [wukv, ckvn]
        for h in range(4):
            tm_group(wukv, wukvv, 2, h * 256 + 128, 128, ckvv, tt, [(0, 128, VV[h])])
        wb, wv = load_win(832, 512)
        proj_r[:] = [wb, xnT]
        for h in range(4):
            ps = PG[h % 2]
            proj_mm_fm(ps, wv, 16, h * 128, 128, xv)
            fm_finish(ps, 128, tt, 64, gscol("dq"), 1.0, False, QT[8 + h][:, tsl])
        wb, wv = load_win(1344, 512)
        proj_r[:] = [wb, xnT]
        for h in range(4):
            ps = PU[h % 2]
            proj_mm_fm(ps, wv, 16, h * 128, 128, xv)
            fm_finish(ps, 128, tt, 64, gcol("dk"), 1.0, False, KT[5 + h][:, tsl])
        wb, wv = load_win(1856, 512)
        proj_r[:] = [wb, xnT]
        tm_group(wb, wv, 16, 0, 512, xv, tt, [(h * 128, 128, VV[4 + h]) for h in range(4)])
        wb, wv = load_win(2368, 512)
        proj_r[:] = [wb, xnT]
        for h in range(4):
            ps = PG[h % 2]
            proj_mm_fm(ps, wv, 16, h * 128, 128, xv)
            fm_finish(ps, 128, tt, 128, gscol("nq"), 1.0, False, QT[12 + h][:, tsl])
        wb, wv = load_win(2880, 512)
        proj_r[:] = [wb, xnT]
        ps = PU[0]
        proj_mm_fm(ps, wv, 16, 0, 128, xv)
        fm_finish(ps, 128, tt, None, None, 1.0, False, KT[9][:, tsl])
        ps = PU[1]
        proj_mm_fm(ps, wv, 16, 128, 128, xv)
        fm_finish(ps, 128, tt, None, None, 1.0, False, KT[10][:, tsl])
        ps = PU[0]
        proj_mm_fm(ps, wv, 16, 256, 128, xv)
        fm_finish(ps, 128, tt, 128, gcol("nks"), 1.0, False, KT[11][:, tsl])
        tm_group(wb, wv, 16, 384, 128, xv, tt, [(0, 128, VV[8])])
        wb, wv = load_win(3392, 268)
        proj_r[:] = [wb, xnT]
        ps = PU[1]
        proj_mm_fm(ps, wv, 16, 0, 128, xv)
        fm_finish(ps, 128, tt, 128, gcol("nkw"), 1.0, False, KT[12][:, tsl])
        tm_group(wb, wv, 16, 128, 128, xv, tt, [(0, 128, VV[9])])
        tm_group(wb, wv, 16, 256, 12, xv, tt, None, sigmoid=True)
        wb, wv = load_win(3660, 512)
        proj_r[:] = [wb, xnT]
        for h in range(4):
            ps = PG[h % 2]
            proj_mm_fm(ps, wv, 16, h * 128, 128, xv)
            fm_finish(ps, 128, tt, None, None, 128 ** -0.5, False, QT[16 + h][:, tsl])
        wb, wv = load_win(4172, 512)
        proj_r[:] = [wb, xnT]
        for h in range(4):
            ps = PU[h % 2]
            proj_mm_fm(ps, wv, 16, h * 128, 128, xv)
            fm_finish(ps, 128, tt, None, None, 1.0, False, KT[13 + h][:, tsl])
        wb, wv = load_win(4684, 512)
        proj_r[:] = [wb, xnT]
        tm_group(wb, wv, 16, 0, 512, xv, tt, [(h * 128, 128, VV[10 + h]) for h in range(4)])

    def wout_add(tt):
        ot = xnT
        ov = ot[:].rearrange("p (c t) -> p c t", c=16)
        k.dma("sp", ov, OT[:, :, tt * T:(tt + 1) * T].rearrange("c p t -> p c t"), key=ot, w=[ot])
        for cg in range(4):
            wb = nextwb()
            wv = wb[:, 0:8192].rearrange("p (c n) -> p c n", c=16)
            k.dma("pool", wv, wsrc("w_out")[:, cg * 512:(cg + 1) * 512].rearrange("(c p) n -> p c n", p=128), key=wb, w=[wb])
            for tb in range(4):
                pd = PD[tb % 2]
                for c in range(16):
                    k.op("pe", "matmul", pd[:], lhsT=ov[:, c, tb * 128:(tb + 1) * 128], rhs=wv[:, c, :],
                         start=(c == 0), stop=(c == 15), r=[wb, ot], w=[pd])
                hv = htile[:, tb * D + cg * 512: tb * D + (cg + 1) * 512]
                k.op("dve", "tensor_tensor", out=hv, in0=pd[:], in1=hv, op=ALU.add, r=[pd], w=[htile])

    for tt in range(NTT):
        load_h(tt, src)
        if A:
            rms_xnT("ffn1_norm")
            ffn(wsrc("ffn1_w_gate"), wsrc("ffn1_w_up"), wsrc("ffn1_w_down"))
            store_h(tt, dst)
            rms_xnT("mix_norm")
            projections(tt)
        else:
            wout_add(tt)
            rms_xnT("ffn2_norm")
            ffn(wsrc("ffn2_w_gate"), wsrc("ffn2_w_up"), wsrc("ffn2_w_down"))
            store_h(tt, dst)


def phase_attn(k, es, nc, C, W, l, cb, cbf, cf, QT, KT, VV, GT, OT, lam_init, only=None):
    NG = SL // 512
    kt_read, vv_read = KT

    def common(ms):
        d = {}
        d["S"] = [k.ps(ms, f"S{i}", 512) for i in range(3)]
        d["O"] = [k.ps(ms, f"O{i}", 512) for i in range(4)]
        d["PT"] = k.ps(ms, "PTt", 1024, BF16)
        d["pt"] = [k.sb(ms, f"pt{i}", 512, BF16) for i in range(3)]
        d["on"] = [k.sb(ms, f"on{i}", 512, F32) for i in range(3)]
        d["sm"] = k.sb(ms, "sm", 16, F32)
        d["ob"] = [k.sb(ms, f"ob{i}", 128, BF16) for i in range(2)]
        d["ot"] = [k.sb(ms, f"ot{i}", 512, BF16) for i in range(2)]
        d["gbc"] = k.sb(ms, "gbco", 128, F32)
        d["junk"] = k.sb(ms, "junk", 128, F32)
        d["cnt"] = {"s": 0, "p": 0, "ob": 0, "ot": 0}
        return d

    def causal_mode(kt, j):
        e = kt - 2 * j
        if e > 1:
            return "skip"
        return "m1" if e == 1 else ("m0" if e == 0 else "full")

    def std_mask(m, kt, qb):
        return cb(m), [cbf]

    def soft_sub(cm, g, pairs, vfn, nv, ktlist, modefn, biasfn, maskfn, fin, preads, nchunk=4):
        ktlist = list(ktlist)
        valid = {qb: [kt for kt in ktlist if modefn(kt, 4 * g + qb) != "skip"] for qb in range(4)}
        steps = []
        for kt in ktlist:
            qbs = [qb for qb in range(4) if modefn(kt, 4 * g + qb) != "skip"]
            if qbs:
                steps.append((kt, qbs))

        def emit_qk(st):
            kt, qbs = st["kt"], st["qbs"]
            lo, hi = qbs[0], qbs[-1] + 1
            Sb = cm["S"][cm["cnt"]["s"] % 3]
            cm["cnt"]["s"] += 1
            P = cm["pt"][cm["cnt"]["p"] % 3]
            cm["cnt"]["p"] += 1
            st["S"], st["P"] = Sb, P
            for i, (lf, rf) in enumerate(pairs):
                k.op("pe", "matmul", Sb[:, lo * 128:hi * 128], lhsT=lf(kt), rhs=rf(g * 512 + lo * 128, g * 512 + hi * 128),
                     start=(i == 0), stop=(i == len(pairs) - 1), r=preads, w=[Sb])

        def emit_act(st):
            kt, qbs, Sb, P = st["kt"], st["qbs"], st["S"], st["P"]
            lo, hi = qbs[0], qbs[-1] + 1
            if biasfn is None:
                k.op("act", "activation", out=P[:, lo * 128:hi * 128], in_=Sb[:, lo * 128:hi * 128], func=AF.Exp,
                     r=[Sb], w=[P])
            else:
                cs = 4 // nchunk
                for c in range(nchunk):
                    a, b = max(lo, c * cs), min(hi, (c + 1) * cs)
                    if a >= b:
                        continue
                    k.op("act", "activation", out=P[:, a * 128:b * 128], in_=Sb[:, a * 128:b * 128],
                         func=AF.Exp, bias=biasfn(kt, 4 * g + (c + 1) * cs - 1), r=[Sb, cf], w=[P])
            for qb in qbs:
                m = modefn(kt, 4 * g + qb)
                if m != "full":
                    map_, mr = maskfn(m, kt, qb)
                    k.op("dve", "tensor_tensor", out=P[:, qb * 128:(qb + 1) * 128], in0=P[:, qb * 128:(qb + 1) * 128],
                         in1=map_, op=ALU.mult, r=mr, w=[P])

        def emit_pv(st):
            kt, qbs, P = st["kt"], st["qbs"], st["P"]
            for qb in qbs:
                k.op("pe", "matmul", cm["O"][qb][:, 0:nv], lhsT=P[:, qb * 128:(qb + 1) * 128], rhs=vfn(kt),
                     start=(kt == valid[qb][0]), stop=(kt == valid[qb][-1]), r=[P] + preads, w=[cm["O"][qb]])

        sts = [{"kt": kt, "qbs": qbs} for (kt, qbs) in steps]
        n = len(sts)
        for i in range(n):
            emit_qk(sts[i])
            if i >= 2:
                emit_pv(sts[i - 2])
            if i >= 1:
                emit_act(sts[i - 1])
        if n >= 1:
            emit_act(sts[n - 1])
        if n >= 2:
            emit_pv(sts[n - 2])
        if n >= 1:
            emit_pv(sts[n - 1])
        for qb in range(4):
            fin(qb, cm["O"][qb])

    def fin_norm(cm, onb):
        def fin(qb, Ob):
            sm = cm["sm"]
            k.op("dve", "tensor_scalar", out=sm[:, qb:qb + 1], in0=Ob[:, 128:129], scalar1=1e-37, scalar2=None,
                 op0=ALU.max, r=[Ob], w=[sm])
            k.op("dve", "reciprocal", out=sm[:, qb:qb + 1], in_=sm[:, qb:qb + 1], r=[sm], w=[sm])
            k.op("dve", "tensor_scalar", out=onb[:, qb * 128:(qb + 1) * 128], in0=Ob[:, 0:128], scalar1=sm[:, qb:qb + 1],
                 scalar2=None, op0=ALU.mult, r=[Ob, sm], w=[onb])
        return fin

    def finish_head(cm, slot, g, onb, extra):
        sm = cm["sm"]
        otb = cm["ot"][cm["cnt"]["ot"] % 2]
        cm["cnt"]["ot"] += 1
        for qb in range(4):
            ov = onb[:, qb * 128:(qb + 1) * 128]
            k.op("dve", "memset", sm[:, 4 + qb:5 + qb], 0.0, w=[sm])
            k.op("act", "activation", out=cm["junk"][:], in_=ov, func=AF.Square, accum_out=sm[:, 4 + qb:5 + qb],
                 r=[onb], w=[cm["junk"], sm])
            k.op("act", "activation", out=sm[:, 8 + qb:9 + qb], in_=sm[:, 4 + qb:5 + qb], func=AF.Sqrt, scale=1.0 / 128,
                 bias=EPS, r=[sm], w=[sm])
            k.op("dve", "reciprocal", out=sm[:, 8 + qb:9 + qb], in_=sm[:, 8 + qb:9 + qb], r=[sm], w=[sm])
            if extra != 1.0:
                k.op("dve", "tensor_scalar", out=sm[:, 8 + qb:9 + qb], in0=sm[:, 8 + qb:9 + qb], scalar1=float(extra),
                     scalar2=None, op0=ALU.mult, r=[sm], w=[sm])
            ob = cm["ob"][cm["cnt"]["ob"] % 2]
            cm["cnt"]["ob"] += 1
            k.op("dve", "scalar_tensor_tensor", out=ob[:], in0=ov, scalar=sm[:, 8 + qb:9 + qb], in1=cm["gbc"][:],
                 op0=ALU.mult, op1=ALU.mult, r=[onb, sm, cm["gbc"]], w=[ob])
            k.op("pe", "transpose", out=cm["PT"][:, qb * 128:(qb + 1) * 128], in_=ob[:], identity=cb("ident"),
                 r=[ob, cbf], w=[cm["PT"]])
        k.op("act", "copy", out=otb[:], in_=cm["PT"][:, 0:512], r=[cm["PT"]], w=[otb])
        k.dma("sp", OT[slot][:, g * 512:(g + 1) * 512], otb[:], key=otb, r=[otb])

    def load_v(v, slot):
        vv = v[:].rearrange("p (t c) -> p t c", c=132)
        for rr in range(2):
            k.dma("sp", vv.rearrange("p (b two) c -> p b two c", two=2)[:, :, rr, 0:128],
                  vv_read(rr, slot).rearrange("(b p) d -> p b d", p=128), key=v, w=[v])
        return vv

    def load_k(buf, slot, rows=128):
        for rr in range(2):
            k.dma("sp", buf[0:rows, :].rearrange("p (b two t) -> p b two t", two=2, t=128)[:, :, rr, :],
                  kt_read(rr, slot)[0:rows, :].rearrange("p (b t) -> p b t", t=128), key=buf, w=[buf])

    def ab_bias(si):
        return lambda kt, j: cf[:, si * 64 + (kt - 2 * j + 62): si * 64 + (kt - 2 * j + 62) + 1]

    def ab_chunks(si):
        sl = SLOPES[si]
        return 4 if sl * 511 > 70 else (2 if sl * 1023 > 70 else 1)

    def far_skip(si, modefn):
        sl = SLOPES[si]

        def f(kt, j):
            rel = kt - 2 * j
            if rel < 0 and sl * 128.0 * (-rel - 1) >= 110.0:
                return "skip"
            return modefn(kt, j)
        return f

    mixers = only if only is not None else ("mla", "diff", "nsa", "sb")

    if "mla" in mixers:
        with ExitStack() as ms:
            cm = common(ms)
            aq = [k.sb(ms, f"aq{i}", SL, BF16) for i in range(2)]
            aqr = [k.sb(ms, f"aqr{i}", SL, BF16) for i in range(2)]
            ak = [k.sb(ms, f"ak{i}", S, BF16) for i in range(2)]
            akr = k.sb(ms, "akr", S, BF16)
            av = [k.sb(ms, f"av{i}", 32 * 132, BF16) for i in range(2)]
            k.dma("sp", cm["gbc"][:], W["mla_o_norm"][l].partition_broadcast(128), key=cm["gbc"], w=[cm["gbc"]])
            load_k(akr, 4, 64)
            for i in range(2):
                k.op("dve", "memset", av[i][:], 1.0, w=[av[i]])
            for h in range(4):
                q, qr, kk, v = aq[h % 2], aqr[h % 2], ak[h % 2], av[h % 2]
                k.dma("sp", q[:], QT[h], key=q, w=[q])
                k.dma("sp", qr[0:64, :], QT[4 + h][0:64, :], key=qr, w=[qr])
                load_k(kk, h)
                vv = load_v(v, h)
                pairs = [(lambda kt, kk=kk: kk[:, kt * 128:(kt + 1) * 128], lambda a, b, q=q: q[:, a:b]),
                         (lambda kt: akr[0:64, kt * 128:(kt + 1) * 128], lambda a, b, qr=qr: qr[0:64, a:b])]
                for g in range(NG):
                    onb = cm["on"][0]
                    soft_sub(cm, g, pairs, lambda kt, vv=vv: vv[:, kt, 0:129], 129, range(0, 8 * g + 8), causal_mode,
                             None, std_mask, fin_norm(cm, onb), [q, qr, kk, akr, v])
                    finish_head(cm, h, g, onb, 1.0)
            k.flush()

    if "diff" in mixers:
        with ExitStack() as ms:
            cm = common(ms)
            aq = [k.sb(ms, f"dq{i}", SL, BF16) for i in range(2)]
            ak = [k.sb(ms, f"dk{i}", S, BF16) for i in range(2)]
            av = [k.sb(ms, f"dv{i}", 32 * 132, BF16) for i in range(2)]
            lt = k.sb(ms, "lt", 256, F32)
            ltmp = k.sb(ms, "ltmp", 64, F32)
            ls = k.sb(ms, "ls", 8, F32)
            k.dma("sp", cm["gbc"][:], W["diff_subln"][l].partition_broadcast(128), key=cm["gbc"], w=[cm["gbc"]])
            for i, nm in enumerate(["diff_lq1", "diff_lk1", "diff_lq2", "diff_lk2"]):
                k.dma("sp", lt[:, i * 64:(i + 1) * 64], W[nm][l].partition_broadcast(128), key=lt, w=[lt])
            for j in range(2):
                k.op("dve", "tensor_tensor", out=ltmp[:], in0=lt[:, j * 128:j * 128 + 64], in1=lt[:, j * 128 + 64:j * 128 + 128],
                     op=ALU.mult, r=[lt], w=[ltmp])
                k.op("dve", "reduce_sum", out=ls[:, j:j + 1], in_=ltmp[:], axis=AX.X, r=[ltmp], w=[ls])
                k.op("act", "activation", out=ls[:, 2 + j:3 + j], in_=ls[:, j:j + 1], func=AF.Exp, r=[ls], w=[ls])
            k.op("dve", "tensor_tensor", out=ls[:, 4:5], in0=ls[:, 3:4], in1=ls[:, 2:3], op=ALU.subtract, r=[ls], w=[ls])
            k.op("dve", "tensor_scalar", out=ls[:, 5:6], in0=ls[:, 4:5], scalar1=float(-lam_init), scalar2=None, op0=ALU.add,
                 r=[ls], w=[ls])
            for i in range(2):
                k.op("dve", "memset", av[i][:], 1.0, w=[av[i]])
            for h in range(4):
                q, kk, v = aq[h % 2], ak[h % 2], av[h % 2]
                k.dma("sp", q[:], QT[8 + h], key=q, w=[q])
                load_k(kk, 5 + h)
                vv = load_v(v, 4 + h)
                for g in range(NG):
                    for sub in range(2):
                        r0, r1 = sub * 64, (sub + 1) * 64
                        pairs = [(lambda kt, kk=kk, r0=r0, r1=r1: kk[r0:r1, kt * 128:(kt + 1) * 128],
                                  lambda a, b, q=q, r0=r0, r1=r1: q[r0:r1, a:b])]
                        soft_sub(cm, g, pairs, lambda kt, vv=vv: vv[:, kt, 0:129], 129, range(0, 8 * g + 8),
                                 far_skip(2 * h, causal_mode), ab_bias(2 * h), std_mask, fin_norm(cm, cm["on"][sub]),
                                 [q, kk, v], nchunk=ab_chunks(2 * h))
                    on0, on1 = cm["on"][0], cm["on"][1]
                    k.op("dve", "scalar_tensor_tensor", out=on0[:], in0=on1[:], scalar=ls[:, 5:6], in1=on0[:],
                         op0=ALU.mult, op1=ALU.add, r=[on1, ls], w=[on0])
                    finish_head(cm, 4 + h, g, on0, 1.0 - lam_init)
            k.flush()

    if "nsa" in mixers:
        with ExitStack() as ms:
            cm = common(ms)
            nq = [k.sb(ms, f"nq{i}", SL, BF16) for i in range(4)]
            nks = k.sb(ms, "nks", S, BF16)
            nkw = k.sb(ms, "nkw", S, BF16)
            nvs = k.sb(ms, "nvs", 32 * 132, BF16)
            nvw = k.sb(ms, "nvw", 32 * 132, BF16)
            raw = [k.sb(ms, f"raw{i}", S, BF16) for i in range(2)]
            wc = [k.sb(ms, f"wc{i}", 32 * 128, BF16) for i in range(2)]
            kcT = k.sb(ms, "kcT", 256, BF16)
            vcx = k.sb(ms, "vcx", 2 * 196, BF16)
            gts = k.sb(ms, "gts", NQB * 12, F32)
            bonus = k.sb(ms, "bonus", NQB * 64, F32)
            maskT = k.sb(ms, "maskT", 4 * 32 * 128, BF16)
            selexp = k.sb(ms, "selexp", 4096, BF16)
            selb = k.sb(ms, "selb", 64, BF16)
            scb = k.sb(ms, "scb", 64, F32)
            sc2 = k.sb(ms, "sc2", 64, F32)
            imp = [k.sb(ms, f"imp{i}", 64, F32) for i in range(4)]
            m8 = k.sb(ms, "m8", 16, F32)
            oc = k.sb(ms, "oc", 4 * 4 * 128, F32)
            pe_f = k.sb(ms, "pe_f", 128, F32)
            pe_b = k.sb(ms, "pe_b", 128, BF16)
            peT = k.sb(ms, "peT", 32, BF16)
            crow = k.sb(ms, "crow", 128, BF16)
            kcn = k.sb(ms, "kcn", 128, BF16)
            gcn = k.sb(ms, "gcn", NGC, F32)
            sm = cm["sm"]
            k.dma("sp", cm["gbc"][:], W["nsa_o_norm"][l].partition_broadcast(128), key=cm["gbc"], w=[cm["gbc"]])
            k.dma("sp", gcn[:], W["gcols"][l], key=gcn, w=[gcn])
            k.dma("sp", bonus[:], W["bonus"], key=bonus, w=[bonus])
            k.dma("sp", gts[:].rearrange("p (b n) -> p b n", n=12), GT.rearrange("(b p) n -> p b n", p=128), key=gts, w=[gts])
            for h in range(4):
                k.dma("sp", nq[h][:], QT[12 + h], key=nq[h], w=[nq[h]])
            load_k(nks, 11)
            load_k(nkw, 12)
            k.op("dve", "memset", nvs[:], 1.0, w=[nvs])
            k.op("dve", "memset", nvw[:], 1.0, w=[nvw])
            vsv = load_v(nvs, 8)
            vwv = load_v(nvw, 9)
            load_k(raw[0], 9)
            load_k(raw[1], 10)
            k.dma("pool", wc[0][:].rearrange("p (l n) -> p l n", n=128), W["nsa_w_ck"][l].rearrange("(l p) n -> p l n", p=128),
                  key=wc[0], w=[wc[0]])
            k.dma("pool", wc[1][:].rearrange("p (l n) -> p l n", n=128), W["nsa_w_cv"][l].rearrange("(l p) n -> p l n", p=128),
                  key=wc[1], w=[wc[1]])
            k.op("dve", "memset", kcT[:], 0.0, w=[kcT])
            k.op("dve", "memset", vcx[:], 0.0, w=[vcx])
            vcv = vcx[:].rearrange("p (t c) -> p t c", c=196)
            k.op("dve", "memset", vcv[:, :, 128:129], 1.0, w=[vcx])
            k.op("dve", "tensor_copy", out=vcv[:, :, 129:193], in_=cb("ov").rearrange("p (t c) -> p t c", c=64), r=[cbf], w=[vcx])
            for which in range(2):
                pen = "nsa_pe_k" if which == 0 else "nsa_pe_v"
                wv = wc[which][:].rearrange("p (l n) -> p l n", n=128)
                k.dma("sp", pe_f[0:32, :], W[pen][l], key=pe_f, w=[pe_f])
                k.op("act", "copy", out=pe_b[0:32, :], in_=pe_f[0:32, :], r=[pe_f], w=[pe_b])
                k.op("pe", "transpose", out=cm["PT"][:, 0:32], in_=pe_b[0:32, :], identity=cb("ident", 32, 32),
                     r=[pe_b, cbf], w=[cm["PT"]])
                k.op("act", "copy", out=peT[:], in_=cm["PT"][:, 0:32], r=[cm["PT"]], w=[peT])
                Oc = cm["O"][0]
                for li in range(32):
                    k.op("pe", "matmul", Oc[0:1, 0:128], lhsT=peT[:, li:li + 1], rhs=wv[:, li, :], start=(li == 0),
                         stop=(li == 31), r=[peT, wc[which]], w=[Oc])
                k.op("act", "copy", out=crow[0:1, :], in_=Oc[0:1, 0:128], r=[Oc], w=[crow])
                for nt in range(2):
                    nr = 128 if nt == 0 else 127
                    Ok = cm["O"][1 + nt]
                    for li in range(32):
                        a0 = nt * 2048 + li
                        k.op("pe", "matmul", Ok[0:nr, 0:128], lhsT=raw[which][:, a0:a0 + 16 * (nr - 1) + 1:16], rhs=wv[:, li, :],
                             start=(li == 0), stop=False, r=[raw[which], wc[which]], w=[Ok])
                    k.op("pe", "matmul", Ok[0:nr, 0:128], lhsT=cb("ones", 1, nr), rhs=crow[0:1, :], start=False, stop=True,
                         r=[crow, cbf], w=[Ok])
                    if which == 0:
                        k.op("dve", "memset", sm[:, 0:1], 0.0, w=[sm])
                        k.op("act", "activation", out=cm["junk"][0:nr, :], in_=Ok[0:nr, 0:128], func=AF.Square,
                             accum_out=sm[0:nr, 0:1], r=[Ok], w=[cm["junk"], sm])
                        k.op("act", "activation", out=sm[0:nr, 1:2], in_=sm[0:nr, 0:1], func=AF.Sqrt, scale=1.0 / 128, bias=EPS,
                             r=[sm], w=[sm])
                        k.op("dve", "reciprocal", out=sm[0:nr, 1:2], in_=sm[0:nr, 1:2], r=[sm], w=[sm])
                        k.op("dve", "memset", kcn[:], 0.0, w=[kcn])
                        k.op("dve", "tensor_scalar", out=kcn[0:nr, :], in0=Ok[0:nr, 0:128], scalar1=sm[0:nr, 1:2], scalar2=None,
                             op0=ALU.mult, r=[Ok, sm], w=[kcn])
                        k.op("pe", "transpose", out=cm["PT"][:, 128:256], in_=kcn[:], identity=cb("ident"), r=[kcn, cbf],
                             w=[cm["PT"]])
                        k.op("dve", "tensor_scalar", out=kcT[:, nt * 128:(nt + 1) * 128], in0=cm["PT"][:, 128:256],
                             scalar1=gcn[:, GC["nkc"]:GC["nkc"] + 1], scalar2=None, op0=ALU.mult, r=[cm["PT"], gcn], w=[kcT])
                    else:
                        k.op("act", "copy", out=vcv[0:nr, nt, 0:128], in_=Ok[0:nr, 0:128], r=[Ok], w=[vcx])
            ocv = oc[:].rearrange("p (q h d) -> p q h d", q=4, h=4)
            mkv = maskT[:].rearrange("p (q t d) -> p q t d", q=4, t=32)

            def cmp_mode(nt, j):
                if nt == 0:
                    return "full" if j >= 9 else ("cm", j)
                return "skip" if j < 8 else ("cm", 9 + j - 8)

            def cmp_mask(m, kt, qb):
                return cb("cm", 128, 128, m[1] * 128), [cbf]

            def sel_mode(kt, j):
                return "skip" if kt - 2 * j > 1 else "m"

            def win_mode(kt, j):
                e = kt - 2 * j
                if e > 1 or e < -4:
                    return "skip"
                if e in (-2, -1):
                    return "full"
                return f"w{e}"

            for g in range(NG):
                for h in range(4):
                    def fin_cmp(qb, Ob, h=h):
                        k.op("dve", "tensor_scalar", out=sm[:, qb:qb + 1], in0=Ob[:, 128:129], scalar1=1e-37, scalar2=None,
                             op0=ALU.max, r=[Ob], w=[sm])
                        k.op("dve", "reciprocal", out=sm[:, qb:qb + 1], in_=sm[:, qb:qb + 1], r=[sm], w=[sm])
                        k.op("dve", "tensor_scalar", out=ocv[:, qb, h, :], in0=Ob[:, 0:128], scalar1=sm[:, qb:qb + 1],
                             scalar2=None, op0=ALU.mult, r=[Ob, sm], w=[oc])
                        if h == 0:
                            k.op("dve", "tensor_scalar", out=imp[qb][:], in0=Ob[:, 129:193], scalar1=sm[:, qb:qb + 1],
                                 scalar2=None, op0=ALU.mult, r=[Ob, sm], w=[imp[qb]])
                        else:
                            k.op("dve", "scalar_tensor_tensor", out=imp[qb][:], in0=Ob[:, 129:193], scalar=sm[:, qb:qb + 1],
                                 in1=imp[qb][:], op0=ALU.mult, op1=ALU.add, r=[Ob, sm], w=[imp[qb]])
                    pairs = [(lambda nt: kcT[:, nt * 128:(nt + 1) * 128], lambda a, b, h=h: nq[h][:, a:b])]
                    cbias = lambda nt, j, h=h: cf[:, 512 + h * 32 + nt * 16 + j: 512 + h * 32 + nt * 16 + j + 1]
                    soft_sub(cm, g, pairs, lambda nt: vcv[:, nt, 0:193], 193, range(2), cmp_mode, cbias, cmp_mask, fin_cmp,
                             [kcT, nq[h], vcx])
                for qb in range(4):
                    qbg = 4 * g + qb
                    k.op("dve", "tensor_tensor", out=scb[:], in0=imp[qb][:], in1=bonus[:, qbg * 64:(qbg + 1) * 64], op=ALU.add,
                         r=[imp[qb], bonus], w=[scb])
                    k.op("dve", "max", out=m8[:, 0:8], in_=scb[:], r=[scb], w=[m8])
                    k.op("dve", "match_replace", out=sc2[:], in_to_replace=m8[:, 0:8], in_values=scb[:], imm_value=-3.0e38,
                         r=[m8, scb], w=[sc2])
                    k.op("dve", "max", out=m8[:, 8:16], in_=sc2[:], r=[sc2], w=[m8])
                    k.op("dve", "tensor_scalar", out=sm[:, 12:13], in0=m8[:, 15:16], scalar1=-1e29, scalar2=None, op0=ALU.max,
                         r=[m8], w=[sm])
                    k.op("dve", "tensor_scalar", out=selb[:], in0=scb[:], scalar1=sm[:, 12:13], scalar2=None, op0=ALU.is_ge,
                         r=[scb, sm], w=[selb])
                    k.op("dve", "tensor_copy", out=selexp[:].rearrange("p (j s) -> p j s", s=64),
                         in_=selb[:, 0:64].unsqueeze(2).to_broadcast([128, 64, 64]), r=[selb], w=[selexp])
                    for kt0 in range(0, 2 * qbg + 2, 4):
                        n = min(4, 2 * qbg + 2 - kt0)
                        Sb = cm["S"][cm["cnt"]["s"] % 3]
                        cm["cnt"]["s"] += 1
                        for j in range(n):
                            k.op("pe", "matmul", Sb[:, j * 128:(j + 1) * 128], lhsT=selexp[:, (kt0 + j) * 128:(kt0 + j + 1) * 128],
                                 rhs=cb("ident"), start=True, stop=True, r=[selexp, cbf], w=[Sb])
                        k.op("act", "copy", out=mkv[:, qb, kt0:kt0 + n, :], in_=Sb[:, 0:n * 128].rearrange("p (t d) -> p t d", d=128),
                             r=[Sb], w=[maskT])
                    k.op("dve", "tensor_tensor", out=mkv[:, qb, 2 * qbg, :], in0=mkv[:, qb, 2 * qbg, :], in1=cb("m0"), op=ALU.mult,
                         r=[cbf], w=[maskT])
                    k.op("dve", "tensor_tensor", out=mkv[:, qb, 2 * qbg + 1, :], in0=mkv[:, qb, 2 * qbg + 1, :], in1=cb("m1"),
                         op=ALU.mult, r=[cbf], w=[maskT])
                for h in range(4):
                    si = 2 * h + 1
                    pairs = [(lambda kt: nks[:, kt * 128:(kt + 1) * 128], lambda a, b, h=h: nq[h][:, a:b])]
                    soft_sub(cm, g, pairs, lambda kt: vsv[:, kt, 0:129], 129, range(0, 8 * g + 8), far_skip(si, sel_mode),
                             ab_bias(si), lambda m, kt, qb: (mkv[:, qb, kt, :], [maskT]), fin_norm(cm, cm["on"][0]),
                             [nks, nq[h], nvs], nchunk=ab_chunks(si))
                    pairs = [(lambda kt: nkw[:, kt * 128:(kt + 1) * 128], lambda a, b, h=h: nq[h][:, a:b])]
                    soft_sub(cm, g, pairs, lambda kt: vwv[:, kt, 0:129], 129, range(max(0, 8 * g - 4), 8 * g + 8),
                             far_skip(si, win_mode), ab_bias(si), std_mask, fin_norm(cm, cm["on"][1]), [nkw, nq[h], nvw],
                             nchunk=ab_chunks(si))
                    onc = cm["on"][2]
                    for qb in range(4):
                        qbg = 4 * g + qb
                        gcol = lambda i: gts[:, qbg * 12 + 3 * h + i: qbg * 12 + 3 * h + i + 1]
                        dst = onc[:, qb * 128:(qb + 1) * 128]
                        k.op("dve", "tensor_scalar", out=dst, in0=ocv[:, qb, h, :], scalar1=gcol(0), scalar2=None, op0=ALU.mult,
                             r=[oc, gts], w=[onc])
                        k.op("dve", "scalar_tensor_tensor", out=dst, in0=cm["on"][0][:, qb * 128:(qb + 1) * 128], scalar=gcol(1),
                             in1=dst, op0=ALU.mult, op1=ALU.add, r=[cm["on"][0], gts], w=[onc])
                        k.op("dve", "scalar_tensor_tensor", out=dst, in0=cm["on"][1][:, qb * 128:(qb + 1) * 128], scalar=gcol(2),
                             in1=dst, op0=ALU.mult, op1=ALU.add, r=[cm["on"][1], gts], w=[onc])
                    finish_head(cm, 8 + h, g, onc, 1.0)
            k.flush()

    if "sb" in mixers:
        with ExitStack() as ms:
            cm = common(ms)
            aq = [k.sb(ms, f"sq{i}", SL, BF16) for i in range(2)]
            ak = [k.sb(ms, f"sk{i}", S, BF16) for i in range(2)]
            av = [k.sb(ms, f"sv{i}", 32 * 132, BF16) for i in range(2)]
            Ef = [k.sb(ms, f"Ef{i}", 512, F32) for i in range(2)]
            Lb = [k.sb(ms, f"Lb{i}", 512, BF16) for i in range(3)]
            Lf = k.sb(ms, "Lf", 512, F32)
            Lsb = [k.sb(ms, f"Lsb{i}", 512, BF16) for i in range(3)]
            k.dma("sp", cm["gbc"][:], W["sb_o_norm"][l].partition_broadcast(128), key=cm["gbc"], w=[cm["gbc"]])
            step = [0]
            for h in range(4):
                q, kk, v = aq[h % 2], ak[h % 2], av[h % 2]
                k.dma("sp", q[:], QT[16 + h], key=q, w=[q])
                load_k(kk, 13 + h)
                vv = load_v(v, 10 + h)
                for g in range(NG):
                    k.op("dve", "memset", Lf[:], 0.0, w=[Lf])
                    ktl = list(range(8 * g + 7, -1, -1))

                    def stage_a(idx, kt, g=g, q=q, kk=kk):
                        st = {}
                        lo = max(0, (kt - 8 * g) // 2)
                        c0, c1 = lo * 128, 512
                        sidx = step[0]
                        step[0] += 1
                        Sb = cm["S"][sidx % 2]
                        E, L = Ef[sidx % 2], Lb[sidx % 3]
                        st.update(kt=kt, idx=idx, lo=lo, c0=c0, c1=c1, L=L, P=cm["pt"][sidx % 3],
                                  Ls_prev=Lsb[(sidx + 2) % 3], Ls_new=Lsb[sidx % 3])
                        lhs = kk[:, kt * 128:(kt + 1) * 128]
                        rhs = q[:, g * 512 + c0: g * 512 + c1]
                        st["lhs"], st["rhs"] = lhs, rhs
                        k.op("pe", "matmul", Sb[:, c0:c1], lhsT=lhs, rhs=rhs, start=True, stop=True, r=[kk, q], w=[Sb])
                        k.op("act", "activation", out=E[:, c0:c1], in_=Sb[:, c0:c1], func=AF.Exp, r=[Sb], w=[E])
                        k.op("act", "activation", out=L[:, c0:c1], in_=E[:, c0:c1], func=AF.Ln, bias=1.0, r=[E], w=[L])
                        if kt >= 8 * g:
                            k.op("dve", "tensor_tensor", out=L[:, c0:c0 + 128], in0=L[:, c0:c0 + 128],
                                 in1=cb("ms0" if kt % 2 == 0 else "ms1"), op=ALU.mult, r=[cbf], w=[L])
                        if kt > 0:
                            k.op("dve", "tensor_tensor", out=Lf[:, c0:c1], in0=Lf[:, c0:c1], in1=L[:, c0:c1], op=ALU.add,
                                 r=[L], w=[Lf])
                            k.op("dve", "tensor_copy", out=st["Ls_new"][:], in_=Lf[:], r=[Lf], w=[st["Ls_new"]])
                        return st

                    def stage_b(st, g=g, q=q, kk=kk, v=v, vv=vv):
                        kt, idx, lo, c0, c1, L, P = st["kt"], st["idx"], st["lo"], st["c0"], st["c1"], st["L"], st["P"]
                        Cb = cm["S"][2]
                        k.op("pe", "matmul", Cb[:, c0:c1], lhsT=cb("ntri"), rhs=L[:, c0:c1], start=True, stop=False,
                             r=[L, cbf], w=[Cb])
                        if idx > 0:
                            k.op("pe", "matmul", Cb[:, c0:c1], lhsT=cb("nones"), rhs=st["Ls_prev"][:, c0:c1], start=False, stop=False,
                                 r=[st["Ls_prev"], cbf], w=[Cb])
                        k.op("pe", "matmul", Cb[:, c0:c1], lhsT=st["lhs"], rhs=st["rhs"], start=False, stop=True, r=[kk, q], w=[Cb])
                        k.op("act", "activation", out=P[:, c0:c1], in_=Cb[:, c0:c1], func=AF.Exp, r=[Cb], w=[P])
                        if kt >= 8 * g:
                            k.op("dve", "tensor_tensor", out=P[:, c0:c0 + 128], in0=P[:, c0:c0 + 128],
                                 in1=cb("ms0" if kt % 2 == 0 else "ms1"), op=ALU.mult, r=[cbf], w=[P])
                        for qb in range(lo, 4):
                            k.op("pe", "matmul", cm["O"][qb][:, 0:128], lhsT=P[:, qb * 128:(qb + 1) * 128], rhs=vv[:, kt, 0:128],
                                 start=(kt == 2 * (4 * g + qb) + 1), stop=(kt == 0), r=[P, v], w=[cm["O"][qb]])

                    prev = None
                    for idx, kt in enumerate(ktl):
                        st = stage_a(idx, kt)
                        if prev is not None:
                            stage_b(prev)
                        prev = st
                    stage_b(prev)
                    onb = cm["on"][0]
                    for qb in range(4):
                        k.op("act", "copy", out=onb[:, qb * 128:(qb + 1) * 128], in_=cm["O"][qb][:, 0:128], r=[cm["O"][qb]], w=[onb])
                    finish_head(cm, 12 + h, g, onb, 1.0)
            k.flush()


_CACHE = {}


def _host_inputs(inputs, C):
    common = {}
    for nm in ["ffn1_w_gate", "ffn1_w_up", "ffn1_w_down", "ffn2_w_gate", "ffn2_w_up", "ffn2_w_down", "w_in", "w_out",
               "mla_w_uq", "mla_w_ukv", "nsa_w_ck", "nsa_w_cv", "nsa_pe_k", "nsa_pe_v", "ffn1_norm", "mix_norm",
               "ffn2_norm", "mla_o_norm", "diff_subln", "nsa_o_norm", "sb_o_norm", "diff_lq1", "diff_lk1", "diff_lq2",
               "diff_lk2"]:
        common[nm] = np.ascontiguousarray(np.asarray(inputs[nm], np.float32))
    common["gcols"] = np.stack([_gain_cols(inputs, l) for l in range(DEPTH)], axis=0).astype(np.float32)
    common["cbf"] = C["cbf"]
    common["cf32"] = C["cf32"]
    common["cosT"] = C["cosT"]
    common["sinT"] = C["sinT"]
    common["bonus"] = C["bonus"]
    return common


def _rows(r):
    j = np.arange(NQB)
    return ((2 * j[:, None] + r) * 128 + np.arange(128)[None, :]).reshape(-1)


def kernel(**inputs):
    Cs = [_consts(0), _consts(1)]
    inputs = {kk: np.asarray(v) for kk, v in inputs.items()}
    _gain_cols(inputs, 0)
    nc = build_program(Cs[0])
    commons = [_host_inputs(inputs, Cs[r]) for r in range(2)]
    x = np.asarray(inputs["x"], np.float32)
    in_maps = []
    for c in range(8):
        b, r = c // 2, c % 2
        m = dict(commons[r])
        m["x"] = np.ascontiguousarray(x[b][_rows(r)])
        in_maps.append(m)
    res = run_bass_kernel_spmd(nc, in_maps, core_ids=list(range(8)))
    out = np.empty((4, S, D), np.float32)
    for c in range(8):
        b, r = c // 2, c % 2
        out[b, _rows(r)] = res.results[c]["y"]
    return out
```

```python
import math
from contextlib import ExitStack

import numpy as np
import ml_dtypes

import concourse.bass as bass
import concourse.mybir as mybir
from concourse.bass_utils import run_bass_kernel_spmd

F32 = mybir.dt.float32
BF16 = mybir.dt.bfloat16
AF = mybir.ActivationFunctionType
ALU = mybir.AluOpType
AX = mybir.AxisListType

D = 2048
S = 4096
DEPTH = 2
DFF = 5632
NFC = DFF // 128
DIN = 5196
SL = 2048
NQB = SL // 128
T = 512
NTT = SL // T
EPS = 1e-6
NSLOPE = 8
SLOPES = [2.0 ** (-(i + 1)) for i in range(8)]
DIFF_SLOPES = SLOPES[0::2]
NSA_SLOPES = SLOPES[1::2]

DEBUG = None
ONLY = None
PRECONV = False


class Buf:
    __slots__ = ("t", "name", "lw", "rd", "sem", "ndma")

    def __init__(self, t, name):
        self.t = t
        self.name = name
        self.lw = None
        self.rd = []
        self.sem = None
        self.ndma = 0

    def __getitem__(self, key):
        return self.t[key]


class Op:
    __slots__ = ("eng", "dma", "calls", "deps", "inc", "count", "sem", "key")

    def __init__(self, eng, dma, calls, key=None):
        self.eng = eng
        self.dma = dma
        self.calls = calls
        self.deps = set()
        self.inc = False
        self.count = 0
        self.sem = None
        self.key = key


class KB:
    CENG = ("pe", "act", "dve", "pool")

    def __init__(self, nc, es):
        self.nc = nc
        self.es = es
        self.engs = {"pe": nc.tensor, "act": nc.scalar, "dve": nc.vector, "pool": nc.gpsimd, "sp": nc.sync}
        self.ops = []
        self.base = 0
        self.csem = {e: es.enter_context(nc.semaphore("cs_" + e)) for e in self.CENG}
        self.ccount = {e: 0 for e in self.CENG}
        self.seen = {e: {} for e in self.engs}
        self.keys = []
        self.bufs = []
        self.n_ins = 0

    def sb(self, es, name, cols, dtype):
        self.uid = getattr(self, "uid", 0) + 1
        name = f"{name}_{self.uid}"
        t = es.enter_context(self.nc.sbuf_tensor(name, [128, cols], dtype))
        b = Buf(t, name)
        self.bufs.append(b)
        return b

    def ps(self, es, name, cols, dtype=F32):
        self.uid = getattr(self, "uid", 0) + 1
        name = f"{name}_{self.uid}"
        t = es.enter_context(self.nc.psum_tensor(name, [128, cols], dtype))
        b = Buf(t, name)
        self.bufs.append(b)
        return b

    def _track(self, op, idx, r, w):
        for b in w:
            if b.lw is not None:
                op.deps.add(b.lw)
            op.deps.update(b.rd)
        for b in r:
            if b.lw is not None:
                op.deps.add(b.lw)
        for b in w:
            b.lw = idx
            b.rd = []
        for b in r:
            if b not in w:
                b.rd.append(idx)
                if len(b.rd) > 64:
                    keep = {}
                    rest = []
                    for i in b.rd:
                        o = self.ops[i]
                        if o.dma:
                            rest.append(i)
                        else:
                            keep[o.eng] = i
                    b.rd = rest + list(keep.values())

    def op(self, eng, method, *args, r=(), w=(), **kw):
        o = Op(eng, False, [(method, args, kw)])
        idx = len(self.ops)
        self.ops.append(o)
        self._track(o, idx, r, w)
        return idx

    def dma(self, eng, out, in_, key, r=(), w=()):
        o = Op(eng, True, [("dma_start", (), {"out": out, "in_": in_})], key=key)
        idx = len(self.ops)
        self.ops.append(o)
        self._track(o, idx, r, w)
        return idx

    def flush(self, final=False):
        nc = self.nc
        ops = self.ops
        n = len(ops)
        for i in range(self.base, n):
            o = ops[i]
            for d in o.deps:
                if d >= self.base:
                    od = ops[d]
                    if not (od.eng == "pe" and o.eng == "pe" and not od.dma and not o.dma):
                        od.inc = True
        last = {}
        for i in range(self.base, n):
            o = ops[i]
            if not o.dma:
                last[o.eng] = i
        for e, i in last.items():
            ops[i].inc = True
        for i in range(self.base, n):
            o = ops[i]
            if o.dma:
                kb = o.key
                if kb.sem is None:
                    pool = self.__dict__.setdefault("sempool", [])
                    if pool:
                        kb.sem, kb.ndma = pool.pop()
                    else:
                        kb.sem = self.es.enter_context(nc.semaphore("ds_" + kb.name))
                        kb.ndma = 0
                    self.keys.append(kb)
                kb.ndma += 1
                o.sem = kb.sem
                o.count = 16 * kb.ndma
            else:
                if o.inc:
                    self.ccount[o.eng] += 1
                o.sem = self.csem[o.eng]
                o.count = self.ccount[o.eng]
        for i in range(self.base, n):
            o = ops[i]
            E = self.engs[o.eng]
            seen = self.seen[o.eng]
            need = {}
            for d in o.deps:
                if d < self.base:
                    continue
                od = ops[d]
                if od.eng == "pe" and o.eng == "pe" and not od.dma and not o.dma:
                    continue
                if not od.dma and not od.inc:
                    raise RuntimeError("dep without inc")
                key = od.sem
                if need.get(key, (None, 0))[1] < od.count:
                    need[key] = (od.sem, od.count)
            for key, (sem, cnt) in need.items():
                if seen.get(key, 0) < cnt:
                    E.wait_ge(sem, cnt)
                    seen[key] = cnt
                    self.n_ins += 1
            ins = None
            for (m, a, kw) in o.calls:
                ins = getattr(E, m)(*a, **kw)
                self.n_ins += 1
            if o.dma:
                ins.then_inc(o.sem, 16)
            elif o.inc:
                ins.then_inc(o.sem, 1)
        targets = [(self.csem[e], self.ccount[e]) for e in self.CENG if self.ccount[e] > 0]
        targets += [(kb.sem, 16 * kb.ndma) for kb in self.keys]
        wait_engs = ["sp"] if final else list(self.engs.keys())
        for e in wait_engs:
            E = self.engs[e]
            seen = self.seen[e]
            for (sem, cnt) in targets:
                if seen.get(sem, 0) < cnt:
                    E.wait_ge(sem, cnt)
                    seen[sem] = cnt
                    self.n_ins += 1
        self.base = n
        pool = self.__dict__.setdefault("sempool", [])
        for kb in self.keys:
            pool.append((kb.sem, kb.ndma))
            kb.sem = None
        self.keys = []
        for b in self.bufs:
            b.lw = None
            b.rd = []


def _consts(r=0):
    c = {}
    p = np.arange(128)
    ident = np.eye(128, dtype=np.float32)
    ones = np.ones((128, 128), np.float32)
    zeros = np.zeros((128, 128), np.float32)
    blk64 = np.kron(np.eye(2), np.ones((64, 64))).astype(np.float32)
    tri = (p[:, None] <= p[None, :]).astype(np.float32)
    tris = (p[:, None] < p[None, :]).astype(np.float32)
    atri = (p[:, None] > p[None, :]).astype(np.float32)
    ntri_incl = -(p[:, None] >= p[None, :]).astype(np.float32)
    nones = -ones
    prot = np.zeros((128, 128), np.float32)
    for m in range(32):
        prot[m + 32, m] = -1.0
    for m in range(32, 64):
        prot[m - 32, m] = 1.0
    if r == 0:
        m0, m1, ms0, ms1 = tri, zeros, tris, zeros
        wm = {-4: atri, -3: ones, 0: tri, 1: zeros}
    else:
        m0, m1, ms0, ms1 = ones, tri, ones, tris
        wm = {-4: zeros, -3: atri, 0: ones, 1: tri}
    cm = []
    for (nt, j) in [(0, j) for j in range(9)] + [(1, j) for j in range(8, 16)]:
        G = 2 * j + r
        cm.append(((16 * p[:, None] + 31 - p[None, :]) <= 128 * G - 2048 * nt).astype(np.float32))
    cm = np.concatenate(cm, axis=1)
    n = np.arange(256)
    cst = 16 * n
    sel_start = 64 * np.arange(64)
    ov = ((cst[:, None] < sel_start[None, :] + 64) & (cst[:, None] + 32 > sel_start[None, :])).astype(np.float32)
    ov[255] = 0
    ov2 = ov.reshape(2, 128, 64).transpose(1, 0, 2)
    parts = [("ident", ident), ("ones", ones), ("blk64", blk64), ("tri", tri), ("tris", tris), ("atri", atri),
             ("ntri", ntri_incl), ("nones", nones), ("prot", prot), ("m0", m0), ("m1", m1), ("ms0", ms0), ("ms1", ms1),
             ("w-4", wm[-4]), ("w-3", wm[-3]), ("w0", wm[0]), ("w1", wm[1]), ("cm", cm), ("ov", ov2.reshape(128, -1))]
    c["cbf"] = np.concatenate([a for _, a in parts], axis=1).astype(ml_dtypes.bfloat16)
    off = {}
    o = 0
    for name, a in parts:
        off[name] = o
        o += a.shape[1]
    c["cbf_off"] = off
    c["cbf_w"] = o
    jl = np.arange(NQB)
    posl = ((2 * jl[:, None] + r) * 128 + p[None, :]).reshape(-1).astype(np.float32)
    inv_freq = (10000.0 ** (-np.arange(0, 64, 2, dtype=np.float32) / 64)).astype(np.float32)
    ang = posl[:, None] * inv_freq[None, :]
    c["cosT"] = np.ascontiguousarray(np.concatenate([np.cos(ang), np.cos(ang)], axis=1).T).astype(np.float32)
    c["sinT"] = np.ascontiguousarray(np.concatenate([np.sin(ang), np.sin(ang)], axis=1).T).astype(np.float32)
    e = np.arange(64) - 62
    base = 128.0 * (e[None, :] - r) + p[:, None] - 127.0
    ab = np.stack([sl * base for sl in SLOPES], axis=1)
    nt = np.arange(2)
    G = 2 * jl + r
    cbase = 16.0 * (128 * nt[None, :, None] + p[:, None, None]) + 31 - 128.0 * G[None, None, :] - 127.0
    cbias = np.stack([sl * cbase for sl in NSA_SLOPES], axis=1)
    cbias = np.minimum(cbias, 60.0)
    t = (128 * G[None, :] + p[:, None])
    jb = np.arange(64)
    cur = t // 64
    valid = sel_start[None, None, :] <= t[:, :, None]
    forced = (jb[None, None, :] == 0) | (jb[None, None, :] == cur[:, :, None]) | (jb[None, None, :] == cur[:, :, None] - 1)
    bonus = np.where(valid, np.where(forced, 1e4, 0.0), -1e30).astype(np.float32)
    cf = np.concatenate([ab.reshape(128, -1), cbias.reshape(128, -1)], axis=1)
    c["cf32"] = cf.astype(np.float32)
    c["bonus"] = bonus.reshape(128, -1).astype(np.float32)
    c["cf_off"] = {"ab": 0, "cbias": 512}
    c["cf_w"] = cf.shape[1]
    return c


GC = {}


def _gain_cols(inputs, l):
    cols = []

    def add(name, v):
        GC[name] = len(cols)
        cols.append(np.asarray(v, np.float32).reshape(128))

    for c in range(4):
        add(f"cq{c}", inputs["mla_cq_norm"][l][c * 128:(c + 1) * 128])
    for c in range(2):
        add(f"ckv{c}", inputs["mla_ckv_norm"][l][c * 128:(c + 1) * 128])
    add("qn", inputs["mla_qn_norm"][l])
    add("qr", np.tile(inputs["mla_qr_norm"][l], 2))
    add("kn", inputs["mla_kn_norm"][l])
    add("kr", np.tile(inputs["mla_kr_norm"][l], 2))
    add("dq", np.tile(inputs["diff_q_norm"][l], 2))
    add("dk", np.tile(inputs["diff_k_norm"][l], 2))
    add("nq", inputs["nsa_q_norm"][l])
    add("nkc", inputs["nsa_kc_norm"][l])
    add("nks", inputs["nsa_ks_norm"][l])
    add("nkw", inputs["nsa_kw_norm"][l])
    return np.stack(cols, axis=1)


NGC = 16


def build_program(C, stop_after=None, debug_out=False):
    nc = bass.Bass("TRN2", target_bir_lowering=False, num_devices=8)

    def din(name, shape, dt=F32):
        return nc.dram_tensor(name, list(shape), dt, kind="ExternalInput").ap()

    skind = "ExternalOutput" if debug_out else "Internal"

    def dscr(name, shape, dt):
        return nc.dram_tensor(name, list(shape), dt, kind=skind).ap()

    x = din("x", [SL, D])
    W = {}
    for nm, shp in [("ffn1_w_gate", [DEPTH, D, DFF]), ("ffn1_w_up", [DEPTH, D, DFF]), ("ffn1_w_down", [DEPTH, DFF, D]),
                    ("ffn2_w_gate", [DEPTH, D, DFF]), ("ffn2_w_up", [DEPTH, D, DFF]), ("ffn2_w_down", [DEPTH, DFF, D]),
                    ("w_in", [DEPTH, D, DIN]), ("w_out", [DEPTH, D, D]),
                    ("mla_w_uq", [DEPTH, 512, 768]), ("mla_w_ukv", [DEPTH, 256, 1024]),
                    ("nsa_w_ck", [DEPTH, 4096, 128]), ("nsa_w_cv", [DEPTH, 4096, 128]),
                    ("nsa_pe_k", [DEPTH, 32, 128]), ("nsa_pe_v", [DEPTH, 32, 128]),
                    ("ffn1_norm", [DEPTH, D]), ("mix_norm", [DEPTH, D]), ("ffn2_norm", [DEPTH, D]),
                    ("mla_o_norm", [DEPTH, 128]), ("diff_subln", [DEPTH, 128]), ("nsa_o_norm", [DEPTH, 128]),
                    ("sb_o_norm", [DEPTH, 128]),
                    ("diff_lq1", [DEPTH, 64]), ("diff_lk1", [DEPTH, 64]), ("diff_lq2", [DEPTH, 64]), ("diff_lk2", [DEPTH, 64]),
                    ("gcols", [DEPTH, 128, NGC])]:
        W[nm] = din(nm, shp)
    cbf_d = din("cbf", [128, C["cbf_w"]], BF16)
    cf_d = din("cf32", [128, C["cf_w"]])
    cos_d = din("cosT", [64, SL])
    sin_d = din("sinT", [64, SL])
    bonus_d = din("bonus", [128, NQB * 64])
    W["bonus"] = bonus_d
    y = nc.dram_tensor("y", [SL, D], F32, kind="ExternalOutput").ap()

    hS1 = dscr("hS1", [SL, D], F32)
    hS3 = dscr("hS3", [SL, D], F32)
    QT = dscr("QT", [20, 128, SL], BF16)
    KVs = nc.dram_tensor("KVs", [2 * 4096, SL], BF16, kind="Internal", addr_space="Shared").ap()
    pid = nc.partition_id()
    rk = pid % 2
    kv_mine = KVs[bass.ts(rk, 4096), :]

    KVloc = dscr("KVloc", [4096, SL], BF16)
    kv_dyn = [kv_mine[i * 512:(i + 1) * 512, :] for i in range(8)]

    class _KTW:
        def __getitem__(self, slot):
            return KVloc[slot * 128:(slot + 1) * 128, :]

    class _VVW:
        def __getitem__(self, slot):
            return KVloc[2304 + slot * 128:2304 + (slot + 1) * 128, :].rearrange("r (a d) -> (r a) d", d=128)

    KT = _KTW()
    VV = _VVW()

    def kt_read(rr, slot):
        return KVs[rr * 4096 + slot * 128: rr * 4096 + (slot + 1) * 128, :]

    def vv_read(rr, slot):
        return KVs[rr * 4096 + 2304 + slot * 128: rr * 4096 + 2304 + (slot + 1) * 128, :].rearrange("r (a d) -> (r a) d", d=128)

    GT = dscr("GT", [SL, 12], F32)
    OT = dscr("OT", [16, 128, SL], BF16)
    WBF = {}
    for nm, shp in [("ffn2_w_gate", [D, DFF]), ("ffn2_w_up", [D, DFF]), ("ffn2_w_down", [DFF, D]), ("w_out", [D, D]),
                    ("ffn1_w_gate", [D, DFF]), ("ffn1_w_up", [D, DFF]), ("ffn1_w_down", [DFF, D]), ("w_in", [D, DIN])]:
        WBF[nm] = nc.dram_tensor("wbf_" + nm, shp, BF16, kind="Internal").ap()
    W["_bf"] = WBF
    W["_have"] = set()

    with ExitStack() as es:
        k = KB(nc, es)
        xsem = es.enter_context(nc.semaphore("xsem"))
        xcnt = [0]
        CO = C["cbf_off"]
        cbf = k.sb(es, "cbf", C["cbf_w"], BF16)
        cf = k.sb(es, "cf", C["cf_w"], F32)
        k.dma("sp", cbf[:], cbf_d[:, :], key=cbf, w=[cbf])
        k.dma("sp", cf[:], cf_d[:, :], key=cf, w=[cf])
        if DEBUG == "recompile":
            k.op("dve", "memset", cf[:, 0:1], 0.0, w=[cf])

        def cb(name, rows=128, cols=128, c0=0):
            o = CO[name] + c0
            return cbf[0:rows, o:o + cols]

        for l in range(DEPTH):
            lam_init = 0.8 - 0.6 * math.exp(-0.3 * l)
            src = x if l == 0 else hS3
            with ExitStack() as pa:
                phase_tokens(k, pa, nc, C, W, l, "A", src, hS1, cb, cbf, cf, cos_d, sin_d,
                             QT, KT, VV, GT, OT, lam_init)
                k.flush()
            for i in range(8):
                nc.sync.dma_start(out=kv_dyn[i], in_=KVloc[i * 512:(i + 1) * 512, :]).then_inc(xsem, 16)
            xcnt[0] += 8 * 16
            nc.sync.wait_ge(xsem, xcnt[0])
            nc.all_core_barrier()
            if stop_after == f"A{l}":
                break
            with ExitStack() as pb:
                conv = [("ffn2_w_gate", l), ("ffn2_w_up", l), ("ffn2_w_down", l), ("w_out", l)]
                if l + 1 < DEPTH:
                    conv += [("ffn1_w_gate", l + 1), ("ffn1_w_up", l + 1), ("ffn1_w_down", l + 1), ("w_in", l + 1)]
                if not PRECONV:
                    conv = []
                for (nm, ll) in conv:
                    kb_ = Buf(None, f"cv_{nm}_{ll}")
                    R = W[nm].shape[1]
                    step_r = R // 8
                    for i in range(8):
                        k.dma("pool", WBF[nm][i * step_r:(i + 1) * step_r, :], W[nm][ll][i * step_r:(i + 1) * step_r, :], key=kb_)
                    W["_have"].add((nm, ll))
                phase_attn(k, pb, nc, C, W, l, cb, cbf, cf, QT, (kt_read, vv_read), None, GT, OT, lam_init, only=ONLY)
                k.flush()
            nc.all_core_barrier()
            if stop_after == f"B{l}":
                break
            dst = y if l == DEPTH - 1 else hS3
            with ExitStack() as pc:
                phase_tokens(k, pc, nc, C, W, l, "C", hS1, dst, cb, cbf, cf, cos_d, sin_d,
                             QT, KT, VV, GT, OT, lam_init)
                k.flush()
            if stop_after == f"C{l}":
                break
        k.flush(final=True)
        print("instructions:", k.n_ins, "ops:", len(k.ops), "dma sems:", len(k.keys))
    return nc


def phase_tokens(k, es, nc, C, W, l, which, src, dst, cb, cbf, cf, cos_d, sin_d, QT, KT, VV, GT, OT, lam_init):
    A = which == "A"
    htile = k.sb(es, "htile", 4 * D, F32)
    xnb = [k.sb(es, f"xnb{i}", D, BF16) for i in range(2)]
    xnT = k.sb(es, "xnT", 16 * T, BF16)
    actT = k.sb(es, "actT", NFC * T, BF16)
    WB = [k.sb(es, f"WB{i}", 8192, BF16) for i in range(2)]
    sg = [k.sb(es, f"sg{i}", T, F32) for i in range(2)]
    gbc = k.sb(es, "gbc", D, F32)
    ss = k.sb(es, "ss", 8, F32)
    PG = [k.ps(es, f"PG{i}", 512) for i in range(2)]
    PU = [k.ps(es, f"PU{i}", 512) for i in range(2)]
    PD = [k.ps(es, f"PD{i}", 512) for i in range(2)]
    PM = k.ps(es, "PM", 512)
    PT = k.ps(es, "PT", 1024, BF16)
    wbi = [0]

    def nextwb():
        b = WB[wbi[0] % 2]
        wbi[0] += 1
        return b

    fn = "ffn1" if A else "ffn2"

    def wsrc(nm):
        return W["_bf"][nm] if (nm, l) in W["_have"] else W[nm][l]
    if A:
        gc = k.sb(es, "gc", NGC, F32)
        gcs = k.sb(es, "gcs", NGC, F32)
        k.dma("sp", gc[:], W["gcols"][l], key=gc, w=[gc])
        for nm, sc in [("qn", 192 ** -0.5), ("qr", 192 ** -0.5), ("dq", 64 ** -0.5), ("nq", 128 ** -0.5)]:
            j = GC[nm]
            k.op("dve", "tensor_scalar", out=gcs[:, j:j + 1], in0=gc[:, j:j + 1], scalar1=float(sc), scalar2=None,
                 op0=ALU.mult, r=[gc], w=[gcs])
        wuq = k.sb(es, "wuq", 4 * 768, BF16)
        wukv = k.sb(es, "wukv", 2 * 1024, BF16)
        k.dma("pool", wuq[:].rearrange("p (c n) -> p c n", c=4),
              W["mla_w_uq"][l].rearrange("(c p) n -> p c n", p=128), key=wuq, w=[wuq])
        k.dma("pool", wukv[:].rearrange("p (c n) -> p c n", c=2),
              W["mla_w_ukv"][l].rearrange("(c p) n -> p c n", p=128), key=wukv, w=[wukv])
        cqf = k.sb(es, "cqf", 4 * T, BF16)
        cqn = k.sb(es, "cqn", 4 * T, BF16)
        ckvn = k.sb(es, "ckvn", 2 * T, BF16)
        sqb = [k.sb(es, f"sqb{i}", T, BF16) for i in range(2)]
        rstd = [k.sb(es, f"rstd{i}", T, F32) for i in range(2)]
        yf = k.sb(es, "yf", T, F32)
        ybf = [k.sb(es, f"ybf{i}", T, BF16) for i in range(3)]
        t1 = k.sb(es, "t1", T, F32)
        t2 = k.sb(es, "t2", T, F32)
        cosb = k.sb(es, "cosb", T, F32)
        sinb = k.sb(es, "sinb", T, F32)
        vout = [k.sb(es, f"vout{i}", 4 * 512, BF16) for i in range(2)]
        gout = k.sb(es, "gout", 4 * 12, F32)
        cnt = {"y": 0, "sq": 0, "v": 0}

    def rms_xnT(gname):
        gain_b = gbc
        k.dma("sp", gbc[:], W[gname][l].partition_broadcast(128), key=gbc, w=[gbc])
        for tb in range(4):
            sqj = xnb[(tb + 1) % 2]
            hb = htile[:, tb * D:(tb + 1) * D]
            k.op("dve", "memset", ss[:, tb:tb + 1], 0.0, w=[ss])
            k.op("act", "activation", out=sqj[:], in_=hb, func=AF.Square, accum_out=ss[:, tb:tb + 1],
                 r=[htile], w=[sqj, ss])
            k.op("act", "activation", out=ss[:, 4 + tb:5 + tb], in_=ss[:, tb:tb + 1], func=AF.Sqrt, scale=1.0 / D, bias=EPS,
                 r=[ss], w=[ss])
            k.op("dve", "reciprocal", out=ss[:, 4 + tb:5 + tb], in_=ss[:, 4 + tb:5 + tb], r=[ss], w=[ss])
            xb = xnb[tb % 2]
            k.op("dve", "scalar_tensor_tensor", out=xb[:], in0=hb, scalar=ss[:, 4 + tb:5 + tb], in1=gain_b[:],
                 op0=ALU.mult, op1=ALU.mult, r=[htile, ss, gain_b], w=[xb])
            for half in range(2):
                for c in range(8):
                    dc = half * 8 + c
                    k.op("pe", "transpose", out=PT[:, c * 128:(c + 1) * 128], in_=xb[:, dc * 128:(dc + 1) * 128],
                         identity=cb("ident"), r=[xb, cbf], w=[PT])
                dstv = xnT[:].rearrange("p (c t) -> p c t", c=16)[:, half * 8:(half + 1) * 8, tb * 128:(tb + 1) * 128]
                k.op("act", "copy", out=dstv, in_=PT[:].rearrange("p (c t) -> p c t", c=8), r=[PT], w=[xnT])

    def ffn(wg, wu, wd):
        xv = xnT[:].rearrange("p (c t) -> p c t", c=16)
        av = actT[:].rearrange("p (c t) -> p c t", c=NFC)
        pi = 0
        for fg in range(NFC // 2):
            wb = nextwb()
            wv = wb[:, 0:8192].rearrange("p (g c n) -> p g c n", g=2, c=16)
            k.dma("pool", wv[:, 0], wg[:, fg * 256:(fg + 1) * 256].rearrange("(c p) n -> p c n", p=128), key=wb, w=[wb])
            k.dma("pool", wv[:, 1], wu[:, fg * 256:(fg + 1) * 256].rearrange("(c p) n -> p c n", p=128), key=wb, w=[wb])
            for fc in range(2):
                pg, pu, sgb = PG[pi % 2], PU[pi % 2], sg[pi % 2]
                pi += 1
                for dc in range(16):
                    k.op("pe", "matmul", pg[:], lhsT=wv[:, 0, dc, fc * 128:(fc + 1) * 128], rhs=xv[:, dc, :],
                         start=(dc == 0), stop=(dc == 15), r=[wb, xnT], w=[pg])
                for dc in range(16):
                    k.op("pe", "matmul", pu[:], lhsT=wv[:, 1, dc, fc * 128:(fc + 1) * 128], rhs=xv[:, dc, :],
                         start=(dc == 0), stop=(dc == 15), r=[wb, xnT], w=[pu])
                k.op("act", "activation", out=sgb[:], in_=pg[:], func=AF.Silu, r=[pg], w=[sgb])
                k.op("dve", "tensor_tensor", out=av[:, fg * 2 + fc, :], in0=sgb[:], in1=pu[:], op=ALU.mult,
                     r=[sgb, pu], w=[actT])
        accs = [PG[0], PG[1], PU[0], PU[1]]
        pieces = [(0, 16), (16, 16), (32, 12)]
        for cg in range(4):
            for pi_, (f0, nf) in enumerate(pieces):
                wb = nextwb()
                wv = wb[:, 0:nf * 512].rearrange("p (c n) -> p c n", c=nf)
                k.dma("pool", wv, wd[f0 * 128:(f0 + nf) * 128, cg * 512:(cg + 1) * 512].rearrange("(c p) n -> p c n", p=128),
                      key=wb, w=[wb])
                for tb in range(4):
                    pd = accs[tb]
                    for fi in range(nf):
                        fc = f0 + fi
                        k.op("pe", "matmul", pd[:, 0:512], lhsT=av[:, fc, tb * 128:(tb + 1) * 128], rhs=wv[:, fi, :],
                             start=(fc == 0), stop=(fc == NFC - 1), r=[wb, actT], w=[pd])
            for tb in range(4):
                pd = accs[tb]
                hv = htile[:, tb * D + cg * 512: tb * D + (cg + 1) * 512]
                k.op("dve", "scalar_tensor_tensor", out=hv, in0=pd[:, 0:512], scalar=0.5, in1=hv,
                     op0=ALU.mult, op1=ALU.add, r=[pd], w=[htile])

    def load_h(tt, srcap):
        k.dma("sp", htile[:].rearrange("p (b d) -> p b d", b=4),
              srcap[tt * T:(tt + 1) * T, :].rearrange("(b p) d -> p b d", p=128), key=htile, w=[htile])

    def store_h(tt, dstap):
        k.dma("sp", dstap[tt * T:(tt + 1) * T, :].rearrange("(b p) d -> p b d", p=128),
              htile[:].rearrange("p (b d) -> p b d", b=4), key=htile, r=[htile])

    def fm_finish(ps, n, tt, norm, gcol, scale, rope, dest, keep=None):
        yb = ybf[cnt["y"] % 3]
        cnt["y"] += 1
        if norm is None:
            k.op("act", "activation", out=yb[0:n, :], in_=ps[0:n, 0:T], func=AF.Copy, scale=float(scale), r=[ps], w=[yb])
        else:
            sq = sqb[cnt["sq"] % 2]
            rs = rstd[cnt["sq"] % 2]
            cnt["sq"] += 1
            k.op("act", "activation", out=sq[0:n, :], in_=ps[0:n, 0:T], func=AF.Square, r=[ps], w=[sq])
            onesm = cb("ones", n, n) if norm == 128 else cb("blk64", n, n)
            k.op("pe", "matmul", PM[0:n, 0:T], lhsT=onesm, rhs=sq[0:n, :], start=True, stop=True, r=[sq, cbf], w=[PM])
            k.op("act", "activation", out=rs[0:n, :], in_=PM[0:n, 0:T], func=AF.Sqrt, scale=1.0 / norm, bias=EPS,
                 r=[PM], w=[rs])
            k.op("dve", "reciprocal", out=rs[0:n, :], in_=rs[0:n, :], r=[rs], w=[rs])
            if not rope:
                k.op("dve", "scalar_tensor_tensor", out=yb[0:n, :], in0=ps[0:n, 0:T], scalar=gcol[0:n, :], in1=rs[0:n, :],
                     op0=ALU.mult, op1=ALU.mult, r=[ps, rs, gc, gcs], w=[yb])
            else:
                k.op("dve", "scalar_tensor_tensor", out=yf[0:n, :], in0=ps[0:n, 0:T], scalar=gcol[0:n, :], in1=rs[0:n, :],
                     op0=ALU.mult, op1=ALU.mult, r=[ps, rs, gc, gcs], w=[yf])
                yb2 = ybf[cnt["y"] % 3]
                cnt["y"] += 1
                k.op("act", "copy", out=yb2[0:n, :], in_=yf[0:n, :], r=[yf], w=[yb2])
                k.op("pe", "matmul", PM[0:n, 0:T], lhsT=cb("prot", n, n), rhs=yb2[0:n, :], start=True, stop=True,
                     r=[yb2, cbf], w=[PM])
                k.op("dve", "tensor_tensor", out=t1[0:n, :], in0=yf[0:n, :], in1=cosb[0:n, :], op=ALU.mult,
                     r=[yf, cosb], w=[t1])
                k.op("dve", "tensor_tensor", out=t2[0:n, :], in0=PM[0:n, 0:T], in1=sinb[0:n, :], op=ALU.mult,
                     r=[PM, sinb], w=[t2])
                k.op("dve", "tensor_tensor", out=yb[0:n, :], in0=t1[0:n, :], in1=t2[0:n, :], op=ALU.add,
                     r=[t1, t2], w=[yb])
        k.dma("sp", dest, yb[0:n, :], key=yb, r=[yb])

    def proj_mm_fm(ps, wv, nin, c0, n, inv):
        for c in range(nin):
            k.op("pe", "matmul", ps[0:n, 0:T], lhsT=wv[:, c, c0:c0 + n], rhs=inv[:, c, :], start=(c == 0),
                 stop=(c == nin - 1), r=list(proj_r), w=[ps])

    proj_r = []

    def load_win(c0, n):
        wb = nextwb()
        wv = wb[:, 0:16 * n].rearrange("p (c n) -> p c n", c=16)
        k.dma("pool", wv, wsrc("w_in")[:, c0:c0 + n].rearrange("(c p) n -> p c n", p=128), key=wb, w=[wb])
        return wb, wv

    def tm_group(wb, wv, nin, c0, n, inv, tt, dest_list, sigmoid=False):
        vo = vout[cnt["v"] % 2]
        cnt["v"] += 1
        vov = vo[:].rearrange("p (b n) -> p b n", b=4)
        for tb in range(4):
            pd = PD[tb % 2]
            for c in range(nin):
                k.op("pe", "matmul", pd[:, 0:n], lhsT=inv[:, c, tb * 128:(tb + 1) * 128], rhs=wv[:, c, c0:c0 + n],
                     start=(c == 0), stop=(c == nin - 1), r=list(proj_r), w=[pd])
            if sigmoid:
                k.op("act", "activation", out=gout[:, tb * 12:(tb + 1) * 12], in_=pd[:, 0:n], func=AF.Sigmoid,
                     r=[pd], w=[gout])
            else:
                k.op("act", "copy", out=vov[:, tb, 0:n], in_=pd[:, 0:n], r=[pd], w=[vo])
        if sigmoid:
            k.dma("sp", GT[tt * T:(tt + 1) * T, :].rearrange("(b p) n -> p b n", p=128),
                  gout[:].rearrange("p (b n) -> p b n", b=4), key=gout, r=[gout])
        else:
            for (co, wd_, dap) in dest_list:
                for tb in range(4):
                    k.dma("sp", dap[tt * T + tb * 128: tt * T + (tb + 1) * 128, :], vov[:, tb, co:co + wd_], key=vo, r=[vo])

    def projections(tt):
        xv = xnT[:].rearrange("p (c t) -> p c t", c=16)
        tsl = slice(tt * T, (tt + 1) * T)
        k.dma("sp", cosb[0:64, :], cos_d[:, tsl], key=cosb, w=[cosb])
        k.dma("sp", sinb[0:64, :], sin_d[:, tsl], key=sinb, w=[sinb])
        gcol = lambda nm: gc[:, GC[nm]:GC[nm] + 1]
        gscol = lambda nm: gcs[:, GC[nm]:GC[nm] + 1]
        wb, wv = load_win(0, 512)
        proj_r[:] = [wb, xnT]
        cqv = cqf[:].rearrange("p (c t) -> p c t", c=4)
        for c in range(4):
            ps = PG[c % 2]
            proj_mm_fm(ps, wv, 16, c * 128, 128, xv)
            k.op("act", "copy", out=cqv[:, c, :], in_=ps[:, 0:T], r=[ps], w=[cqf])
            sq = sqb[c % 2]
            k.op("act", "activation", out=sq[:], in_=ps[:, 0:T], func=AF.Square, r=[ps], w=[sq])
            k.op("pe", "matmul", PM[:, 0:T], lhsT=cb("ones"), rhs=sq[:], start=(c == 0), stop=(c == 3),
                 r=[sq, cbf], w=[PM])
        rs = rstd[0]
        k.op("act", "activation", out=rs[:], in_=PM[:, 0:T], func=AF.Sqrt, scale=1.0 / 512, bias=EPS, r=[PM], w=[rs])
        k.op("dve", "reciprocal", out=rs[:], in_=rs[:], r=[rs], w=[rs])
        cqnv = cqn[:].rearrange("p (c t) -> p c t", c=4)
        for c in range(4):
            k.op("dve", "scalar_tensor_tensor", out=cqnv[:, c, :], in0=cqv[:, c, :], scalar=gcol(f"cq{c}"), in1=rs[:],
                 op0=ALU.mult, op1=ALU.mult, r=[cqf, rs, gc], w=[cqn])
        wb, wv = load_win(512, 256 + 64)
        proj_r[:] = [wb, xnT]
        for c in range(2):
            ps = PG[c % 2]
            proj_mm_fm(ps, wv, 16, c * 128, 128, xv)
            k.op("act", "copy", out=cqv[:, c, :], in_=ps[:, 0:T], r=[ps], w=[cqf])
            sq = sqb[c % 2]
            k.op("act", "activation", out=sq[:], in_=ps[:, 0:T], func=AF.Square, r=[ps], w=[sq])
            k.op("pe", "matmul", PM[:, 0:T], lhsT=cb("ones"), rhs=sq[:], start=(c == 0), stop=(c == 1),
                 r=[sq, cbf], w=[PM])
        rs = rstd[1]
        k.op("act", "activation", out=rs[:], in_=PM[:, 0:T], func=AF.Sqrt, scale=1.0 / 256, bias=EPS, r=[PM], w=[rs])
        k.op("dve", "reciprocal", out=rs[:], in_=rs[:], r=[rs], w=[rs])
        ckvv = ckvn[:].rearrange("p (c t) -> p c t", c=2)
        for c in range(2):
            k.op("dve", "scalar_tensor_tensor", out=ckvv[:, c, :], in0=cqv[:, c, :], scalar=gcol(f"ckv{c}"), in1=rs[:],
                 op0=ALU.mult, op1=ALU.mult, r=[cqf, rs, gc], w=[ckvn])
        ps = PU[0]
        proj_mm_fm(ps, wv, 16, 256, 64, xv)
        fm_finish(ps, 64, tt, 64, gcol("kr"), 1.0, True, KT[4][0:64, tsl])
        wuqv = wuq[:].rearrange("p (c n) -> p c n", c=4)
        wukvv = wukv[:].rearrange("p (c n) -> p c n", c=2)
        for h in range(4):
            proj_r[:] = [wuq, cqn]
            ps = PG[h % 2]
            proj_mm_fm(ps, wuqv, 4, h * 192, 128, cqnv)
            fm_finish(ps, 128, tt, 128, gscol("qn"), 1.0, False, QT[h][:, tsl])
            ps = PU[h % 2]
            proj_mm_fm(ps, wuqv, 4, h * 192 + 128, 64, cqnv)
            fm_finish(ps, 64, tt, 64, gscol("qr"), 1.0, True, QT[4 + h][0:64, tsl])
            proj_r[:] = [wukv, ckvn]
            ps = PG[(h + 1) % 2]
            proj_mm_fm(ps, wukvv, 2, h * 256, 128, ckvv)
            fm_finish(ps, 128, tt, 128, gcol("kn"), 1.0, False, KT[h][:, tsl])
        proj_r[:] = [wukv, ckvn]
        for h in range(4):
            tm_group(wukv, wukvv, 2, h * 256 + 128, 128, ckvv, tt, [(0, 128, VV[h])])
        wb, wv = load_win(832, 512)
        proj_r[:] = [wb, xnT]
        for h in range(4):
            ps = PG[h % 2]
            proj_mm_fm(ps, wv, 16, h * 128, 128, xv)
            fm_finish(ps, 128, tt, 64, gscol("dq"), 1.0, False, QT[8 + h][:, tsl])
        wb, wv = load_win(1344, 512)
        proj_r[:] = [wb, xnT]
        for h in range(4):
            ps = PU[h % 2]
            proj_mm_fm(ps, wv, 16, h * 128, 128, xv)
            fm_finish(ps, 128, tt, 64, gcol("dk"), 1.0, False, KT[5 + h][:, tsl])
        wb, wv = load_win(1856, 512)
        proj_r[:] = [wb, xnT]
        tm_group(wb, wv, 16, 0, 512, xv, tt, [(h * 128, 128, VV[4 + h]) for h in range(4)])
        wb, wv = load_win(2368, 512)
        proj_r[:] = [wb, xnT]
        for h in range(4):
            ps = PG[h % 2]
            proj_mm_fm(ps, wv, 16, h * 128, 128, xv)
            fm_finish(ps, 128, tt, 128, gscol("nq"), 1.0, False, QT[12 + h][:, tsl])
        wb, wv = load_win(2880, 512)
        proj_r[:] = [wb, xnT]
        ps = PU[0]
        proj_mm_fm(ps, wv, 16, 0, 128, xv)
        fm_finish(ps, 128, tt, None, None, 1.0, False, KT[9][:, tsl])
        ps = PU[1]
        proj_mm_fm(ps, wv, 16, 128, 128, xv)
        fm_finish(ps, 128, tt, None, None, 1.0, False, KT[10][:, tsl])
        ps = PU[0]
        proj_mm_fm(ps, wv, 16, 256, 128, xv)
        fm_finish(ps, 128, tt, 128, gcol("nks"), 1.0, False, KT[11][:, tsl])
        tm_group(wb, wv, 16, 384, 128, xv, tt, [(0, 128, VV[8])])
        wb, wv = load_win(3392, 268)
        proj_r[:] = [wb, xnT]
        ps = PU[1]
        proj_mm_fm(ps, wv, 16, 0, 128, xv)
        fm_finish(ps, 128, tt, 128, gcol("nkw"), 1.0, False, KT[12][:, tsl])
        tm_group(wb, wv, 16, 128, 128, xv, tt, [(0, 128, VV[9])])
        tm_group(wb, wv, 16, 256, 12, xv, tt, None, sigmoid=True)
        wb, wv = load_win(3660, 512)
        proj_r[:] = [wb, xnT]
        for h in range(4):
            ps = PG[h % 2]
            proj_mm_fm(ps, wv, 16, h * 128, 128, xv)
            fm_finish(ps, 128, tt, None, None, 128 ** -0.5, False, QT[16 + h][:, tsl])
        wb, wv = load_win(4172, 512)
        proj_r[:] = [wb, xnT]
        for h in range(4):
            ps = PU[h % 2]
            proj_mm_fm(ps, wv, 16, h * 128, 128, xv)
            fm_finish(ps, 128, tt, None, None, 1.0, False, KT[13 + h][:, tsl])
        wb, wv = load_win(4684, 512)
        proj_r[:] = [wb, xnT]
        tm_group(wb, wv, 16, 0, 512, xv, tt, [(h * 128, 128, VV[10 + h]) for h in range(4)])

    def wout_add(tt):
        ot = xnT
        ov = ot[:].rearrange("p (c t) -> p c t", c=16)
        k.dma("sp", ov, OT[:, :, tt * T:(tt + 1) * T].rearrange("c p t -> p c t"), key=ot, w=[ot])
        for cg in range(4):
            wb = nextwb()
            wv = wb[:, 0:8192].rearrange("p (c n) -> p c n", c=16)
            k.dma("pool", wv, wsrc("w_out")[:, cg * 512:(cg + 1) * 512].rearrange("(c p) n -> p c n", p=128), key=wb, w=[wb])
            for tb in range(4):
                pd = PD[tb % 2]
                for c in range(16):
                    k.op("pe", "matmul", pd[:], lhsT=ov[:, c, tb * 128:(tb + 1) * 128], rhs=wv[:, c, :],
                         start=(c == 0), stop=(c == 15), r=[wb, ot], w=[pd])
                hv = htile[:, tb * D + cg * 512: tb * D + (cg + 1) * 512]
                k.op("dve", "tensor_tensor", out=hv, in0=pd[:], in1=hv, op=ALU.add, r=[pd], w=[htile])

    for tt in range(NTT):
        load_h(tt, src)
        if A:
            rms_xnT("ffn1_norm")
            ffn(wsrc("ffn1_w_gate"), wsrc("ffn1_w_up"), wsrc("ffn1_w_down"))
            store_h(tt, dst)
            rms_xnT("mix_norm")
            projections(tt)
        else:
            wout_add(tt)
            rms_xnT("ffn2_norm")
            ffn(wsrc("ffn2_w_gate"), wsrc("ffn2_w_up"), wsrc("ffn2_w_down"))
            store_h(tt, dst)


def phase_attn(k, es, nc, C, W, l, cb, cbf, cf, QT, KT, VV, GT, OT, lam_init, only=None):
    NG = SL // 512
    kt_read, vv_read = KT

    def common(ms):
        d = {}
        d["S"] = [k.ps(ms, f"S{i}", 512) for i in range(3)]
        d["O"] = [k.ps(ms, f"O{i}", 512) for i in range(4)]
        d["PT"] = k.ps(ms, "PTt", 1024, BF16)
        d["pt"] = [k.sb(ms, f"pt{i}", 512, BF16) for i in range(3)]
        d["on"] = [k.sb(ms, f"on{i}", 512, F32) for i in range(3)]
        d["sm"] = k.sb(ms, "sm", 16, F32)
        d["ob"] = [k.sb(ms, f"ob{i}", 128, BF16) for i in range(2)]
        d["ot"] = [k.sb(ms, f"ot{i}", 512, BF16) for i in range(2)]
        d["gbc"] = k.sb(ms, "gbco", 128, F32)
        d["junk"] = k.sb(ms, "junk", 128, F32)
        d["cnt"] = {"s": 0, "p": 0, "ob": 0, "ot": 0}
        return d

    def causal_mode(kt, j):
        e = kt - 2 * j
        if e > 1:
            return "skip"
        return "m1" if e == 1 else ("m0" if e == 0 else "full")

    def std_mask(m, kt, qb):
        return cb(m), [cbf]

    def soft_sub(cm, g, pairs, vfn, nv, ktlist, modefn, biasfn, maskfn, fin, preads, nchunk=4):
        ktlist = list(ktlist)
        valid = {qb: [kt for kt in ktlist if modefn(kt, 4 * g + qb) != "skip"] for qb in range(4)}
        steps = []
        for kt in ktlist:
            qbs = [qb for qb in range(4) if modefn(kt, 4 * g + qb) != "skip"]
            if qbs:
                steps.append((kt, qbs))

        def emit_qk(st):
            kt, qbs = st["kt"], st["qbs"]
            lo, hi = qbs[0], qbs[-1] + 1
            Sb = cm["S"][cm["cnt"]["s"] % 3]
            cm["cnt"]["s"] += 1
            P = cm["pt"][cm["cnt"]["p"] % 3]
            cm["cnt"]["p"] += 1
            st["S"], st["P"] = Sb, P
            for i, (lf, rf) in enumerate(pairs):
                k.op("pe", "matmul", Sb[:, lo * 128:hi * 128], lhsT=lf(kt), rhs=rf(g * 512 + lo * 128, g * 512 + hi * 128),
                     start=(i == 0), stop=(i == len(pairs) - 1), r=preads, w=[Sb])

        def emit_act(st):
            kt, qbs, Sb, P = st["kt"], st["qbs"], st["S"], st["P"]
            lo, hi = qbs[0], qbs[-1] + 1
            if biasfn is None:
                k.op("act", "activation", out=P[:, lo * 128:hi * 128], in_=Sb[:, lo * 128:hi * 128], func=AF.Exp,
                     r=[Sb], w=[P])
            else:
                cs = 4 // nchunk
                for c in range(nchunk):
                    a, b = max(lo, c * cs), min(hi, (c + 1) * cs)
                    if a >= b:
                        continue
                    k.op("act", "activation", out=P[:, a * 128:b * 128], in_=Sb[:, a * 128:b * 128],
                         func=AF.Exp, bias=biasfn(kt, 4 * g + (c + 1) * cs - 1), r=[Sb, cf], w=[P])
            for qb in qbs:
                m = modefn(kt, 4 * g + qb)
                if m != "full":
                    map_, mr = maskfn(m, kt, qb)
                    k.op("dve", "tensor_tensor", out=P[:, qb * 128:(qb + 1) * 128], in0=P[:, qb * 128:(qb + 1) * 128],
                         in1=map_, op=ALU.mult, r=mr, w=[P])

        def emit_pv(st):
            kt, qbs, P = st["kt"], st["qbs"], st["P"]
            for qb in qbs:
                k.op("pe", "matmul", cm["O"][qb][:, 0:nv], lhsT=P[:, qb * 128:(qb + 1) * 128], rhs=vfn(kt),
                     start=(kt == valid[qb][0]), stop=(kt == valid[qb][-1]), r=[P] + preads, w=[cm["O"][qb]])

        sts = [{"kt": kt, "qbs": qbs} for (kt, qbs) in steps]
        n = len(sts)
        for i in range(n):
            emit_qk(sts[i])
            if i >= 2:
                emit_pv(sts[i - 2])
            if i >= 1:
                emit_act(sts[i - 1])
        if n >= 1:
            emit_act(sts[n - 1])
        if n >= 2:
            emit_pv(sts[n - 2])
        if n >= 1:
            emit_pv(sts[n - 1])
        for qb in range(4):
            fin(qb, cm["O"][qb])

    def fin_norm(cm, onb):
        def fin(qb, Ob):
            sm = cm["sm"]
            k.op("dve", "tensor_scalar", out=sm[:, qb:qb + 1], in0=Ob[:, 128:129], scalar1=1e-37, scalar2=None,
                 op0=ALU.max, r=[Ob], w=[sm])
            k.op("dve", "reciprocal", out=sm[:, qb:qb + 1], in_=sm[:, qb:qb + 1], r=[sm], w=[sm])
            k.op("dve", "tensor_scalar", out=onb[:, qb * 128:(qb + 1) * 128], in0=Ob[:, 0:128], scalar1=sm[:, qb:qb + 1],
                 scalar2=None, op0=ALU.mult, r=[Ob, sm], w=[onb])
        return fin

    def finish_head(cm, slot, g, onb, extra):
        sm = cm["sm"]
        otb = cm["ot"][cm["cnt"]["ot"] % 2]
        cm["cnt"]["ot"] += 1
        for qb in range(4):
            ov = onb[:, qb * 128:(qb + 1) * 128]
            k.op("dve", "memset", sm[:, 4 + qb:5 + qb], 0.0, w=[sm])
            k.op("act", "activation", out=cm["junk"][:], in_=ov, func=AF.Square, accum_out=sm[:, 4 + qb:5 + qb],
                 r=[onb], w=[cm["junk"], sm])
            k.op("act", "activation", out=sm[:, 8 + qb:9 + qb], in_=sm[:, 4 + qb:5 + qb], func=AF.Sqrt, scale=1.0 / 128,
                 bias=EPS, r=[sm], w=[sm])
            k.op("dve", "reciprocal", out=sm[:, 8 + qb:9 + qb], in_=sm[:, 8 + qb:9 + qb], r=[sm], w=[sm])
            if extra != 1.0:
                k.op("dve", "tensor_scalar", out=sm[:, 8 + qb:9 + qb], in0=sm[:, 8 + qb:9 + qb], scalar1=float(extra),
                     scalar2=None, op0=ALU.mult, r=[sm], w=[sm])
            ob = cm["ob"][cm["cnt"]["ob"] % 2]
            cm["cnt"]["ob"] += 1
            k.op("dve", "scalar_tensor_tensor", out=ob[:], in0=ov, scalar=sm[:, 8 + qb:9 + qb], in1=cm["gbc"][:],
                 op0=ALU.mult, op1=ALU.mult, r=[onb, sm, cm["gbc"]], w=[ob])
            k.op("pe", "transpose", out=cm["PT"][:, qb * 128:(qb + 1) * 128], in_=ob[:], identity=cb("ident"),
                 r=[ob, cbf], w=[cm["PT"]])
        k.op("act", "copy", out=otb[:], in_=cm["PT"][:, 0:512], r=[cm["PT"]], w=[otb])
        k.dma("sp", OT[slot][:, g * 512:(g + 1) * 512], otb[:], key=otb, r=[otb])

    def load_v(v, slot):
        vv = v[:].rearrange("p (t c) -> p t c", c=132)
        for rr in range(2):
            k.dma("sp", vv.rearrange("p (b two) c -> p b two c", two=2)[:, :, rr, 0:128],
                  vv_read(rr, slot).rearrange("(b p) d -> p b d", p=128), key=v, w=[v])
        return vv

    def load_k(buf, slot, rows=128):
        for rr in range(2):
            k.dma("sp", buf[0:rows, :].rearrange("p (b two t) -> p b two t", two=2, t=128)[:, :, rr, :],
                  kt_read(rr, slot)[0:rows, :].rearrange("p (b t) -> p b t", t=128), key=buf, w=[buf])

    def ab_bias(si):
        return lambda kt, j: cf[:, si * 64 + (kt - 2 * j + 62): si * 64 + (kt - 2 * j + 62) + 1]

    def ab_chunks(si):
        sl = SLOPES[si]
        return 4 if sl * 511 > 70 else (2 if sl * 1023 > 70 else 1)

    def far_skip(si, modefn):
        sl = SLOPES[si]

        def f(kt, j):
            rel = kt - 2 * j
            if rel < 0 and sl * 128.0 * (-rel - 1) >= 110.0:
                return "skip"
            return modefn(kt, j)
        return f

    mixers = only if only is not None else ("mla", "diff", "nsa", "sb")

    if "mla" in mixers:
        with ExitStack() as ms:
            cm = common(ms)
            aq = [k.sb(ms, f"aq{i}", SL, BF16) for i in range(2)]
            aqr = [k.sb(ms, f"aqr{i}", SL, BF16) for i in range(2)]
            ak = [k.sb(ms, f"ak{i}", S, BF16) for i in range(2)]
            akr = k.sb(ms, "akr", S, BF16)
            av = [k.sb(ms, f"av{i}", 32 * 132, BF16) for i in range(2)]
            k.dma("sp", cm["gbc"][:], W["mla_o_norm"][l].partition_broadcast(128), key=cm["gbc"], w=[cm["gbc"]])
            load_k(akr, 4, 64)
            for i in range(2):
                k.op("dve", "memset", av[i][:], 1.0, w=[av[i]])
            for h in range(4):
                q, qr, kk, v = aq[h % 2], aqr[h % 2], ak[h % 2], av[h % 2]
                k.dma("sp", q[:], QT[h], key=q, w=[q])
                k.dma("sp", qr[0:64, :], QT[4 + h][0:64, :], key=qr, w=[qr])
                load_k(kk, h)
                vv = load_v(v, h)
                pairs = [(lambda kt, kk=kk: kk[:, kt * 128:(kt + 1) * 128], lambda a, b, q=q: q[:, a:b]),
                         (lambda kt: akr[0:64, kt * 128:(kt + 1) * 128], lambda a, b, qr=qr: qr[0:64, a:b])]
                for g in range(NG):
                    onb = cm["on"][0]
                    soft_sub(cm, g, pairs, lambda kt, vv=vv: vv[:, kt, 0:129], 129, range(0, 8 * g + 8), causal_mode,
                             None, std_mask, fin_norm(cm, onb), [q, qr, kk, akr, v])
                    finish_head(cm, h, g, onb, 1.0)
            k.flush()

    if "diff" in mixers:
        with ExitStack() as ms:
            cm = common(ms)
            aq = [k.sb(ms, f"dq{i}", SL, BF16) for i in range(2)]
            ak = [k.sb(ms, f"dk{i}", S, BF16) for i in range(2)]
            av = [k.sb(ms, f"dv{i}", 32 * 132, BF16) for i in range(2)]
            lt = k.sb(ms, "lt", 256, F32)
            ltmp = k.sb(ms, "ltmp", 64, F32)
            ls = k.sb(ms, "ls", 8, F32)
            k.dma("sp", cm["gbc"][:], W["diff_subln"][l].partition_broadcast(128), key=cm["gbc"], w=[cm["gbc"]])
            for i, nm in enumerate(["diff_lq1", "diff_lk1", "diff_lq2", "diff_lk2"]):
                k.dma("sp", lt[:, i * 64:(i + 1) * 64], W[nm][l].partition_broadcast(128), key=lt, w=[lt])
            for j in range(2):
                k.op("dve", "tensor_tensor", out=ltmp[:], in0=lt[:, j * 128:j * 128 + 64], in1=lt[:, j * 128 + 64:j * 128 + 128],
                     op=ALU.mult, r=[lt], w=[ltmp])
                k.op("dve", "reduce_sum", out=ls[:, j:j + 1], in_=ltmp[:], axis=AX.X, r=[ltmp], w=[ls])
                k.op("act", "activation", out=ls[:, 2 + j:3 + j], in_=ls[:, j:j + 1], func=AF.Exp, r=[ls], w=[ls])
            k.op("dve", "tensor_tensor", out=ls[:, 4:5], in0=ls[:, 3:4], in1=ls[:, 2:3], op=ALU.subtract, r=[ls], w=[ls])
            k.op("dve", "tensor_scalar", out=ls[:, 5:6], in0=ls[:, 4:5], scalar1=float(-lam_init), scalar2=None, op0=ALU.add,
                 r=[ls], w=[ls])
            for i in range(2):
                k.op("dve", "memset", av[i][:], 1.0, w=[av[i]])
            for h in range(4):
                q, kk, v = aq[h % 2], ak[h % 2], av[h % 2]
                k.dma("sp", q[:], QT[8 + h], key=q, w=[q])
                load_k(kk, 5 + h)
                vv = load_v(v, 4 + h)
                for g in range(NG):
                    for sub in range(2):
                        r0, r1 = sub * 64, (sub + 1) * 64
                        pairs = [(lambda kt, kk=kk, r0=r0, r1=r1: kk[r0:r1, kt * 128:(kt + 1) * 128],
                                  lambda a, b, q=q, r0=r0, r1=r1: q[r0:r1, a:b])]
                        soft_sub(cm, g, pairs, lambda kt, vv=vv: vv[:, kt, 0:129], 129, range(0, 8 * g + 8),
                                 far_skip(2 * h, causal_mode), ab_bias(2 * h), std_mask, fin_norm(cm, cm["on"][sub]),
                                 [q, kk, v], nchunk=ab_chunks(2 * h))
                    on0, on1 = cm["on"][0], cm["on"][1]
                    k.op("dve", "scalar_tensor_tensor", out=on0[:], in0=on1[:], scalar=ls[:, 5:6], in1=on0[:],
                         op0=ALU.mult, op1=ALU.add, r=[on1, ls], w=[on0])
                    finish_head(cm, 4 + h, g, on0, 1.0 - lam_init)
            k.flush()

    if "nsa" in mixers:
        with ExitStack() as ms:
            cm = common(ms)
            nq = [k.sb(ms, f"nq{i}", SL, BF16) for i in range(4)]
            nks = k.sb(ms, "nks", S, BF16)
            nkw = k.sb(ms, "nkw", S, BF16)
            nvs = k.sb(ms, "nvs", 32 * 132, BF16)
            nvw = k.sb(ms, "nvw", 32 * 132, BF16)
            raw = [k.sb(ms, f"raw{i}", S, BF16) for i in range(2)]
            wc = [k.sb(ms, f"wc{i}", 32 * 128, BF16) for i in range(2)]
            kcT = k.sb(ms, "kcT", 256, BF16)
            vcx = k.sb(ms, "vcx", 2 * 196, BF16)
            gts = k.sb(ms, "gts", NQB * 12, F32)
            bonus = k.sb(ms, "bonus", NQB * 64, F32)
            maskT = k.sb(ms, "maskT", 4 * 32 * 128, BF16)
            selexp = k.sb(ms, "selexp", 4096, BF16)
            selb = k.sb(ms, "selb", 64, BF16)
            scb = k.sb(ms, "scb", 64, F32)
            sc2 = k.sb(ms, "sc2", 64, F32)
            imp = [k.sb(ms, f"imp{i}", 64, F32) for i in range(4)]
            m8 = k.sb(ms, "m8", 16, F32)
            oc = k.sb(ms, "oc", 4 * 4 * 128, F32)
            pe_f = k.sb(ms, "pe_f", 128, F32)
            pe_b = k.sb(ms, "pe_b", 128, BF16)
            peT = k.sb(ms, "peT", 32, BF16)
            crow = k.sb(ms, "crow", 128, BF16)
            kcn = k.sb(ms, "kcn", 128, BF16)
            gcn = k.sb(ms, "gcn", NGC, F32)
            sm = cm["sm"]
            k.dma("sp", cm["gbc"][:], W["nsa_o_norm"][l].partition_broadcast(128), key=cm["gbc"], w=[cm["gbc"]])
            k.dma("sp", gcn[:], W["gcols"][l], key=gcn, w=[gcn])
            k.dma("sp", bonus[:], W["bonus"], key=bonus, w=[bonus])
            k.dma("sp", gts[:].rearrange("p (b n) -> p b n", n=12), GT.rearrange("(b p) n -> p b n", p=128), key=gts, w=[gts])
            for h in range(4):
                k.dma("sp", nq[h][:], QT[12 + h], key=nq[h], w=[nq[h]])
            load_k(nks, 11)
            load_k(nkw, 12)
            k.op("dve", "memset", nvs[:], 1.0, w=[nvs])
            k.op("dve", "memset", nvw[:], 1.0, w=[nvw])
            vsv = load_v(nvs, 8)
            vwv = load_v(nvw, 9)
            load_k(raw[0], 9)
            load_k(raw[1], 10)
            k.dma("pool", wc[0][:].rearrange("p (l n) -> p l n", n=128), W["nsa_w_ck"][l].rearrange("(l p) n -> p l n", p=128),
                  key=wc[0], w=[wc[0]])
            k.dma("pool", wc[1][:].rearrange("p (l n) -> p l n", n=128), W["nsa_w_cv"][l].rearrange("(l p) n -> p l n", p=128),
                  key=wc[1], w=[wc[1]])
            k.op("dve", "memset", kcT[:], 0.0, w=[kcT])
            k.op("dve", "memset", vcx[:], 0.0, w=[vcx])
            vcv = vcx[:].rearrange("p (t c) -> p t c", c=196)
            k.op("dve", "memset", vcv[:, :, 128:129], 1.0, w=[vcx])
            k.op("dve", "tensor_copy", out=vcv[:, :, 129:193], in_=cb("ov").rearrange("p (t c) -> p t c", c=64), r=[cbf], w=[vcx])
            for which in range(2):
                pen = "nsa_pe_k" if which == 0 else "nsa_pe_v"
                wv = wc[which][:].rearrange("p (l n) -> p l n", n=128)
                k.dma("sp", pe_f[0:32, :], W[pen][l], key=pe_f, w=[pe_f])
                k.op("act", "copy", out=pe_b[0:32, :], in_=pe_f[0:32, :], r=[pe_f], w=[pe_b])
                k.op("pe", "transpose", out=cm["PT"][:, 0:32], in_=pe_b[0:32, :], identity=cb("ident", 32, 32),
                     r=[pe_b, cbf], w=[cm["PT"]])
                k.op("act", "copy", out=peT[:], in_=cm["PT"][:, 0:32], r=[cm["PT"]], w=[peT])
                Oc = cm["O"][0]
                for li in range(32):
                    k.op("pe", "matmul", Oc[0:1, 0:128], lhsT=peT[:, li:li + 1], rhs=wv[:, li, :], start=(li == 0),
                         stop=(li == 31), r=[peT, wc[which]], w=[Oc])
                k.op("act", "copy", out=crow[0:1, :], in_=Oc[0:1, 0:128], r=[Oc], w=[crow])
                for nt in range(2):
                    nr = 128 if nt == 0 else 127
                    Ok = cm["O"][1 + nt]
                    for li in range(32):
                        a0 = nt * 2048 + li
                        k.op("pe", "matmul", Ok[0:nr, 0:128], lhsT=raw[which][:, a0:a0 + 16 * (nr - 1) + 1:16], rhs=wv[:, li, :],
                             start=(li == 0), stop=False, r=[raw[which], wc[which]], w=[Ok])
                    k.op("pe", "matmul", Ok[0:nr, 0:128], lhsT=cb("ones", 1, nr), rhs=crow[0:1, :], start=False, stop=True,
                         r=[crow, cbf], w=[Ok])
                    if which == 0:
                        k.op("dve", "memset", sm[:, 0:1], 0.0, w=[sm])
                        k.op("act", "activation", out=cm["junk"][0:nr, :], in_=Ok[0:nr, 0:128], func=AF.Square,
                             accum_out=sm[0:nr, 0:1], r=[Ok], w=[cm["junk"], sm])
                        k.op("act", "activation", out=sm[0:nr, 1:2], in_=sm[0:nr, 0:1], func=AF.Sqrt, scale=1.0 / 128, bias=EPS,
                             r=[sm], w=[sm])
                        k.op("dve", "reciprocal", out=sm[0:nr, 1:2], in_=sm[0:nr, 1:2], r=[sm], w=[sm])
                        k.op("dve", "memset", kcn[:], 0.0, w=[kcn])
                        k.op("dve", "tensor_scalar", out=kcn[0:nr, :], in0=Ok[0:nr, 0:128], scalar1=sm[0:nr, 1:2], scalar2=None,
                             op0=ALU.mult, r=[Ok, sm], w=[kcn])
                        k.op("pe", "transpose", out=cm["PT"][:, 128:256], in_=kcn[:], identity=cb("ident"), r=[kcn, cbf],
                             w=[cm["PT"]])
                        k.op("dve", "tensor_scalar", out=kcT[:, nt * 128:(nt + 1) * 128], in0=cm["PT"][:, 128:256],
                             scalar1=gcn[:, GC["nkc"]:GC["nkc"] + 1], scalar2=None, op0=ALU.mult, r=[cm["PT"], gcn], w=[kcT])
                    else:
                        k.op("act", "copy", out=vcv[0:nr, nt, 0:128], in_=Ok[0:nr, 0:128], r=[Ok], w=[vcx])
            ocv = oc[:].rearrange("p (q h d) -> p q h d", q=4, h=4)
            mkv = maskT[:].rearrange("p (q t d) -> p q t d", q=4, t=32)

            def cmp_mode(nt, j):
                if nt == 0:
                    return "full" if j >= 9 else ("cm", j)
                return "skip" if j < 8 else ("cm", 9 + j - 8)

            def cmp_mask(m, kt, qb):
                return cb("cm", 128, 128, m[1] * 128), [cbf]

            def sel_mode(kt, j):
                return "skip" if kt - 2 * j > 1 else "m"

            def win_mode(kt, j):
                e = kt - 2 * j
                if e > 1 or e < -4:
                    return "skip"
                if e in (-2, -1):
                    return "full"
                return f"w{e}"

            for g in range(NG):
                for h in range(4):
                    def fin_cmp(qb, Ob, h=h):
                        k.op("dve", "tensor_scalar", out=sm[:, qb:qb + 1], in0=Ob[:, 128:129], scalar1=1e-37, scalar2=None,
                             op0=ALU.max, r=[Ob], w=[sm])
                        k.op("dve", "reciprocal", out=sm[:, qb:qb + 1], in_=sm[:, qb:qb + 1], r=[sm], w=[sm])
                        k.op("dve", "tensor_scalar", out=ocv[:, qb, h, :], in0=Ob[:, 0:128], scalar1=sm[:, qb:qb + 1],
                             scalar2=None, op0=ALU.mult, r=[Ob, sm], w=[oc])
                        if h == 0:
                            k.op("dve", "tensor_scalar", out=imp[qb][:], in0=Ob[:, 129:193], scalar1=sm[:, qb:qb + 1],
                                 scalar2=None, op0=ALU.mult, r=[Ob, sm], w=[imp[qb]])
                        else:
                            k.op("dve", "scalar_tensor_tensor", out=imp[qb][:], in0=Ob[:, 129:193], scalar=sm[:, qb:qb + 1],
                                 in1=imp[qb][:], op0=ALU.mult, op1=ALU.add, r=[Ob, sm], w=[imp[qb]])
                    pairs = [(lambda nt: kcT[:, nt * 128:(nt + 1) * 128], lambda a, b, h=h: nq[h][:, a:b])]
                    cbias = lambda nt, j, h=h: cf[:, 512 + h * 32 + nt * 16 + j: 512 + h * 32 + nt * 16 + j + 1]
                    soft_sub(cm, g, pairs, lambda nt: vcv[:, nt, 0:193], 193, range(2), cmp_mode, cbias, cmp_mask, fin_cmp,
                             [kcT, nq[h], vcx])
                for qb in range(4):
                    qbg = 4 * g + qb
                    k.op("dve", "tensor_tensor", out=scb[:], in0=imp[qb][:], in1=bonus[:, qbg * 64:(qbg + 1) * 64], op=ALU.add,
                         r=[imp[qb], bonus], w=[scb])
                    k.op("dve", "max", out=m8[:, 0:8], in_=scb[:], r=[scb], w=[m8])
                    k.op("dve", "match_replace", out=sc2[:], in_to_replace=m8[:, 0:8], in_values=scb[:], imm_value=-3.0e38,
                         r=[m8, scb], w=[sc2])
                    k.op("dve", "max", out=m8[:, 8:16], in_=sc2[:], r=[sc2], w=[m8])
                    k.op("dve", "tensor_scalar", out=sm[:, 12:13], in0=m8[:, 15:16], scalar1=-1e29, scalar2=None, op0=ALU.max,
                         r=[m8], w=[sm])
                    k.op("dve", "tensor_scalar", out=selb[:], in0=scb[:], scalar1=sm[:, 12:13], scalar2=None, op0=ALU.is_ge,
                         r=[scb, sm], w=[selb])
                    k.op("dve", "tensor_copy", out=selexp[:].rearrange("p (j s) -> p j s", s=64),
                         in_=selb[:, 0:64].unsqueeze(2).to_broadcast([128, 64, 64]), r=[selb], w=[selexp])
                    for kt0 in range(0, 2 * qbg + 2, 4):
                        n = min(4, 2 * qbg + 2 - kt0)
                        Sb = cm["S"][cm["cnt"]["s"] % 3]
                        cm["cnt"]["s"] += 1
                        for j in range(n):
                            k.op("pe", "matmul", Sb[:, j * 128:(j + 1) * 128], lhsT=selexp[:, (kt0 + j) * 128:(kt0 + j + 1) * 128],
                                 rhs=cb("ident"), start=True, stop=True, r=[selexp, cbf], w=[Sb])
                        k.op("act", "copy", out=mkv[:, qb, kt0:kt0 + n, :], in_=Sb[:, 0:n * 128].rearrange("p (t d) -> p t d", d=128),
                             r=[Sb], w=[maskT])
                    k.op("dve", "tensor_tensor", out=mkv[:, qb, 2 * qbg, :], in0=mkv[:, qb, 2 * qbg, :], in1=cb("m0"), op=ALU.mult,
                         r=[cbf], w=[maskT])
                    k.op("dve", "tensor_tensor", out=mkv[:, qb, 2 * qbg + 1, :], in0=mkv[:, qb, 2 * qbg + 1, :], in1=cb("m1"),
                         op=ALU.mult, r=[cbf], w=[maskT])
                for h in range(4):
                    si = 2 * h + 1
                    pairs = [(lambda kt: nks[:, kt * 128:(kt + 1) * 128], lambda a, b, h=h: nq[h][:, a:b])]
                    soft_sub(cm, g, pairs, lambda kt: vsv[:, kt, 0:129], 129, range(0, 8 * g + 8), far_skip(si, sel_mode),
                             ab_bias(si), lambda m, kt, qb: (mkv[:, qb, kt, :], [maskT]), fin_norm(cm, cm["on"][0]),
                             [nks, nq[h], nvs], nchunk=ab_chunks(si))
                    pairs = [(lambda kt: nkw[:, kt * 128:(kt + 1) * 128], lambda a, b, h=h: nq[h][:, a:b])]
                    soft_sub(cm, g, pairs, lambda kt: vwv[:, kt, 0:129], 129, range(max(0, 8 * g - 4), 8 * g + 8),
                             far_skip(si, win_mode), ab_bias(si), std_mask, fin_norm(cm, cm["on"][1]), [nkw, nq[h], nvw],
                             nchunk=ab_chunks(si))
                    onc = cm["on"][2]
                    for qb in range(4):
                        qbg = 4 * g + qb
                        gcol = lambda i: gts[:, qbg * 12 + 3 * h + i: qbg * 12 + 3 * h + i + 1]
                        dst = onc[:, qb * 128:(qb + 1) * 128]
                        k.op("dve", "tensor_scalar", out=dst, in0=ocv[:, qb, h, :], scalar1=gcol(0), scalar2=None, op0=ALU.mult,
                             r=[oc, gts], w=[onc])
                        k.op("dve", "scalar_tensor_tensor", out=dst, in0=cm["on"][0][:, qb * 128:(qb + 1) * 128], scalar=gcol(1),
                             in1=dst, op0=ALU.mult, op1=ALU.add, r=[cm["on"][0], gts], w=[onc])
                        k.op("dve", "scalar_tensor_tensor", out=dst, in0=cm["on"][1][:, qb * 128:(qb + 1) * 128], scalar=gcol(2),
                             in1=dst, op0=ALU.mult, op1=ALU.add, r=[cm["on"][1], gts], w=[onc])
                    finish_head(cm, 8 + h, g, onc, 1.0)
            k.flush()

    if "sb" in mixers:
        with ExitStack() as ms:
            cm = common(ms)
            aq = [k.sb(ms, f"sq{i}", SL, BF16) for i in range(2)]
            ak = [k.sb(ms, f"sk{i}", S, BF16) for i in range(2)]
            av = [k.sb(ms, f"sv{i}", 32 * 132, BF16) for i in range(2)]
            Ef = [k.sb(ms, f"Ef{i}", 512, F32) for i in range(2)]
            Lb = [k.sb(ms, f"Lb{i}", 512, BF16) for i in range(3)]
            Lf = k.sb(ms, "Lf", 512, F32)
            Lsb = [k.sb(ms, f"Lsb{i}", 512, BF16) for i in range(3)]
            k.dma("sp", cm["gbc"][:], W["sb_o_norm"][l].partition_broadcast(128), key=cm["gbc"], w=[cm["gbc"]])
            step = [0]
            for h in range(4):
                q, kk, v = aq[h % 2], ak[h % 2], av[h % 2]
                k.dma("sp", q[:], QT[16 + h], key=q, w=[q])
                load_k(kk, 13 + h)
                vv = load_v(v, 10 + h)
                for g in range(NG):
                    k.op("dve", "memset", Lf[:], 0.0, w=[Lf])
                    ktl = list(range(8 * g + 7, -1, -1))

                    def stage_a(idx, kt, g=g, q=q, kk=kk):
                        st = {}
                        lo = max(0, (kt - 8 * g) // 2)
                        c0, c1 = lo * 128, 512
                        sidx = step[0]
                        step[0] += 1
                        Sb = cm["S"][sidx % 2]
                        E, L = Ef[sidx % 2], Lb[sidx % 3]
                        st.update(kt=kt, idx=idx, lo=lo, c0=c0, c1=c1, L=L, P=cm["pt"][sidx % 3],
                                  Ls_prev=Lsb[(sidx + 2) % 3], Ls_new=Lsb[sidx % 3])
                        lhs = kk[:, kt * 128:(kt + 1) * 128]
                        rhs = q[:, g * 512 + c0: g * 512 + c1]
                        st["lhs"], st["rhs"] = lhs, rhs
                        k.op("pe", "matmul", Sb[:, c0:c1], lhsT=lhs, rhs=rhs, start=True, stop=True, r=[kk, q], w=[Sb])
                        k.op("act", "activation", out=E[:, c0:c1], in_=Sb[:, c0:c1], func=AF.Exp, r=[Sb], w=[E])
                        k.op("act", "activation", out=L[:, c0:c1], in_=E[:, c0:c1], func=AF.Ln, bias=1.0, r=[E], w=[L])
                        if kt >= 8 * g:
                            k.op("dve", "tensor_tensor", out=L[:, c0:c0 + 128], in0=L[:, c0:c0 + 128],
                                 in1=cb("ms0" if kt % 2 == 0 else "ms1"), op=ALU.mult, r=[cbf], w=[L])
                        if kt > 0:
                            k.op("dve", "tensor_tensor", out=Lf[:, c0:c1], in0=Lf[:, c0:c1], in1=L[:, c0:c1], op=ALU.add,
                                 r=[L], w=[Lf])
                            k.op("dve", "tensor_copy", out=st["Ls_new"][:], in_=Lf[:], r=[Lf], w=[st["Ls_new"]])
                        return st

                    def stage_b(st, g=g, q=q, kk=kk, v=v, vv=vv):
                        kt, idx, lo, c0, c1, L, P = st["kt"], st["idx"], st["lo"], st["c0"], st["c1"], st["L"], st["P"]
                        Cb = cm["S"][2]
                        k.op("pe", "matmul", Cb[:, c0:c1], lhsT=cb("ntri"), rhs=L[:, c0:c1], start=True, stop=False,
                             r=[L, cbf], w=[Cb])
                        if idx > 0:
                            k.op("pe", "matmul", Cb[:, c0:c1], lhsT=cb("nones"), rhs=st["Ls_prev"][:, c0:c1], start=False, stop=False,
                                 r=[st["Ls_prev"], cbf], w=[Cb])
                        k.op("pe", "matmul", Cb[:, c0:c1], lhsT=st["lhs"], rhs=st["rhs"], start=False, stop=True, r=[kk, q], w=[Cb])
                        k.op("act", "activation", out=P[:, c0:c1], in_=Cb[:, c0:c1], func=AF.Exp, r=[Cb], w=[P])
                        if kt >= 8 * g:
                            k.op("dve", "tensor_tensor", out=P[:, c0:c0 + 128], in0=P[:, c0:c0 + 128],
                                 in1=cb("ms0" if kt % 2 == 0 else "ms1"), op=ALU.mult, r=[cbf], w=[P])
                        for qb in range(lo, 4):
                            k.op("pe", "matmul", cm["O"][qb][:, 0:128], lhsT=P[:, qb * 128:(qb + 1) * 128], rhs=vv[:, kt, 0:128],
                                 start=(kt == 2 * (4 * g + qb) + 1), stop=(kt == 0), r=[P, v], w=[cm["O"][qb]])

                    prev = None
                    for idx, kt in enumerate(ktl):
                        st = stage_a(idx, kt)
                        if prev is not None:
                            stage_b(prev)
                        prev = st
                    stage_b(prev)
                    onb = cm["on"][0]
                    for qb in range(4):
                        k.op("act", "copy", out=onb[:, qb * 128:(qb + 1) * 128], in_=cm["O"][qb][:, 0:128], r=[cm["O"][qb]], w=[onb])
                    finish_head(cm, 12 + h, g, onb, 1.0)
            k.flush()


_CACHE = {}


def _host_inputs(inputs, C):
    common = {}
    for nm in ["ffn1_w_gate", "ffn1_w_up", "ffn1_w_down", "ffn2_w_gate", "ffn2_w_up", "ffn2_w_down", "w_in", "w_out",
               "mla_w_uq", "mla_w_ukv", "nsa_w_ck", "nsa_w_cv", "nsa_pe_k", "nsa_pe_v", "ffn1_norm", "mix_norm",
               "ffn2_norm", "mla_o_norm", "diff_subln", "nsa_o_norm", "sb_o_norm", "diff_lq1", "diff_lk1", "diff_lq2",
               "diff_lk2"]:
        common[nm] = np.ascontiguousarray(np.asarray(inputs[nm], np.float32))
    common["gcols"] = np.stack([_gain_cols(inputs, l) for l in range(DEPTH)], axis=0).astype(np.float32)
    common["cbf"] = C["cbf"]
    common["cf32"] = C["cf32"]
    common["cosT"] = C["cosT"]
    common["sinT"] = C["sinT"]
    common["bonus"] = C["bonus"]
    return common


def _rows(r):
    j = np.arange(NQB)
    return ((2 * j[:, None] + r) * 128 + np.arange(128)[None, :]).reshape(-1)


def kernel(**inputs):
    Cs = [_consts(0), _consts(1)]
    inputs = {kk: np.asarray(v) for kk, v in inputs.items()}
    _gain_cols(inputs, 0)
    nc = build_program(Cs[0])
    commons = [_host_inputs(inputs, Cs[r]) for r in range(2)]
    x = np.asarray(inputs["x"], np.float32)
    in_maps = []
    for c in range(8):
        b, r = c // 2, c % 2
        m = dict(commons[r])
        m["x"] = np.ascontiguousarray(x[b][_rows(r)])
        in_maps.append(m)
    res = run_bass_kernel_spmd(nc, in_maps, core_ids=list(range(8)))
    out = np.empty((4, S, D), np.float32)
    for c in range(8):
        b, r = c // 2, c % 2
        out[b, _rows(r)] = res.results[c]["y"]
    return out
```
